# Optimizing a Trainium2 kernel written in Bass

```python
import jax
import jax.numpy as jnp
from jax import lax
import numpy as np

D_MODEL = 1024
BATCH = 8
SEQ = 2048
DEPTH = 4
DEC_BATCH = 128
DEC_SEQ = 8
PAST_LEN = 16384
PAGE_SIZE = 128

N_MIXERS = 2
N_CONV_LAYERS = (DEPTH + 1) // 2
N_RWKV_LAYERS = DEPTH // 2
CONV_WIDTH = 31
CONV_STATE = CONV_WIDTH - 1
RWKV_HEAD = 64
RWKV_HEADS = D_MODEL // RWKV_HEAD
DECAY_LORA = 64
AAA_LORA = 64
MV_LORA = 32
GATE_LORA = 160
LNX_EPS = 64e-5
MEM_LEN = 256
XA_HEADS = 4
XA_HEAD_DIM = D_MODEL // XA_HEADS
D_FF = 4 * D_MODEL
RMS_EPS = 1e-6
LN_EPS = 1e-5
L2_EPS = 1e-12

kernel_name = 'hybrid_conformer_rwkv7_memxattn_step'


def rms_norm(x, g):
    xf = x.astype(jnp.float32)
    y = xf * lax.rsqrt(jnp.mean(xf * xf, axis=-1, keepdims=True) + RMS_EPS)
    return (y * g.astype(jnp.float32)).astype(x.dtype)


def standardize(x, eps):
    xf = x.astype(jnp.float32)
    xc = xf - jnp.mean(xf, axis=-1, keepdims=True)
    return xc * lax.rsqrt(jnp.mean(xc * xc, axis=-1, keepdims=True) + eps)


def conformer_conv(h, conv0, p, ci):
    d = h.shape[-1]
    u = h @ p['cv_w_in'][ci] + p['cv_b_in'][ci]
    u = u[..., :d] * jax.nn.sigmoid(u[..., d:])
    up = jnp.concatenate([conv0.astype(u.dtype), u], axis=1)
    filt = p['cv_w_dw'][ci][:, None, :].astype(u.dtype)
    z = lax.conv_general_dilated(up, filt, window_strides=(1,), padding='VALID',
                                 dimension_numbers=('NWC', 'WIO', 'NWC'),
                                 feature_group_count=d)
    z = z + p['cv_b_dw'][ci]
    z = standardize(z, LN_EPS) * p['cv_ln_g'][ci] + p['cv_ln_b'][ci]
    z = jax.nn.silu(z).astype(h.dtype)
    out = z @ p['cv_w_out'][ci] + p['cv_b_out'][ci]
    return out, up[:, -CONV_STATE:]


def wkv7_scan(r, w, k, v, a, b, s0):
    def step(s, inp):
        rt, wt, kt, vt, at, bt = inp
        sa = jnp.einsum('bhvk,bhk->bhv', s, at)
        s = s * wt[:, :, None, :] + sa[..., None] * bt[:, :, None, :] + vt[..., None] * kt[:, :, None, :]
        yt = jnp.einsum('bhvk,bhk->bhv', s, rt)
        return s, yt
    seq = tuple(jnp.moveaxis(z, 1, 0) for z in (r, w, k, v, a, b))
    s, y = lax.scan(step, s0, seq)
    return jnp.moveaxis(y, 0, 1), s


def rwkv7_time_mix(h, shift0, s0, v_first, p, ri):
    bsz, t, d = h.shape
    nh, n = RWKV_HEADS, RWKV_HEAD
    f32 = jnp.float32
    prev = jnp.concatenate([shift0[:, None, :].astype(h.dtype), h[:, :-1]], axis=1)
    xx = prev - h
    mix = p['rw_mix'][ri]
    xr, xw, xk, xv, xa, xg = (h + xx * mix[i] for i in range(6))
    r = xr @ p['rw_w_r'][ri]
    k = xk @ p['rw_w_k'][ri]
    v = xv @ p['rw_w_v'][ri]
    w_pre = (p['rw_w0'][ri] + jnp.tanh(xw @ p['rw_w1'][ri]) @ p['rw_w2'][ri]).astype(f32)
    w_log = -jax.nn.softplus(-w_pre) - 0.5
    decay = jnp.exp(-jnp.exp(w_log))
    if v_first is None:
        v_first = v
    else:
        vi = ri - 1
        v = v + (v_first - v) * jax.nn.sigmoid(p['rw_v0'][vi] + (xv @ p['rw_v1'][vi]) @ p['rw_v2'][vi])
    a = jax.nn.sigmoid(p['rw_a0'][ri] + (xa @ p['rw_a1'][ri]) @ p['rw_a2'][ri])
    g = jax.nn.sigmoid(xg @ p['rw_g1'][ri]) @ p['rw_g2'][ri]
    kk = (k * p['rw_k_k'][ri]).reshape(bsz, t, nh, n).astype(f32)
    kk = kk / jnp.maximum(jnp.sqrt(jnp.sum(kk * kk, axis=-1, keepdims=True)), L2_EPS)
    k = k * (1 + (a - 1) * p['rw_k_a'][ri])
    rh = r.reshape(bsz, t, nh, n).astype(f32)
    kh = k.reshape(bsz, t, nh, n).astype(f32)
    vh = v.reshape(bsz, t, nh, n).astype(f32)
    ah = a.reshape(bsz, t, nh, n).astype(f32)
    y, s = wkv7_scan(rh, decay.reshape(bsz, t, nh, n), kh, vh, -kk, kk * ah, s0.astype(f32))
    y = standardize(y, LNX_EPS).reshape(bsz, t, d) * p['rw_lnx_g'][ri] + p['rw_lnx_b'][ri]
    bonus = jnp.sum(rh * kh * p['rw_r_k'][ri], axis=-1, keepdims=True) * vh
    y = (y + bonus.reshape(bsz, t, d)).astype(h.dtype)
    out = (y * g) @ p['rw_w_o'][ri]
    return out, h[:, -1], s, v_first


def cross_attend(h, w_q, mem_k, mem_v, w_o):
    bsz, t, d = h.shape
    q = (h @ w_q).reshape(bsz, t, XA_HEADS, XA_HEAD_DIM)
    sc = jnp.einsum('bthd,bmhd->bhtm', q, mem_k.astype(q.dtype)).astype(jnp.float32) * (XA_HEAD_DIM ** -0.5)
    pr = jax.nn.softmax(sc, axis=-1).astype(h.dtype)
    o = jnp.einsum('bhtm,bmhd->bthd', pr, mem_v.astype(h.dtype)).reshape(bsz, t, d)
    return o @ w_o


def sq_relu_mlp(h, w_up, w_down):
    u = jax.nn.relu(h @ w_up)
    return (u * u) @ w_down


def trunk(x, mem_k, mem_v, conv_state, shift_state, wkv_state, p):
    conv_new, shift_new, wkv_new = [], [], []
    v_first = None
    for layer in range(DEPTH):
        g = p['norm_g'][layer]
        h = rms_norm(x, g[0])
        if layer % N_MIXERS == 0:
            ci = layer // N_MIXERS
            out, cs = conformer_conv(h, conv_state[ci], p, ci)
            conv_new.append(cs)
        else:
            ri = layer // N_MIXERS
            out, sh, s, v_first = rwkv7_time_mix(h, shift_state[ri], wkv_state[ri], v_first, p, ri)
            shift_new.append(sh)
            wkv_new.append(s)
        x = x + out
        x = x + cross_attend(rms_norm(x, g[1]), p['xa_w_q'][layer], mem_k[layer], mem_v[layer], p['xa_w_o'][layer])
        x = x + sq_relu_mlp(rms_norm(x, g[2]), p['mlp_w_up'][layer], p['mlp_w_down'][layer])
    y = rms_norm(x, p['final_g'])
    return y, jnp.stack(conv_new), jnp.stack(shift_new), jnp.stack(wkv_new)


def setup_inputs(seed: int = 0) -> dict:
    key = jax.random.key(seed)
    keys = iter(jax.random.split(key, 64))
    f32 = jnp.float32
    D, NC, NR, H, N = D_MODEL, N_CONV_LAYERS, N_RWKV_LAYERS, RWKV_HEADS, RWKV_HEAD

    def normal(shape, scale):
        return jax.random.normal(next(keys), shape, f32) * scale

    def around(shape, center, scale):
        return center + normal(shape, scale)

    return {
        'x_prompt': normal((BATCH, SEQ, D), 1.0),
        'x_sample': normal((DEC_BATCH, DEC_SEQ, D), 1.0),
        'mem_prompt': normal((BATCH, MEM_LEN, D), 1.0),
        'cache_mem_k': normal((DEPTH, DEC_BATCH, MEM_LEN, XA_HEADS, XA_HEAD_DIM), 1.0),
        'cache_mem_v': normal((DEPTH, DEC_BATCH, MEM_LEN, XA_HEADS, XA_HEAD_DIM), 1.0),
        'state_conv': normal((NC, DEC_BATCH, CONV_STATE, D), 0.5),
        'state_shift': normal((NR, DEC_BATCH, D), 1.0),
        'state_wkv': normal((NR, DEC_BATCH, H, N, N), 0.3),
        'norm_g': around((DEPTH, 3, D), 1.0, 0.02),
        'final_g': around((D,), 1.0, 0.02),
        'cv_w_in': normal((NC, D, 2 * D), D ** -0.5),
        'cv_b_in': normal((NC, 2 * D), 0.02),
        'cv_w_dw': normal((NC, CONV_WIDTH, D), CONV_WIDTH ** -0.5),
        'cv_b_dw': normal((NC, D), 0.02),
        'cv_ln_g': around((NC, D), 1.0, 0.02),
        'cv_ln_b': normal((NC, D), 0.02),
        'cv_w_out': normal((NC, D, D), D ** -0.5),
        'cv_b_out': normal((NC, D), 0.02),
        'rw_mix': jax.random.uniform(next(keys), (NR, 6, D), f32),
        'rw_w_r': normal((NR, D, D), D ** -0.5),
        'rw_w_k': normal((NR, D, D), D ** -0.5),
        'rw_w_v': normal((NR, D, D), D ** -0.5),
        'rw_w_o': normal((NR, D, D), D ** -0.5),
        'rw_w0': normal((NR, D), 0.5),
        'rw_w1': normal((NR, D, DECAY_LORA), D ** -0.5),
        'rw_w2': normal((NR, DECAY_LORA, D), 0.5 * DECAY_LORA ** -0.5),
        'rw_a0': normal((NR, D), 0.1),
        'rw_a1': normal((NR, D, AAA_LORA), D ** -0.5),
        'rw_a2': normal((NR, AAA_LORA, D), 0.5 * AAA_LORA ** -0.5),
        'rw_v0': around((max(NR - 1, 0), D), 0.5, 0.1),
        'rw_v1': normal((max(NR - 1, 0), D, MV_LORA), D ** -0.5),
        'rw_v2': normal((max(NR - 1, 0), MV_LORA, D), 0.5 * MV_LORA ** -0.5),
        'rw_g1': normal((NR, D, GATE_LORA), D ** -0.5),
        'rw_g2': normal((NR, GATE_LORA, D), GATE_LORA ** -0.5),
        'rw_k_k': around((NR, D), 0.85, 0.05),
        'rw_k_a': around((NR, D), 1.0, 0.05),
        'rw_r_k': normal((NR, H, N), 0.1),
        'rw_lnx_g': around((NR, D), 1.0, 0.02),
        'rw_lnx_b': normal((NR, D), 0.02),
        'xa_w_q': normal((DEPTH, D, D), D ** -0.5),
        'xa_w_kv': normal((DEPTH, D, 2 * D), D ** -0.5),
        'xa_w_o': normal((DEPTH, D, D), D ** -0.5),
        'mlp_w_up': normal((DEPTH, D, D_FF), D ** -0.5),
        'mlp_w_down': normal((DEPTH, D_FF, D), D_FF ** -0.5),
    }


def reference(x_prompt, x_sample, mem_prompt, cache_mem_k, cache_mem_v, state_conv, state_shift, state_wkv,
              norm_g, final_g,
              cv_w_in, cv_b_in, cv_w_dw, cv_b_dw, cv_ln_g, cv_ln_b, cv_w_out, cv_b_out,
              rw_mix, rw_w_r, rw_w_k, rw_w_v, rw_w_o, rw_w0, rw_w1, rw_w2, rw_a0, rw_a1, rw_a2,
              rw_v0, rw_v1, rw_v2, rw_g1, rw_g2, rw_k_k, rw_k_a, rw_r_k, rw_lnx_g, rw_lnx_b,
              xa_w_q, xa_w_kv, xa_w_o, mlp_w_up, mlp_w_down):
    p = {
        'norm_g': norm_g, 'final_g': final_g,
        'cv_w_in': cv_w_in, 'cv_b_in': cv_b_in, 'cv_w_dw': cv_w_dw, 'cv_b_dw': cv_b_dw,
        'cv_ln_g': cv_ln_g, 'cv_ln_b': cv_ln_b, 'cv_w_out': cv_w_out, 'cv_b_out': cv_b_out,
        'rw_mix': rw_mix, 'rw_w_r': rw_w_r, 'rw_w_k': rw_w_k, 'rw_w_v': rw_w_v, 'rw_w_o': rw_w_o,
        'rw_w0': rw_w0, 'rw_w1': rw_w1, 'rw_w2': rw_w2, 'rw_a0': rw_a0, 'rw_a1': rw_a1, 'rw_a2': rw_a2,
        'rw_v0': rw_v0, 'rw_v1': rw_v1, 'rw_v2': rw_v2, 'rw_g1': rw_g1, 'rw_g2': rw_g2,
        'rw_k_k': rw_k_k, 'rw_k_a': rw_k_a, 'rw_r_k': rw_r_k, 'rw_lnx_g': rw_lnx_g, 'rw_lnx_b': rw_lnx_b,
        'xa_w_q': xa_w_q, 'xa_w_o': xa_w_o, 'mlp_w_up': mlp_w_up, 'mlp_w_down': mlp_w_down,
    }
    bp, mlen = mem_prompt.shape[0], mem_prompt.shape[1]
    mem_kv = jnp.einsum('bmd,lde->lbme', mem_prompt, xa_w_kv)
    mem_k_prompt = mem_kv[..., :D_MODEL].reshape(DEPTH, bp, mlen, XA_HEADS, XA_HEAD_DIM)
    mem_v_prompt = mem_kv[..., D_MODEL:].reshape(DEPTH, bp, mlen, XA_HEADS, XA_HEAD_DIM)
    conv0 = jnp.zeros((N_CONV_LAYERS, bp, CONV_STATE, D_MODEL), x_prompt.dtype)
    shift0 = jnp.zeros((N_RWKV_LAYERS, bp, D_MODEL), x_prompt.dtype)
    wkv0 = jnp.zeros((N_RWKV_LAYERS, bp, RWKV_HEADS, RWKV_HEAD, RWKV_HEAD), jnp.float32)
    y_prompt, conv_prompt, shift_prompt, wkv_prompt = trunk(
        x_prompt, mem_k_prompt, mem_v_prompt, conv0, shift0, wkv0, p)
    y_sample, conv_sample, shift_sample, wkv_sample = trunk(
        x_sample, cache_mem_k, cache_mem_v, state_conv, state_shift, state_wkv, p)
    return (y_prompt, y_sample, mem_k_prompt, mem_v_prompt, conv_prompt, shift_prompt, wkv_prompt,
            conv_sample, shift_sample, wkv_sample)
```

```python
import contextlib
import numpy as np
import concourse.bass as bass
import concourse.mybir as mybir
from concourse.bass_utils import run_bass_kernel_spmd

F32 = mybir.dt.float32
BF16 = mybir.dt.bfloat16
ALU = mybir.AluOpType
AF = mybir.ActivationFunctionType
AX = mybir.AxisListType

D = 1024
NCH = 8
TP = 2048
NSEQ = 16
TS = 128
T = TP + TS
DEPTH = 4
MEM = 256
DFF = 4096
EPOCH = 16000
NDMASEM = 24

BLOCKS = [(0, 512), (512, 512), (1024, 512), (1536, 512), (2048, 128)]


class Rec:
    ENGS = ("pe", "act", "dve", "pool", "sp")

    def __init__(self, nc, stack):
        self.nc = nc
        self.stack = stack
        self.lists = {e: [] for e in self.ENGS}
        self.count = {e: 0 for e in self.ENGS}
        self.sems = {}
        self.seen = {e: {} for e in self.ENGS}
        self.last_w = {}
        self.readers = {}
        self.dma_slot = 0
        self.dma_val = [0] * NDMASEM
        for i in range(NDMASEM):
            self.sems[("dma", i)] = stack.enter_context(nc.semaphore(f"dma{i}"))
        self.nwait = 0

    def _sem(self, key):
        if key not in self.sems:
            self.sems[key] = self.stack.enter_context(self.nc.semaphore(f"s_{key[0]}_{key[1]}"))
        return self.sems[key]

    def _wait(self, eng, ev):
        key, val = ev
        if self.seen[eng].get(key, 0) >= val:
            return
        self.seen[eng][key] = val
        sem = self._sem(key)
        self.lists[eng].append(lambda e, sem=sem, val=val: e.wait_ge(sem, val))
        self.nwait += 1

    def _deps(self, eng, r, w):
        evs = []
        for res in r:
            ev = self.last_w.get(res)
            if ev is not None:
                evs.append(ev)
        for res in w:
            ev = self.last_w.get(res)
            if ev is not None:
                evs.append(ev)
            evs.extend(self.readers.get(res, ()))
        for ev in evs:
            self._wait(eng, ev)

    def _commit(self, ev, r, w):
        for res in r:
            self.readers.setdefault(res, []).append(ev)
        for res in w:
            self.last_w[res] = ev
            self.readers[res] = []

    def op(self, eng, fn, r=(), w=()):
        self._deps(eng, r, w)
        n = self.count[eng]
        key = (eng, n // EPOCH)
        val = n % EPOCH + 1
        sem = self._sem(key)
        self.count[eng] = n + 1
        self.lists[eng].append(lambda e, fn=fn, sem=sem: fn(e).then_inc(sem, 1))
        self._commit((key, val), r, w)

    def dma(self, eng, out, in_, r=(), w=(), **kw):
        self._deps(eng, r, w)
        s = self.dma_slot
        self.dma_slot = (s + 1) % NDMASEM
        key = ("dma", s)
        if self.dma_val[s] > 0:
            self._wait(eng, (key, self.dma_val[s]))
        self.dma_val[s] += 16
        val = self.dma_val[s]
        sem = self.sems[key]
        self.lists[eng].append(
            lambda e, out=out, in_=in_, sem=sem, kw=kw: e.dma_start(out=out, in_=in_, **kw).then_inc(sem, 16))
        self._commit((key, val), r, w)

    def barrier(self):
        evs = []
        for e in self.ENGS:
            n = self.count[e]
            if n > 0:
                evs.append(((e, (n - 1) // EPOCH), (n - 1) % EPOCH + 1))
        for s in range(NDMASEM):
            if self.dma_val[s] > 0:
                evs.append((("dma", s), self.dma_val[s]))
        for e in self.ENGS:
            for ev in evs:
                if ev[0][0] == e:
                    continue
                self._wait(e, ev)

    def finish(self):
        for s in range(NDMASEM):
            if self.dma_val[s] > 0:
                self._wait("sp", (("dma", s), self.dma_val[s]))
        for e in self.ENGS:
            n = self.count[e]
            if n > 0 and e != "sp":
                self._wait("sp", ((e, (n - 1) // EPOCH), (n - 1) % EPOCH + 1))
        nc = self.nc
        lists = self.lists
        with nc.Block() as block:
            @block.tensor
            def _(e):
                for f in lists["pe"]:
                    f(e)

            @block.scalar
            def _(e):
                for f in lists["act"]:
                    f(e)

            @block.vector
            def _(e):
                for f in lists["dve"]:
                    f(e)

            @block.gpsimd
            def _(e):
                for f in lists["pool"]:
                    f(e)

            @block.sync
            def _(e):
                for f in lists["sp"]:
                    f(e)


def vec_layout():
    names = []
    for l in range(DEPTH):
        for j in range(3):
            names.append(("norm_g", l, j))
    names.append(("final_g",))
    for ci in range(2):
        names += [("cv_b_in", ci, 0), ("cv_b_in", ci, 1)]
        for j in range(31):
            names.append(("cv_w_dw", ci, j))
        names += [("cv_b_dw", ci), ("cv_ln_g", ci), ("cv_ln_b", ci), ("cv_b_out", ci)]
    for ri in range(2):
        for j in range(6):
            names.append(("rw_mix", ri, j))
        names += [("rw_w0", ri), ("rw_a0", ri), ("rw_k_k", ri), ("rw_k_a", ri), ("rw_r_k", ri),
                  ("rw_lnx_g", ri), ("rw_lnx_b", ri)]
    names.append(("rw_v0", 0))
    return {n: i for i, n in enumerate(names)}


VIDX = vec_layout()
NVEC = len(VIDX)


def pack_vecs(inp):
    out = np.zeros((128, NVEC, 8), np.float32)

    def put(key, v):
        out[:, VIDX[key], :] = np.asarray(v, np.float32).reshape(8, 128).T

    for l in range(DEPTH):
        for j in range(3):
            put(("norm_g", l, j), inp["norm_g"][l, j])
    put(("final_g",), inp["final_g"])
    for ci in range(2):
        put(("cv_b_in", ci, 0), inp["cv_b_in"][ci, :D])
        put(("cv_b_in", ci, 1), inp["cv_b_in"][ci, D:])
        for j in range(31):
            put(("cv_w_dw", ci, j), inp["cv_w_dw"][ci, j])
        put(("cv_b_dw", ci), inp["cv_b_dw"][ci])
        put(("cv_ln_g", ci), inp["cv_ln_g"][ci])
        put(("cv_ln_b", ci), inp["cv_ln_b"][ci])
        put(("cv_b_out", ci), inp["cv_b_out"][ci])
    for ri in range(2):
        for j in range(6):
            put(("rw_mix", ri, j), inp["rw_mix"][ri, j])
        put(("rw_w0", ri), inp["rw_w0"][ri])
        put(("rw_a0", ri), inp["rw_a0"][ri])
        put(("rw_k_k", ri), inp["rw_k_k"][ri])
        put(("rw_k_a", ri), inp["rw_k_a"][ri])
        put(("rw_r_k", ri), inp["rw_r_k"][ri].reshape(-1))
        put(("rw_lnx_g", ri), inp["rw_lnx_g"][ri])
        put(("rw_lnx_b", ri), inp["rw_lnx_b"][ri])
    put(("rw_v0", 0), inp["rw_v0"][0])
    return out.reshape(128, NVEC * 8)


class Builder:
    def __init__(self, phases=("mlp",)):
        self.phases = phases
        self.nc = bass.Bass("TRN2", target_bir_lowering=False)
        self.stack = contextlib.ExitStack()
        self.R = Rec(self.nc, self.stack)
        self.dins = {}
        self.douts = {}

    def dram_in(self, name, shape):
        if name not in self.dins:
            self.dins[name] = self.nc.dram_tensor(name, list(shape), F32, kind="ExternalInput").ap()
        return self.dins[name]

    def dram_out(self, name, shape):
        if name not in self.douts:
            self.douts[name] = self.nc.dram_tensor(name, list(shape), F32, kind="ExternalOutput").ap()
        return self.douts[name]

    def sb(self, name, shape, dtype):
        self.uid = getattr(self, "uid", 0) + 1
        return self.stack.enter_context(self.nc.sbuf_tensor(f"{name}_{self.uid}", list(shape), dtype))

    def ps(self, name, shape, dtype=F32):
        self.uid = getattr(self, "uid", 0) + 1
        return self.stack.enter_context(self.nc.psum_tensor(f"{name}_{self.uid}", list(shape), dtype))

    w_up_d = property(lambda self: self.dram_in("mlp_w_up", [DEPTH, D, DFF]))
    w_dn_d = property(lambda self: self.dram_in("mlp_w_down", [DEPTH, DFF, D]))
    memT_d = property(lambda self: self.dram_in("memT", [D, MEM]))
    wkv_d = property(lambda self: self.dram_in("xa_w_kv", [DEPTH, D, 2 * D]))
    wq_d = property(lambda self: self.dram_in("xa_w_q", [DEPTH, D, D]))
    wo_d = property(lambda self: self.dram_in("xa_w_o", [DEPTH, D, D]))
    cKT_d = property(lambda self: self.dram_in("cKT", [DEPTH, NSEQ, 4, 256, MEM]))
    cV_d = property(lambda self: self.dram_in("cV", [DEPTH, NSEQ, MEM, D]))
    memk_d = property(lambda self: self.dram_out("mem_k", [DEPTH, MEM, D]))
    memv_d = property(lambda self: self.dram_out("mem_v", [DEPTH, MEM, D]))

    def vec(self, key):
        i = VIDX[key]
        return self.vecs[:, i * 8:(i + 1) * 8]

    def build(self):
        nc, R = self.nc, self.R
        self.xT_d = self.dram_in("xT", [D, T])
        self.vecs_d = self.dram_in("vecs", [128, NVEC * 8])
        self.yT_d = self.dram_out("yT", [D, T])

        self.x = self.sb("x", [128, NCH, T], F32)
        self.vecs = self.sb("vecs_sb", [128, NVEC * 8], F32)
        self.ones_m = self.sb("ones_m", [128, 128], BF16)
        self.psb = None
        self.ones_1 = self.sb("ones_1", [128, 128], BF16)
        R.op("pool", lambda e: e.memset(self.ones_1[:], 1.0), w=["ones_1"])
        if "attn" in self.phases:
            self.memT = self.sb("memT_sb", [128, NCH, MEM], BF16)
            R.dma("pool", self.memT[:], self.memT_d.rearrange("(c p) m -> p c m", p=128), w=["memT"])

        R.op("pool", lambda e: e.memset(self.ones_m[:], 1.0 / D), w=["ones_m"])
        self.eps_rms = self.sb("eps_rms", [128, 1], F32)
        R.op("pool", lambda e: e.memset(self.eps_rms[:], 1e-6), w=["consts"])
        self.eps_lnx = self.sb("eps_lnx", [128, 1], F32)
        R.op("pool", lambda e: e.memset(self.eps_lnx[:], 64e-5), w=["consts"])
        if "rwkv" in self.phases:
            cd = self.dram_in("rconst", [128, 4 * 8 * 64 + 128 + 128])
            self.cmask = self.sb("cmask", [128, 4, 8, 64], BF16)
            self.identf = self.sb("identf", [128, 128], F32)
            self.identb = self.sb("identb", [128, 128], BF16)
            self.blk64 = self.sb("blk64", [128, 128], BF16)
            R.dma("pool", self.cmask[:].rearrange("p a b c -> p (a b c)"), cd[:, 0:2048], w=["consts"])
            R.dma("sp", self.identf[:], cd[:, 2048:2176], w=["consts"])
            R.dma("pool", self.identb[:], cd[:, 2048:2176], w=["consts"])
            R.dma("pool", self.blk64[:], cd[:, 2176:2304], w=["consts"])
            self.xspill = self.nc.dram_tensor("xspill", [D, T], F32, kind="Internal").ap()
            self.vfirst_d = self.nc.dram_tensor("vfirst", [D, T], F32, kind="Internal").ap()
        self.eps_ln = self.sb("eps_ln", [128, 1], F32)
        R.op("pool", lambda e: e.memset(self.eps_ln[:], 1e-5), w=["consts"])
        R.dma("sp", self.vecs[:], self.vecs_d[:], w=["vecs"])
        xv = self.xT_d.rearrange("(c p) t -> p c t", p=128)
        for c in range(NCH):
            R.dma("sp", self.x[:, c, :], xv[:, c, :], w=[("x", c, b) for b in range(len(BLOCKS))])

        with contextlib.ExitStack() as st:
            old, self.stack = self.stack, st
            for l in range(NLAYERS):
                if "conv" in self.phases and l % 2 == 0:
                    self.phase_conv(l)
                if "rwkv" in self.phases and l % 2 == 1:
                    self.phase_rwkv(l)
                if "attn" in self.phases:
                    self.phase_attn(l)
                if "mlp" in self.phases:
                    self.phase_mlp(l)
            self.final_norm()
            R.barrier()
            self.stack = old
        R.finish()
        return nc

    def rms_rstd(self, blk, rstd, tagps=7):
        R = self.R
        t0, n = BLOCKS[blk]
        sq = self.sq_scr
        ps = self.psb[tagps]
        R.op("act", lambda e: e.activation(out=sq[:, :, :n], in_=self.x[:, :, t0:t0 + n], func=AF.Square),
             r=[("x", c, blk) for c in range(NCH)], w=["sq_scr"])

        def mm(e):
            ins = None
            for c in range(NCH):
                ins = e.matmul(ps[:, :n], self.ones_m[:], sq[:, c, :n], start=(c == 0), stop=(c == NCH - 1))
            return ins
        R.op("pe", mm, r=["sq_scr", "ones_m"], w=[("ps", tagps)])
        R.op("act", lambda e: e.activation(out=rstd[:, :n], in_=ps[:, :n], func=AF.Sqrt, bias=self.eps_rms[:, 0:1]),
             r=[("ps", tagps), "consts"], w=["rstd"])
        R.op("dve", lambda e: e.reciprocal(out=rstd[:, :n], in_=rstd[:, :n]), r=["rstd"], w=["rstd"])

    def norm_block(self, blk, gkey, out_tile, out_res, out_off=None):
        R = self.R
        t0, n = BLOCKS[blk]
        off = t0 if out_off is None else out_off
        self.rms_rstd(blk, self.rstd)
        g = self.vec(gkey)

        def f(e):
            ins = None
            for c in range(NCH):
                ins = e.scalar_tensor_tensor(out=out_tile[:, c, off:off + n], in0=self.x[:, c, t0:t0 + n],
                                             scalar=g[:, c:c + 1], in1=self.rstd[:, :n],
                                             op0=ALU.mult, op1=ALU.mult)
            return ins
        R.op("dve", f, r=[("x", c, blk) for c in range(NCH)] + ["rstd", "vecs"], w=out_res)


    def phase_conv(self, l):
        R = self.R
        ci = l // 2
        with contextlib.ExitStack() as st:
            old, self.stack = self.stack, st
            self.sq_scr = self.sb("sq_scr", [128, NCH, 512], BF16)
            self.rstd = self.sb("rstd", [128, 512], F32)
            self.psb = [self.ps(f"psb{i}", [128, 512], F32) for i in range(8)]
            win = self.sb("win", [128, NCH, 2 * D], BF16)
            wout = self.sb("wout", [128, NCH, D], BF16)
            hbk = self.sb("hbk", [128, NCH, 512], BF16)
            up = self.sb("up", [128, NCH, NSEQ * 38], F32)
            ups = up[:].rearrange("p c (b t) -> p c b t", t=38)
            gate = [self.sb(f"gate{i}", [128, 512], F32) for i in range(2)]
            z = self.sb("z", [128, NCH, 512], F32)
            zb = self.sb("zb", [128, NCH, 512], BF16)
            mean = self.sb("mean", [128, 512], F32)
            var = self.sb("var", [128, 512], F32)
            mr = self.sb("mr", [128, 512], F32)
            tmp = [self.sb(f"tmpc{i}", [128, 512], F32) for i in range(2)]
            psb = self.psb
            win_v = self.dram_in("cv_w_in", [2, D, 2 * D])[ci].rearrange("(c p) e -> p c e", p=128)
            wout_v = self.dram_in("cv_w_out", [2, D, D])[ci].rearrange("(c p) e -> p c e", p=128)
            for j in range(4):
                R.dma("pool", win[:, :, j * 512:(j + 1) * 512], win_v[:, :, j * 512:(j + 1) * 512], w=["win"])
            for j in range(2):
                R.dma("pool", wout[:, :, j * 512:(j + 1) * 512], wout_v[:, :, j * 512:(j + 1) * 512], w=["wout"])
            sconv_d = self.dram_in("sconvT", [2, D, NSEQ, 30])
            convp_d = self.dram_out("convpT", [2, D, 30])
            convs_d = self.dram_out("convsT", [2, D, NSEQ, 30])
            R.op("pool", lambda e: e.memset(up[:, :, 0:30], 0.0), w=["up"])
            b1 = self.vec(("cv_b_in", ci, 0))
            b2 = self.vec(("cv_b_in", ci, 1))
            bdw = self.vec(("cv_b_dw", ci))
            lng = self.vec(("cv_ln_g", ci))
            lnb = self.vec(("cv_ln_b", ci))
            bout = self.vec(("cv_b_out", ci))
            wdw = [self.vec(("cv_w_dw", ci, j)) for j in range(31)]
            ng = 0
            for blk, (t0, n) in enumerate(BLOCKS):
                samp = blk == 4
                if samp:
                    for c in range(NCH):
                        R.dma("sp", ups[:, c, :, 0:30], sconv_d[ci, c * 128:(c + 1) * 128], r=["up"], w=["up"])
                self.norm_block(blk, ("norm_g", l, 0), hbk, ["hbk"], out_off=0)
                if 0 < blk < 4:
                    R.op("pool", lambda e: e.tensor_copy(out=up[:, :, 0:30], in_=up[:, :, 512:542]), r=["up"], w=["up"])
                for ec in range(NCH):
                    psA, psG = psb[(ec % 2) * 2], psb[(ec % 2) * 2 + 1]
                    pa, pg = (ec % 2) * 2, (ec % 2) * 2 + 1

                    def mm(e, psA=psA, psG=psG, ec=ec, n=n):
                        ins = None
                        for c in range(NCH):
                            ins = e.matmul(psA[:, :n], win[:, c, ec * 128:(ec + 1) * 128], hbk[:, c, :n],
                                           start=(c == 0), stop=(c == NCH - 1))
                        for c in range(NCH):
                            ins = e.matmul(psG[:, :n], win[:, c, D + ec * 128:D + (ec + 1) * 128], hbk[:, c, :n],
                                           start=(c == 0), stop=(c == NCH - 1))
                        return ins
                    R.op("pe", mm, r=["win", "hbk"], w=[("ps", pa), ("ps", pg)])
                    gb = ng % 2
                    ng += 1
                    R.op("act", lambda e, psG=psG, gb=gb, ec=ec, n=n: e.activation(
                        out=gate[gb][:, :n], in_=psG[:, :n], func=AF.Sigmoid, bias=b2[:, ec:ec + 1]),
                        r=[("ps", pg), "vecs"], w=[("gate", gb)])
                    if not samp:
                        R.op("dve", lambda e, psA=psA, gb=gb, ec=ec, n=n: e.scalar_tensor_tensor(
                            out=up[:, ec, 30:30 + n], in0=psA[:, :n], scalar=b1[:, ec:ec + 1], in1=gate[gb][:, :n],
                            op0=ALU.add, op1=ALU.mult), r=[("ps", pa), ("gate", gb), "vecs"], w=["up"])
                    else:
                        R.op("dve", lambda e, psA=psA, gb=gb, ec=ec, n=n: e.scalar_tensor_tensor(
                            out=ups[:, ec, :, 30:38], in0=psA[:, :n].rearrange("p (b t) -> p b t", t=8),
                            scalar=b1[:, ec:ec + 1], in1=gate[gb][:, :n].rearrange("p (b t) -> p b t", t=8),
                            op0=ALU.add, op1=ALU.mult), r=[("ps", pa), ("gate", gb), "vecs"], w=["up"])
                for c in range(NCH):
                    def src(j, c=c, n=n, samp=samp):
                        if samp:
                            return ups[:, c, :, j:j + 8]
                        return up[:, c, j:j + n]

                    def dst(tile, n=n, samp=samp):
                        if samp:
                            return tile[:, :n].rearrange("p (b t) -> p b t", t=8)
                        return tile[:, :n]

                    za = dst(z[:, c, :])
                    R.op("dve", lambda e, c=c, src=src, za=za: e.tensor_scalar(
                        out=za, in0=src(0), scalar1=wdw[0][:, c:c + 1], scalar2=bdw[:, c:c + 1], op0=ALU.mult, op1=ALU.add),
                        r=["up", "vecs"], w=[("z", c)])
                    for j in range(1, 31):
                        R.op("dve", lambda e, c=c, j=j, src=src, za=za: e.scalar_tensor_tensor(
                            out=za, in0=src(j), scalar=wdw[j][:, c:c + 1], in1=za, op0=ALU.mult, op1=ALU.add),
                            r=["up", "vecs", ("z", c)], w=[("z", c)])
                    R.op("pool", lambda e, c=c, n=n: e.tensor_copy(out=zb[:, c, :n], in_=z[:, c, :n]), r=[("z", c)], w=[("zb", c)])
                    R.op("act", lambda e, c=c, n=n: e.activation(out=self.sq_scr[:, c, :n], in_=z[:, c, :n], func=AF.Square),
                         r=[("z", c)], w=["sq_scr"])
                if samp and DBG.get("dump"):
                    dd = self.dram_out("dbg_z", [128, NCH, 128])
                    R.dma("sp", dd[:], z[:, :, :128], r=[("z", c) for c in range(NCH)])
                    dd2 = self.dram_out("dbg_up", [128, NCH, NSEQ * 38])
                    R.dma("sp", dd2[:], up[:], r=["up"])
                def mmst(e, n=n):
                    ins = None
                    for c in range(NCH):
                        ins = e.matmul(psb[4][:, :n], self.ones_m[:], zb[:, c, :n], start=(c == 0), stop=(c == NCH - 1))
                    for c in range(NCH):
                        ins = e.matmul(psb[5][:, :n], self.ones_m[:], self.sq_scr[:, c, :n], start=(c == 0), stop=(c == NCH - 1))
                    return ins
                R.op("pe", mmst, r=["ones_m", "sq_scr"] + [("zb", c) for c in range(NCH)], w=[("ps", 4), ("ps", 5)])
                R.op("dve", lambda e, n=n: e.tensor_copy(out=mean[:, :n], in_=psb[4][:, :n]), r=[("ps", 4)], w=["mean"])
                R.op("dve", lambda e, n=n: e.tensor_tensor(out=var[:, :n], in0=mean[:, :n], in1=mean[:, :n], op=ALU.mult),
                     r=["mean"], w=["var"])
                R.op("dve", lambda e, n=n: e.tensor_tensor(out=var[:, :n], in0=psb[5][:, :n], in1=var[:, :n], op=ALU.subtract),
                     r=[("ps", 5), "var"], w=["var"])
                R.op("act", lambda e, n=n: e.activation(out=var[:, :n], in_=var[:, :n], func=AF.Sqrt, bias=self.eps_ln[:, 0:1]),
                     r=["var", "consts"], w=["var"])
                R.op("dve", lambda e, n=n: e.reciprocal(out=var[:, :n], in_=var[:, :n]), r=["var"], w=["var"])
                R.op("dve", lambda e, n=n: e.tensor_tensor(out=mr[:, :n], in0=mean[:, :n], in1=var[:, :n], op=ALU.mult),
                     r=["mean", "var"], w=["mr"])
                for c in range(NCH):
                    tb = c % 2
                    R.op("dve", lambda e, c=c, n=n, tb=tb: e.tensor_tensor(out=tmp[tb][:, :n], in0=z[:, c, :n], in1=var[:, :n],
                                                                           op=ALU.mult), r=[("z", c), "var"], w=[("tmpc", tb)])
                    R.op("dve", lambda e, c=c, n=n, tb=tb: e.tensor_tensor(out=tmp[tb][:, :n], in0=tmp[tb][:, :n], in1=mr[:, :n],
                                                                           op=ALU.subtract), r=[("tmpc", tb), "mr"], w=[("tmpc", tb)])
                    R.op("act", lambda e, c=c, n=n, tb=tb: e.activation(out=hbk[:, c, :n], in_=tmp[tb][:, :n], func=AF.Silu,
                                                                        bias=lnb[:, c:c + 1], scale=lng[:, c:c + 1]),
                         r=[("tmpc", tb), "vecs"], w=["hbk"])
                for ec in range(NCH):
                    pi = ec % 2
                    ps = psb[pi]

                    def mm(e, ps=ps, ec=ec, n=n):
                        ins = None
                        for c in range(NCH):
                            ins = e.matmul(ps[:, :n], wout[:, c, ec * 128:(ec + 1) * 128], hbk[:, c, :n],
                                           start=(c == 0), stop=(c == NCH - 1))
                        return ins
                    R.op("pe", mm, r=["wout", "hbk"], w=[("ps", pi)])
                    R.op("dve", lambda e, ps=ps, ec=ec, t0=t0, n=n: e.scalar_tensor_tensor(
                        out=self.x[:, ec, t0:t0 + n], in0=ps[:, :n], scalar=bout[:, ec:ec + 1], in1=self.x[:, ec, t0:t0 + n],
                        op0=ALU.add, op1=ALU.add), r=[("ps", pi), "vecs"], w=[("x", ec, blk)])
                if blk == 3:
                    for c in range(NCH):
                        R.dma("sp", convp_d[ci, c * 128:(c + 1) * 128, :], up[:, c, 512:542], r=["up"])
                if samp:
                    for c in range(NCH):
                        R.dma("sp", convs_d[ci, c * 128:(c + 1) * 128], ups[:, c, :, 8:38], r=["up"])
            R.barrier()
            self.stack = old


    def phase_rwkv(self, l):
        R = self.R
        nc = self.nc
        ri = l // 2
        NB = 128
        xs_d = self.xspill
        xs_v = xs_d.rearrange("(c p) t -> p c t", p=128)
        allx = [("x", c, b) for c in range(NCH) for b in range(len(BLOCKS))]
        for c in range(NCH):
            R.dma("sp", xs_v[:, c, :], self.x[:, c, :], r=allx)
        R.barrier()
        with contextlib.ExitStack() as st:
            old, self.stack = self.stack, st
            self.psb_save = self.psb
            def slot(j):
                return self.x[:, j // 2, (j % 2) * 1024:(j % 2) * 1024 + 1024]
            def bslot(j):
                return slot(j).rearrange("p (a b) -> p a b", b=NB)
            names = ["xb", "hf", "xx", "rf", "kf", "vf", "lw", "cs1", "cs2", "eNi", "af", "kkn", "yf"]
            Fb = {nm: bslot(i) for i, nm in enumerate(names)}
            SAV = slot(13)[:, 0:512].rearrange("p (c v) -> p c v", v=64)
            ytm = slot(14)[:, 0:512].rearrange("p (c v) -> p c v", v=64)
            yc = slot(15)[:, 0:512].rearrange("p (c v) -> p c v", v=64)
            bf = lambda nm, shape: self.sb(nm, shape, BF16)
            xm = [bf("xm0", [128, NCH, NB])] * 2
            At, Bt, Kt, Rt, Vb, BWb, KWb, gfb, sqb = [bf(nm, [128, NCH, NB]) for nm in
                                                      ("At", "Bt", "Kt", "Rt", "Vb", "BWb", "KWb", "gfb", "sqb")]
            yg = xm[0]
            tw = bf("tw", [64, NB]); ta = bf("ta", [64, NB]); tv = bf("tv", [32, NB]); tg = bf("tg", [128, 2, NB])
            Pm, Qm, Tm, MKA, MBR, MKR, AtT, VT, BWT, KWT, XV, SAb = [
                bf(nm, [128, NCH, 64]) for nm in ("Pm", "Qm", "Tm", "MKA", "MBR", "MKR", "AtT", "VT", "BWT", "KWT", "XV", "SAb")]
            Ah = bf("Ah", [128, NCH, 64])
            Sf = self.sb("Sf", [128, NCH, 64], F32)
            Sb = bf("Sb", [128, NCH, 64])
            WC = self.sb("WC", [128, NCH, 16], F32)
            hlast = self.sb("hlast", [128, NCH, 16], F32)
            st1 = self.sb("st1", [128, NCH], F32)
            st2 = self.sb("st2", [128, NCH], F32)
            rnb = self.sb("rnb", [128, NB], F32)
            wr, wk, wv, wo = [bf(nm, [128, NCH, D]) for nm in ("wr", "wk", "wv", "wo")]
            w1 = bf("w1", [128, NCH, 64]); w2 = bf("w2", [64, D])
            a1 = bf("a1", [128, NCH, 64]); a2 = bf("a2", [64, D])
            g1 = bf("g1", [128, NCH, 160]); g2 = bf("g2", [128, 2, D])
            if ri > 0:
                v1 = bf("v1", [128, NCH, 32]); v2 = bf("v2", [32, D])
            ps = [self.ps(f"rps{i}", [128, 512], F32) for i in range(8)]
            cm = self.cmask
            mSU, mSL, mU, mI = (cm[:, i] for i in range(4))

            def wload(dst, name, shape, view, nsplit=1):
                src = self.dram_in(name, shape)[ri if name not in ("rw_v1", "rw_v2") else 0]
                src = src.rearrange(view, p=128) if view else src
                if nsplit == 1:
                    R.dma("pool", dst, src, w=[name])
                else:
                    for j in range(nsplit):
                        R.dma("pool", dst[:, :, j * 512:(j + 1) * 512], src[:, :, j * 512:(j + 1) * 512], w=[name])
            for t_, nm in ((wr, "rw_w_r"), (wk, "rw_w_k"), (wv, "rw_w_v"), (wo, "rw_w_o")):
                wload(t_[:], nm, [2, D, D], "(c p) e -> p c e", 2)
            wload(w1[:], "rw_w1", [2, D, 64], "(c p) e -> p c e")
            wload(w2[:], "rw_w2", [2, 64, D], None)
            wload(a1[:], "rw_a1", [2, D, 64], "(c p) e -> p c e")
            wload(a2[:], "rw_a2", [2, 64, D], None)
            wload(g1[:], "rw_g1", [2, D, 160], "(c p) e -> p c e")
            g2d = self.dram_in("rw_g2", [2, 160, D])[ri]
            R.dma("pool", g2[:, 0, :], g2d[0:128, :], w=["rw_g2"])
            R.dma("pool", g2[0:32, 1, :], g2d[128:160, :], w=["rw_g2"])
            if ri > 0:
                wload(v1[:], "rw_v1", [1, D, 32], "(c p) e -> p c e")
                wload(v2[:], "rw_v2", [1, 32, D], None)
            WALL = ["rw_w_r", "rw_w_k", "rw_w_v", "rw_w_o", "rw_w1", "rw_w2", "rw_a1", "rw_a2", "rw_g1", "rw_g2",
                    "rw_v1", "rw_v2"]
            sshift_d = self.dram_in("sshiftT", [2, D, NSEQ])
            swkv_d = self.dram_in("swkvT", [2, NSEQ, 16, 64, 64])
            shiftp_d = self.dram_out("shiftpT", [2, 128, NCH])
            shifts_d = self.dram_out("shiftsT", [2, 128, NCH, NSEQ])
            wkvp_d = self.dram_out("wkvpT", [2, 16, 64, 64])
            wkvs_d = self.dram_out("wkvsT", [2, NSEQ, 16, 64, 64])
            vfd = self.vfirst_d.rearrange("(c p) t -> p c t", p=128)

            def V(key):
                return self.vec(key)[:, :].unsqueeze(2).to_broadcast([128, NCH, NB])

            def dve(fn, r, w):
                R.op("dve", fn, r=list(r) + ["vecs", "consts"], w=w)

            def act(fn, r, w):
                R.op("act", fn, r=list(r) + ["vecs", "consts"], w=w)

            npj = [0]

            def proj(W, wname, src, srcres, cols, evac):
                for ec in range(cols // 128):
                    pi = 4 + npj[0] % 3
                    npj[0] += 1
                    p_ = ps[pi]

                    def mm(e, p_=p_, ec=ec):
                        ins = None
                        for c in range(NCH):
                            ins = e.matmul(p_[:, :NB], W[:, c, ec * 128:(ec + 1) * 128], src[:, c, :],
                                           start=(c == 0), stop=(c == NCH - 1))
                        return ins
                    R.op("pe", mm, r=[wname, srcres], w=[("rps", pi)])
                    evac(p_, pi, ec)

            R.op("pool", lambda e: e.memset(Sf[:], 0.0), w=["Sf"])
            R.op("pool", lambda e: e.memset(Sb[:], 0.0), w=["Sb"])
            R.op("pool", lambda e: e.memset(hlast[:], 0.0), w=["hlast"])

            nblk = 17

            def do_block(blk):
                samp = blk == 16
                t0 = blk * NB
                C = 8 if samp else 64
                NCK = NB // C
                LV = 2 if samp else 5
                xb, hf, xx = Fb["xb"], Fb["hf"], Fb["xx"]
                rf, kf, vf, lw, cs1, cs2 = Fb["rf"], Fb["kf"], Fb["vf"], Fb["lw"], Fb["cs1"], Fb["cs2"]
                eNi, af, kkn, yf = Fb["eNi"], Fb["af"], Fb["kkn"], Fb["yf"]
                R.dma("sp", xb, xs_v[:, :, t0:t0 + NB], w=["xb"])
                act(lambda e: e.activation(out=sqb[:], in_=xb, func=AF.Square), ["xb"], ["sqb"])

                def mmn(e):
                    ins = None
                    for c in range(NCH):
                        ins = e.matmul(ps[6][:, :NB], self.ones_m[:], sqb[:, c, :], start=(c == 0), stop=(c == NCH - 1))
                    return ins
                R.op("pe", mmn, r=["sqb", "ones_m"], w=[("rps", 6)])
                act(lambda e: e.activation(out=rnb[:], in_=ps[6][:, :NB], func=AF.Sqrt, bias=self.eps_rms[:, 0:1]),
                    [("rps", 6)], ["rnb"])
                dve(lambda e: e.reciprocal(out=rnb[:], in_=rnb[:]), ["rnb"], ["rnb"])
                dve(lambda e: e.tensor_tensor(out=hf, in0=xb, in1=V(("norm_g", l, 0)), op=ALU.mult), ["xb"], ["hf"])
                dve(lambda e: e.tensor_tensor(out=hf, in0=hf, in1=rnb[:, :].unsqueeze(1).to_broadcast([128, NCH, NB]),
                                              op=ALU.mult), ["hf", "rnb"], ["hf"])
                if samp:
                    for c in range(NCH):
                        R.dma("sp", hlast[:, c, :], sshift_d[ri, c * 128:(c + 1) * 128, :], w=["hlast"])
                    h4 = hf.rearrange("p c (b t) -> p c b t", t=8)
                    x4 = xx.rearrange("p c (b t) -> p c b t", t=8)
                    for c in range(NCH):
                        dve(lambda e, c=c: e.tensor_tensor(out=x4[:, c, :, 1:8], in0=h4[:, c, :, 0:7], in1=h4[:, c, :, 1:8],
                                                           op=ALU.subtract), ["hf"], ["xx"])
                        dve(lambda e, c=c: e.tensor_tensor(out=x4[:, c, :, 0], in0=hlast[:, c, :], in1=h4[:, c, :, 0],
                                                           op=ALU.subtract), ["hf", "hlast"], ["xx"])
                    for c in range(NCH):
                        R.dma("sp", shifts_d[ri, :, c, :], h4[:, c, :, 7], r=["hf"], allow_slow_non_contiguous=True)
                else:
                    dve(lambda e: e.tensor_tensor(out=xx[:, :, 1:NB], in0=hf[:, :, 0:NB - 1], in1=hf[:, :, 1:NB],
                                                  op=ALU.subtract), ["hf"], ["xx"])
                    dve(lambda e: e.tensor_tensor(out=xx[:, :, 0], in0=hlast[:, :, 0], in1=hf[:, :, 0], op=ALU.subtract),
                        ["hf", "hlast"], ["xx"])
                    dve(lambda e: e.tensor_copy(out=hlast[:, :, 0], in_=hf[:, :, NB - 1]), ["hf", "xx"], ["hlast"])
                    if blk == 15:
                        R.dma("sp", shiftp_d[ri], hlast[:, :, 0], r=["hlast"], allow_slow_non_contiguous=True)
                nmx = [0]

                def mix(i):
                    b = 0
                    dve(lambda e: e.tensor_tensor(out=xm[b][:], in0=xx, in1=V(("rw_mix", ri, i)), op=ALU.mult),
                        ["xx"], [("xm", b)])
                    dve(lambda e: e.tensor_tensor(out=xm[b][:], in0=xm[b][:], in1=hf, op=ALU.add), ["hf", ("xm", b)], [("xm", b)])
                    return xm[b], ("xm", b)

                def lora1(W, wname, src, srcres, rank, dst, dstres, func):
                    pi = 4 + npj[0] % 3
                    npj[0] += 1
                    p_ = ps[pi]

                    def mm(e):
                        ins = None
                        for c in range(NCH):
                            ins = e.matmul(p_[:rank, :NB], W[:, c, :rank], src[:, c, :], start=(c == 0), stop=(c == NCH - 1))
                        return ins
                    R.op("pe", mm, r=[wname, srcres], w=[("rps", pi)])
                    if func is None:
                        dve(lambda e: e.tensor_copy(out=dst[:rank, :], in_=p_[:rank, :NB]), [("rps", pi)], [dstres])
                    else:
                        act(lambda e: e.activation(out=dst[:rank, :], in_=p_[:rank, :NB], func=func), [("rps", pi)], [dstres])

                def lora2(W2, wname, mid, midres, rank, evac):
                    for ec in range(NCH):
                        pi = 4 + npj[0] % 3
                        npj[0] += 1
                        p_ = ps[pi]
                        R.op("pe", lambda e, p_=p_, ec=ec: e.matmul(p_[:, :NB], W2[:rank, ec * 128:(ec + 1) * 128], mid[:rank, :],
                                                                    start=True, stop=True),
                             r=[wname, midres], w=[("rps", pi)])
                        evac(p_, pi, ec)

                xw, xwr = mix(1)
                lora1(w1, "rw_w1", xw, xwr, 64, tw, "tw", AF.Tanh)
                w0v = self.vec(("rw_w0", ri))
                lora2(w2, "rw_w2", tw, "tw", 64, lambda p_, pi, ec: act(
                    lambda e: e.activation(out=lw[:, ec, :], in_=p_[:, :NB], func=AF.Sigmoid, bias=w0v[:, ec:ec + 1]),
                    [("rps", pi)], ["lw"]))
                dve(lambda e: e.tensor_scalar(out=lw, in0=lw, scalar1=-0.6065306597126334, scalar2=None, op0=ALU.mult),
                    ["lw"], ["lw"])
                lw4 = lw.rearrange("p c (k t) -> p c k t", t=C)
                a4 = cs1.rearrange("p c (k t) -> p c k t", t=C)
                b4 = cs2.rearrange("p c (k t) -> p c k t", t=C)
                src4, srcn = lw4, "lw"
                dsts = [(a4, "cs1"), (b4, "cs2")]
                sh = 1
                k_ = 0
                while sh < C:
                    d4, dn = dsts[k_ % 2]
                    for c in range(NCH):
                        dve(lambda e, c=c, d4=d4, src4=src4, sh=sh: e.tensor_tensor(
                            out=d4[:, c, :, sh:C], in0=src4[:, c, :, sh:C], in1=src4[:, c, :, 0:C - sh], op=ALU.add),
                            [srcn], [dn])
                        dve(lambda e, c=c, d4=d4, src4=src4, sh=sh: e.tensor_copy(out=d4[:, c, :, 0:sh], in_=src4[:, c, :, 0:sh]),
                            [srcn], [dn])
                    src4, srcn = d4, dn
                    sh *= 2
                    k_ += 1
                Li4, Lin = src4, srcn
                Li = cs1 if Lin == "cs1" else cs2
                Le, Len = (cs2, "cs2") if Lin == "cs1" else (cs1, "cs1")
                dve(lambda e: e.tensor_tensor(out=Le, in0=Li, in1=lw, op=ALU.subtract), [Lin, "lw"], [Len])
                for c in range(NCH):
                    act(lambda e, c=c: e.activation(out=WC[:, c, :NCK], in_=Li4[:, c, :, C - 1], func=AF.Exp), [Lin], ["WC"])
                act(lambda e: e.activation(out=eNi, in_=Li, func=AF.Exp, scale=-1.0), [Lin], ["eNi"])
                act(lambda e: e.activation(out=Le, in_=Le, func=AF.Exp), [Len], [Len])
                act(lambda e: e.activation(out=Li, in_=Li, func=AF.Exp), [Lin], [Lin])
                eLe, eLen, eLi, eLin = Le, Len, Li, Lin
                xr, xrr = mix(0)

                def ev_r(p_, pi, ec):
                    dve(lambda e: e.tensor_copy(out=rf[:, ec, :], in_=p_[:, :NB]), [("rps", pi)], ["rf"])
                    dve(lambda e: e.tensor_tensor(out=Rt[:, ec, :], in0=p_[:, :NB], in1=eLi[:, ec, :], op=ALU.mult),
                        [("rps", pi), eLin], ["Rt"])
                proj(wr, "rw_w_r", xr, xrr, D, ev_r)
                xa, xar = mix(4)
                lora1(a1, "rw_a1", xa, xar, 64, ta, "ta", None)
                a0v = self.vec(("rw_a0", ri))
                lora2(a2, "rw_a2", ta, "ta", 64, lambda p_, pi, ec: act(
                    lambda e: e.activation(out=af[:, ec, :], in_=p_[:, :NB], func=AF.Sigmoid, bias=a0v[:, ec:ec + 1]),
                    [("rps", pi)], ["af"]))
                xk, xkr = mix(2)
                proj(wk, "rw_w_k", xk, xkr, D, lambda p_, pi, ec: dve(
                    lambda e: e.tensor_copy(out=kf[:, ec, :], in_=p_[:, :NB]), [("rps", pi)], ["kf"]))
                dve(lambda e: e.tensor_tensor(out=kkn, in0=kf, in1=V(("rw_k_k", ri)), op=ALU.mult), ["kf"], ["kkn"])
                act(lambda e: e.activation(out=sqb[:], in_=kkn, func=AF.Square), ["kkn"], ["sqb"])
                for c in range(NCH):
                    pi = 4 + npj[0] % 3
                    npj[0] += 1
                    p_ = ps[pi]
                    R.op("pe", lambda e, p_=p_, c=c: e.matmul(p_[:, :NB], self.blk64[:], sqb[:, c, :], start=True, stop=True),
                         r=["sqb", "consts"], w=[("rps", pi)])
                    act(lambda e, p_=p_: e.activation(out=rnb[:], in_=p_[:, :NB], func=AF.Sqrt), [("rps", pi)], ["rnb"])
                    dve(lambda e: e.tensor_scalar(out=rnb[:], in0=rnb[:], scalar1=1e-12, scalar2=None, op0=ALU.max), ["rnb"], ["rnb"])
                    dve(lambda e: e.reciprocal(out=rnb[:], in_=rnb[:]), ["rnb"], ["rnb"])
                    dve(lambda e, c=c: e.tensor_tensor(out=kkn[:, c, :], in0=kkn[:, c, :], in1=rnb[:], op=ALU.mult),
                        ["kkn", "rnb"], ["kkn"])
                dve(lambda e: e.scalar_tensor_tensor(out=At[:], in0=kkn, scalar=-1.0, in1=eLe, op0=ALU.mult, op1=ALU.mult),
                    ["kkn", eLen], ["At"])
                tmpA, tmpAn = eLe, eLen
                dve(lambda e: e.scalar_tensor_tensor(out=tmpA, in0=af, scalar=-1.0, in1=V(("rw_k_a", ri)), op0=ALU.add, op1=ALU.mult),
                    ["af", "At"], [tmpAn])
                dve(lambda e: e.scalar_tensor_tensor(out=kf, in0=tmpA, scalar=1.0, in1=kf, op0=ALU.add, op1=ALU.mult),
                    [tmpAn, "kf"], ["kf"])
                dve(lambda e: e.tensor_tensor(out=kkn, in0=kkn, in1=af, op=ALU.mult), ["kkn", "af", "At"], ["kkn"])
                dve(lambda e: e.tensor_tensor(out=kkn, in0=kkn, in1=eNi, op=ALU.mult), ["kkn", "eNi"], ["kkn"])
                dve(lambda e: e.tensor_copy(out=Bt[:], in_=kkn), ["kkn"], ["Bt"])
                WCb = WC[:, :, :NCK].unsqueeze(3).to_broadcast([128, NCH, NCK, C])
                dve(lambda e: e.tensor_tensor(out=BWb[:].rearrange("p c (k t) -> p c k t", t=C),
                                              in0=kkn.rearrange("p c (k t) -> p c k t", t=C), in1=WCb, op=ALU.mult),
                    ["kkn", "WC"], ["BWb"])
                dve(lambda e: e.tensor_tensor(out=tmpA, in0=rf, in1=V(("rw_r_k", ri)), op=ALU.mult), ["rf", "kf"], [tmpAn])
                dve(lambda e: e.tensor_tensor(out=sqb[:], in0=tmpA, in1=kf, op=ALU.mult), [tmpAn, "kf", "kkn"], ["sqb"])
                dve(lambda e: e.tensor_tensor(out=tmpA, in0=kf, in1=eNi, op=ALU.mult), ["kf", "eNi", "sqb"], [tmpAn])
                dve(lambda e: e.tensor_copy(out=Kt[:], in_=tmpA), [tmpAn], ["Kt"])
                dve(lambda e: e.tensor_tensor(out=KWb[:].rearrange("p c (k t) -> p c k t", t=C),
                                              in0=tmpA.rearrange("p c (k t) -> p c k t", t=C), in1=WCb, op=ALU.mult),
                    [tmpAn, "WC"], ["KWb"])
                xv, xvr = mix(3)
                proj(wv, "rw_w_v", xv, xvr, D, lambda p_, pi, ec: dve(
                    lambda e: e.tensor_copy(out=vf[:, ec, :], in_=p_[:, :NB]), [("rps", pi)], ["vf"]))
                if ri == 0:
                    R.dma("sp", vfd[:, :, t0:t0 + NB], vf, r=["vf"])
                else:
                    lora1(v1, "rw_v1", xv, xvr, 32, tv, "tv", None)
                    v0v = self.vec(("rw_v0", 0))
                    vg, vgn = eLi, eLin
                    lora2(v2, "rw_v2", tv, "tv", 32, lambda p_, pi, ec: act(
                        lambda e: e.activation(out=vg[:, ec, :], in_=p_[:, :NB], func=AF.Sigmoid, bias=v0v[:, ec:ec + 1]),
                        [("rps", pi), "Rt"], [vgn]))
                    R.dma("sp", tmpA, vfd[:, :, t0:t0 + NB], r=["Kt", "KWb"], w=[tmpAn])
                    dve(lambda e: e.tensor_tensor(out=tmpA, in0=tmpA, in1=vf, op=ALU.subtract), [tmpAn, "vf"], [tmpAn])
                    dve(lambda e: e.tensor_tensor(out=tmpA, in0=tmpA, in1=vg, op=ALU.mult), [tmpAn, vgn], [tmpAn])
                    dve(lambda e: e.tensor_tensor(out=vf, in0=vf, in1=tmpA, op=ALU.add), [tmpAn, "vf"], ["vf"])
                dve(lambda e: e.tensor_copy(out=Vb[:], in_=vf), ["vf"], ["Vb"])
                xg, xgr = mix(5)
                for hf_ in range(2):
                    rk_ = 128 if hf_ == 0 else 32
                    pi = 4 + npj[0] % 3
                    npj[0] += 1
                    p_ = ps[pi]

                    def mmg(e, p_=p_, hf_=hf_, rk_=rk_):
                        ins = None
                        for c in range(NCH):
                            ins = e.matmul(p_[:rk_, :NB], g1[:, c, hf_ * 128:hf_ * 128 + rk_], xg[:, c, :],
                                           start=(c == 0), stop=(c == NCH - 1))
                        return ins
                    R.op("pe", mmg, r=["rw_g1", xgr], w=[("rps", pi)])
                    act(lambda e, p_=p_, hf_=hf_, rk_=rk_: e.activation(out=tg[:rk_, hf_, :], in_=p_[:rk_, :NB], func=AF.Sigmoid),
                        [("rps", pi)], ["tg"])
                for ec in range(NCH):
                    pi = 4 + npj[0] % 3
                    npj[0] += 1
                    p_ = ps[pi]

                    def mmg2(e, p_=p_, ec=ec):
                        e.matmul(p_[:, :NB], g2[:, 0, ec * 128:(ec + 1) * 128], tg[:, 0, :], start=True, stop=False)
                        return e.matmul(p_[:, :NB], g2[:32, 1, ec * 128:(ec + 1) * 128], tg[:32, 1, :], start=False, stop=True)
                    R.op("pe", mmg2, r=["rw_g2", "tg"], w=[("rps", pi)])
                    dve(lambda e, p_=p_, ec=ec: e.tensor_copy(out=gfb[:, ec, :], in_=p_[:, :NB]), [("rps", pi)], ["gfb"])

                RL = [(0, 128)] if C == 64 else [(0, C), (64, 64 + C)]

                def headmm2(pi, lhs, rhs, mrows, ncols, rres):
                    def f(e):
                        ins = None
                        for h in range(16):
                            pb, c = (h % 2) * 64, h // 2
                            ins = e.matmul(ps[pi][pb:pb + mrows, c * 64:c * 64 + ncols], lhs(pb, c), rhs(pb, c),
                                           start=True, stop=True)
                        return ins
                    R.op("pe", f, r=rres, w=[("rps", pi)])

                def pv(pi, ncols):
                    return ps[pi][:, :].rearrange("p (c t) -> p c t", t=64)[:, :, :ncols]

                def rows_op(fn, r, w):
                    for (r0, r1) in RL:
                        dve(lambda e, r0=r0, r1=r1: fn(e, r0, r1), r, w)

                def do_chunk(ck):
                    o = ck * C
                    fmx = lambda tile: (lambda pb, c: tile[pb:pb + 64, c, o:o + C])
                    tk = lambda tile: (lambda pb, c: tile[pb:pb + C, c, :C])
                    tvv = lambda tile: (lambda pb, c: tile[pb:pb + C, c, :])

                    def evm(dst, dstn, pi, mask):
                        rows_op(lambda e, r0, r1: e.tensor_tensor(out=dst[r0:r1, :, :C], in0=pv(pi, C)[r0:r1],
                                                                  in1=mask[r0:r1, :, :C], op=ALU.mult),
                                [("rps", pi)], [dstn])

                    def evc(dst, dstn, pi, ncols):
                        rows_op(lambda e, r0, r1: e.tensor_copy(out=dst[r0:r1, :, :ncols], in_=pv(pi, ncols)[r0:r1]),
                                [("rps", pi)], [dstn])

                    headmm2(0, fmx(Bt), fmx(At), C, C, ["Bt", "At"])
                    evm(Pm, "Pm", 0, mSU)
                    headmm2(1, fmx(At), fmx(Bt), C, C, ["Bt", "At"])
                    evm(Qm, "Qm", 1, mSL)
                    rows_op(lambda e, r0, r1: e.tensor_tensor(out=Tm[r0:r1, :, :C], in0=Pm[r0:r1, :, :C],
                                                              in1=mI[r0:r1, :, :C], op=ALU.add), ["Pm"], ["Tm"])
                    headmm2(2, fmx(Kt), fmx(At), C, C, ["Kt", "At"])
                    evm(MKA, "MKA", 2, mSU)
                    headmm2(3, fmx(Bt), fmx(Rt), C, C, ["Bt", "Rt"])
                    evm(MBR, "MBR", 3, mU)
                    headmm2(0, fmx(Kt), fmx(Rt), C, C, ["Kt", "Rt"])
                    evm(MKR, "MKR", 0, mU)
                    if DBG.get("rl", 9) < 2.2:
                        return
                    for n_ in range(1, LV + 1):
                        if n_ < LV:
                            headmm2(0, tk(Qm), tk(Pm), C, C, ["Pm", "Qm"])
                        headmm2(1, tk(Pm), tk(Qm), C, C, ["Pm", "Qm"])
                        if n_ < LV:
                            evc(Pm, "Pm", 0, C)
                        evc(Qm, "Qm", 1, C)
                        headmm2(2, tk(Qm), tk(Tm), C, C, ["Qm", "Tm"])
                        rows_op(lambda e, r0, r1: e.tensor_tensor(out=Tm[r0:r1, :, :C], in0=pv(2, C)[r0:r1],
                                                                  in1=Tm[r0:r1, :, :C], op=ALU.add),
                                [("rps", 2), "Tm"], ["Tm"])
                    if DBG.get("rl", 9) < 2.5:
                        return
                    ptv = pv(4, 64)
                    for srcT, srcn_, dstT, dstn_ in ((At, "At", AtT, "AtT"), (Vb, "Vb", VT, "VT"),
                                                     (BWb, "BWb", BWT, "BWT"), (KWb, "KWb", KWT, "KWT")):
                        def ftr(e, srcT=srcT):
                            ins = None
                            for h in range(16):
                                pb, c = (h % 2) * 64, h // 2
                                ins = e.matmul(ps[4][pb:pb + C, c * 64:(c + 1) * 64], srcT[pb:pb + 64, c, o:o + C],
                                               self.identb[pb:pb + 64, pb:pb + 64], start=True, stop=True)
                            return ins
                        R.op("pe", ftr, r=[srcn_, "consts"], w=[("rps", 4)])
                        rows_op(lambda e, r0, r1, dstT=dstT: e.tensor_copy(out=dstT[r0:r1, :, :], in_=ptv[r0:r1]),
                                [("rps", 4)], [dstn_])
                    headmm2(3, tvv(AtT), tk(Tm), 64, C, ["AtT", "Tm"])
                    dve(lambda e: e.tensor_copy(out=Ah[:, :, :C], in_=pv(3, C)), [("rps", 3)], ["Ah"])
                    if DBG.get("rl", 9) < 2.7:
                        return
                    headmm2(0, tk(MKA), tvv(VT), C, 64, ["MKA", "VT"])
                    evc(XV, "XV", 0, 64)
                    headmm2(1, tk(Tm), tvv(XV), C, 64, ["Tm", "XV"])
                    evc(SAV, "SAV", 1, 64)
                    if DBG.get("rl", 9) < 3:
                        return
                    if samp:
                        R.dma("sp", Sf[:], swkv_d[ri, ck].rearrange("(c h2) k v -> (h2 k) c v", h2=2), w=["Sf"])
                        dve(lambda e: e.tensor_copy(out=Sb[:], in_=Sf[:]), ["Sf"], ["Sb"])
                    fS = lambda pb, c: Sb[pb:pb + 64, c, :]
                    headmm2(2, lambda pb, c: Ah[pb:pb + 64, c, :C], fS, C, 64, ["Ah", "Sb"])
                    rows_op(lambda e, r0, r1: e.tensor_tensor(out=SAb[r0:r1, :, :], in0=pv(2, 64)[r0:r1],
                                                              in1=SAV[r0:r1, :, :], op=ALU.add),
                            [("rps", 2), "SAV"], ["SAb"])

                    def fy(e):
                        ins = None
                        for h in range(16):
                            pb, c = (h % 2) * 64, h // 2
                            o_ = ps[3][pb:pb + C, c * 64:(c + 1) * 64]
                            e.matmul(o_, Rt[pb:pb + 64, c, o:o + C], Sb[pb:pb + 64, c, :], start=True, stop=False)
                            e.matmul(o_, MBR[pb:pb + C, c, :C], SAb[pb:pb + C, c, :], start=False, stop=False)
                            ins = e.matmul(o_, MKR[pb:pb + C, c, :C], VT[pb:pb + C, c, :], start=False, stop=True)
                        return ins
                    R.op("pe", fy, r=["Rt", "Sb", "MBR", "SAb", "MKR", "VT"], w=[("rps", 3)])
                    evc(ytm, "ytm", 3, 64)

                    def fs(e):
                        ins = None
                        for h in range(16):
                            pb, c = (h % 2) * 64, h // 2
                            o_ = ps[0][pb:pb + 64, c * 64:(c + 1) * 64]
                            e.matmul(o_, BWT[pb:pb + C, c, :], SAb[pb:pb + C, c, :], start=True, stop=False)
                            ins = e.matmul(o_, KWT[pb:pb + C, c, :], VT[pb:pb + C, c, :], start=False, stop=True)
                        return ins
                    R.op("pe", fs, r=["BWT", "SAb", "KWT", "VT"], w=[("rps", 0)])
                    dve(lambda e: e.tensor_tensor(out=Sf[:], in0=Sf[:], in1=WC[:, :, ck:ck + 1].to_broadcast([128, NCH, 64]),
                                                  op=ALU.mult), ["Sf", "WC"], ["Sf"])
                    dve(lambda e: e.tensor_tensor(out=Sf[:], in0=Sf[:], in1=pv(0, 64), op=ALU.add), ["Sf", ("rps", 0)], ["Sf"])
                    dve(lambda e: e.tensor_copy(out=Sb[:], in_=Sf[:]), ["Sf"], ["Sb"])
                    if samp:
                        R.dma("sp", wkvs_d[ri, ck].rearrange("(c h2) k v -> (h2 k) c v", h2=2), Sf[:], r=["Sf"])
                    elif blk == 15 and ck == NCK - 1:
                        R.dma("sp", wkvp_d[ri].rearrange("(c h2) k v -> (h2 k) c v", h2=2), Sf[:], r=["Sf"])
                    if DBG.get("rl", 9) < 4:
                        return
                    rows_op(lambda e, r0, r1: e.tensor_reduce(out=st1[r0:r1, :], in_=ytm[r0:r1], axis=AX.X, op=ALU.add),
                            ["ytm"], ["st1"])
                    rows_op(lambda e, r0, r1: e.tensor_scalar(out=st1[r0:r1, :], in0=st1[r0:r1, :], scalar1=1.0 / 64,
                                                              scalar2=None, op0=ALU.mult), ["st1"], ["st1"])
                    rows_op(lambda e, r0, r1: e.tensor_tensor(out=yc[r0:r1], in0=ytm[r0:r1],
                                                              in1=st1[r0:r1, :].unsqueeze(2).to_broadcast([r1 - r0, NCH, 64]),
                                                              op=ALU.subtract), ["ytm", "st1"], ["yc"])
                    rows_op(lambda e, r0, r1: e.tensor_tensor(out=ytm[r0:r1], in0=yc[r0:r1], in1=yc[r0:r1], op=ALU.mult),
                            ["yc"], ["ytm"])
                    rows_op(lambda e, r0, r1: e.tensor_reduce(out=st2[r0:r1, :], in_=ytm[r0:r1], axis=AX.X, op=ALU.add),
                            ["ytm"], ["st2"])
                    for (r0, r1) in RL:
                        act(lambda e, r0=r0, r1=r1: e.activation(out=st2[r0:r1, :], in_=st2[r0:r1, :], func=AF.Sqrt,
                                                                 scale=1.0 / 64, bias=self.eps_lnx[r0:r1, 0:1]),
                            ["st2"], ["st2"])
                    rows_op(lambda e, r0, r1: e.reciprocal(out=st2[r0:r1, :], in_=st2[r0:r1, :]), ["st2"], ["st2"])
                    rows_op(lambda e, r0, r1: e.tensor_tensor(out=yc[r0:r1], in0=yc[r0:r1],
                                                              in1=st2[r0:r1, :].unsqueeze(2).to_broadcast([r1 - r0, NCH, 64]),
                                                              op=ALU.mult), ["yc", "st2"], ["yc"])

                    def ftb(e):
                        ins = None
                        for h in range(16):
                            pb, c = (h % 2) * 64, h // 2
                            ins = e.matmul(ps[1][pb:pb + 64, c * 64:c * 64 + C], yc[pb:pb + C, c, :],
                                           self.identf[pb:pb + C, pb:pb + C], start=True, stop=True)
                        return ins
                    R.op("pe", ftb, r=["yc", "consts"], w=[("rps", 1)])
                    dve(lambda e: e.tensor_copy(out=yf[:, :, o:o + C], in_=pv(1, C)), [("rps", 1)], ["yf"])

                for ck in range(NCK if DBG.get("rl", 9) >= 2 else 0):
                    do_chunk(ck)
                dve(lambda e: e.tensor_tensor(out=yf, in0=yf, in1=V(("rw_lnx_g", ri)), op=ALU.mult), ["yf"], ["yf"])
                dve(lambda e: e.tensor_tensor(out=yf, in0=yf, in1=V(("rw_lnx_b", ri)), op=ALU.add), ["yf"], ["yf"])
                for c in range(NCH):
                    pi = 4 + npj[0] % 3
                    npj[0] += 1
                    p_ = ps[pi]
                    R.op("pe", lambda e, p_=p_, c=c: e.matmul(p_[:, :NB], self.blk64[:], sqb[:, c, :], start=True, stop=True),
                         r=["sqb", "consts"], w=[("rps", pi)])
                    dve(lambda e, p_=p_, c=c: e.tensor_tensor(out=rnb[:], in0=p_[:, :NB], in1=vf[:, c, :], op=ALU.mult),
                        [("rps", pi), "vf"], ["rnb"])
                    dve(lambda e, c=c: e.tensor_tensor(out=yf[:, c, :], in0=yf[:, c, :], in1=rnb[:], op=ALU.add),
                        ["yf", "rnb"], ["yf"])
                dve(lambda e: e.tensor_tensor(out=yg[:], in0=yf, in1=gfb[:], op=ALU.mult), ["yf", "gfb"], [("xm", 0)])

                def ev_o(p_, pi, ec):
                    dve(lambda e: e.tensor_tensor(out=xb[:, ec, :], in0=xb[:, ec, :], in1=p_[:, :NB], op=ALU.add),
                        [("rps", pi), "xb"], ["xb"])
                proj(wo, "rw_w_o", yg, ("xm", 0), D, ev_o)
                R.dma("sp", xs_v[:, :, t0:t0 + NB], xb, r=["xb"])
            for blk in (DBG.get("blks", range(nblk)) if DBG.get("rl", 9) >= 1 else []):
                do_block(blk)
            R.barrier()
            self.psb = self.psb_save
            self.stack = old
        for c in range(NCH):
            R.dma("sp", self.x[:, c, :], xs_v[:, c, :], w=[("x", c, b) for b in range(len(BLOCKS))])
        R.barrier()

    def phase_attn(self, l):
        R = self.R
        with contextlib.ExitStack() as st:
            old, self.stack = self.stack, st
            self.sq_scr = self.sb("sq_scr", [128, NCH, 512], BF16)
            self.rstd = self.sb("rstd", [128, 512], F32)
            self.psb = [self.ps(f"psb{i}", [128, 512], F32) for i in range(8)]
            wq = self.sb("wq", [128, NCH, D], BF16)
            wo = self.sb("wo", [128, NCH, D], BF16)
            KTp = self.sb("KTp", [128, NCH, MEM], BF16)
            Vp = self.sb("Vp", [128, 2, D], BF16)
            hbk = self.sb("hbk", [128, NCH, 512], BF16)
            qT = self.sb("qT", [128, NCH, 512], BF16)
            oT = self.sb("oT", [128, NCH, 512], BF16)
            PT = [self.sb(f"PT{i}", [128, 2, 512], BF16) for i in range(2)]
            rden = self.sb("rden", [128, 512], F32)
            wkvb = [self.sb(f"wkvb{i}", [128, NCH, 512], BF16) for i in range(2)]
            kvo = [self.sb(f"kvo{i}", [128, 512], F32) for i in range(2)]
            KTs = [self.sb(f"KTs{i}", [128, NCH, MEM], BF16) for i in range(2)]
            Vs = [self.sb(f"Vs{i}", [128, 2, D], BF16) for i in range(2)]
            PTs = self.sb("PTs", [128, 8, 8], BF16)
            rdens = self.sb("rdens", [128, 4, 8], F32)
            psb = self.psb

            wkv_v = self.wkv_d[l].rearrange("(c p) e -> p c e", p=128)
            nk = 0
            for j in range(4):
                b = j % 2
                R.dma("pool", wkvb[b][:], wkv_v[:, :, j * 512:(j + 1) * 512], w=[("wkvb", b)])
                for mt in range(2):
                    ps = psb[mt]

                    def mm(e, ps=ps, b=b, mt=mt):
                        ins = None
                        for c in range(NCH):
                            ins = e.matmul(ps[:, :], self.memT[:, c, mt * 128:(mt + 1) * 128], wkvb[b][:, c, :],
                                           start=(c == 0), stop=(c == NCH - 1))
                        return ins
                    R.op("pe", mm, r=["memT", ("wkvb", b)], w=[("ps", mt)])
                    kb = nk % 2
                    nk += 1
                    R.op("dve", lambda e, ps=ps, kb=kb: e.tensor_copy(out=kvo[kb][:], in_=ps[:]),
                         r=[("ps", mt)], w=[("kvo", kb)])
                    dst = self.memk_d if j < 2 else self.memv_d
                    R.dma("sp", dst[l, mt * 128:(mt + 1) * 128, (j % 2) * 512:(j % 2 + 1) * 512], kvo[kb][:],
                          r=[("kvo", kb)])
                    if j >= 2:
                        R.op("dve", lambda e, ps=ps, mt=mt, j=j: e.tensor_copy(
                            out=Vp[:, mt, (j - 2) * 512:(j - 1) * 512], in_=ps[:]),
                            r=[("ps", mt)], w=["Vp"])
                if j < 2:
                    for ec in range(4):
                        pi = 2 + ec % 2
                        ps = psb[pi]

                        def mm(e, ps=ps, b=b, ec=ec):
                            ins = None
                            for c in range(NCH):
                                ins = e.matmul(ps[:, :MEM], wkvb[b][:, c, ec * 128:(ec + 1) * 128], self.memT[:, c, :],
                                               start=(c == 0), stop=(c == NCH - 1))
                            return ins
                        R.op("pe", mm, r=["memT", ("wkvb", b)], w=[("ps", pi)])
                        R.op("dve", lambda e, ps=ps, j=j, ec=ec: e.tensor_copy(out=KTp[:, j * 4 + ec, :], in_=ps[:, :MEM]),
                             r=[("ps", pi)], w=["KTp"])

            for hf in range(2):
                R.dma("pool", wq[:, :, hf * 512:(hf + 1) * 512],
                      self.wq_d[l].rearrange("(c p) e -> p c e", p=128)[:, :, hf * 512:(hf + 1) * 512], w=["wq"])
                R.dma("pool", wo[:, :, hf * 512:(hf + 1) * 512],
                      self.wo_d[l].rearrange("(c p) e -> p c e", p=128)[:, :, hf * 512:(hf + 1) * 512], w=["wo"])

            npt = 0
            lvl = DBG.get("lvl", 9)
            for blk, (t0, n) in enumerate(BLOCKS if lvl >= 2 else []):
                self.norm_block(blk, ("norm_g", l, 1), hbk, ["hbk"], out_off=0)
                for ec in range(NCH):
                    pi = ec % 2
                    ps = psb[pi]

                    def mm(e, ps=ps, ec=ec, n=n):
                        ins = None
                        for c in range(NCH):
                            ins = e.matmul(ps[:, :n], wq[:, c, ec * 128:(ec + 1) * 128], hbk[:, c, :n],
                                           start=(c == 0), stop=(c == NCH - 1))
                        return ins
                    R.op("pe", mm, r=["wq", "hbk"], w=[("ps", pi)])
                    R.op("dve", lambda e, ps=ps, ec=ec, n=n: e.tensor_copy(out=qT[:, ec, :n], in_=ps[:, :n]),
                         r=[("ps", pi)], w=[("qT", ec)])
                if lvl < 3:
                    continue
                if blk < 4:
                    for h in range(4):
                        pb = npt % 2
                        npt += 1
                        for mt in range(2):
                            pi = 2 + mt
                            ps = psb[pi]

                            def mm(e, ps=ps, h=h, mt=mt, n=n):
                                ins = None
                                for dc in range(2):
                                    ins = e.matmul(ps[:, :n], KTp[:, h * 2 + dc, mt * 128:(mt + 1) * 128],
                                                   qT[:, h * 2 + dc, :n], start=(dc == 0), stop=(dc == 1))
                                return ins
                            R.op("pe", mm, r=["KTp", ("qT", h * 2), ("qT", h * 2 + 1)], w=[("ps", pi)])
                            R.op("act", lambda e, ps=ps, pb=pb, mt=mt, n=n: e.activation(
                                out=PT[pb][:, mt, :n], in_=ps[:, :n], func=AF.Exp, scale=1.0 / 16.0),
                                r=[("ps", pi)], w=[("PT", pb, mt)])
                        ps4 = psb[4]

                        def mmd(e, pb=pb, n=n):
                            ins = None
                            for mt in range(2):
                                ins = e.matmul(ps4[:, :n], self.ones_1[:], PT[pb][:, mt, :n], start=(mt == 0), stop=(mt == 1))
                            return ins
                        R.op("pe", mmd, r=["ones_1", ("PT", pb, 0), ("PT", pb, 1)], w=[("ps", 4)])
                        R.op("dve", lambda e, n=n: e.reciprocal(out=rden[:, :n], in_=ps4[:, :n]), r=[("ps", 4)], w=["rden"])
                        for dc in range(2):
                            pi = 5 + dc
                            ps = psb[pi]

                            def mmv(e, ps=ps, h=h, dc=dc, pb=pb, n=n):
                                ins = None
                                for mt in range(2):
                                    ins = e.matmul(ps[:, :n], Vp[:, mt, h * 256 + dc * 128:h * 256 + (dc + 1) * 128],
                                                   PT[pb][:, mt, :n], start=(mt == 0), stop=(mt == 1))
                                return ins
                            R.op("pe", mmv, r=["Vp", ("PT", pb, 0), ("PT", pb, 1)], w=[("ps", pi)])
                            R.op("dve", lambda e, ps=ps, h=h, dc=dc, n=n: e.tensor_tensor(
                                out=oT[:, h * 2 + dc, :n], in0=ps[:, :n], in1=rden[:, :n], op=ALU.mult),
                                r=[("ps", pi), "rden"], w=[("oT", h * 2 + dc)])
                else:
                    for sb_ in range(DBG.get("nseq", NSEQ)):
                        kb = sb_ % 2
                        R.dma("pool", KTs[kb][:], self.cKT_d[l, sb_].rearrange("h (dc p) m -> p (h dc) m", p=128),
                              w=[("KTs", kb)])
                        R.dma("pool", Vs[kb][:], self.cV_d[l, sb_].rearrange("(mt p) e -> p mt e", p=128),
                              w=[("Vs", kb)])
                        c0 = sb_ * 8
                        ps = psb[2 + sb_ % 2]
                        pi = 2 + sb_ % 2

                        def mms(e, ps=ps, kb=kb, c0=c0):
                            ins = None
                            for h in range(4):
                                for mt in range(2):
                                    for dc in range(2):
                                        ins = e.matmul(ps[:, (h * 2 + mt) * 8:(h * 2 + mt + 1) * 8],
                                                       KTs[kb][:, h * 2 + dc, mt * 128:(mt + 1) * 128],
                                                       qT[:, h * 2 + dc, c0:c0 + 8], start=(dc == 0), stop=(dc == 1))
                            return ins
                        R.op("pe", mms, r=[("KTs", kb)] + [("qT", c) for c in range(NCH)], w=[("ps", pi)])
                        R.op("act", lambda e, ps=ps: e.activation(out=PTs[:].rearrange("p a b -> p (a b)"), in_=ps[:, :64],
                                                                   func=AF.Exp, scale=1.0 / 16.0),
                             r=[("ps", pi)], w=["PTs"])
                        ps4 = psb[4]

                        def mmd(e):
                            ins = None
                            for h in range(4):
                                for mt in range(2):
                                    ins = e.matmul(ps4[:, h * 8:(h + 1) * 8], self.ones_1[:], PTs[:, h * 2 + mt, :],
                                                   start=(mt == 0), stop=(mt == 1))
                            return ins
                        R.op("pe", mmd, r=["ones_1", "PTs"], w=[("ps", 4)])
                        R.op("dve", lambda e: e.reciprocal(out=rdens[:].rearrange("p a b -> p (a b)"), in_=ps4[:, :32]),
                             r=[("ps", 4)], w=["rdens"])
                        pi2 = 5 + sb_ % 2
                        psv = psb[pi2]

                        def mmv(e, psv=psv, kb=kb):
                            ins = None
                            for h in range(4):
                                for dc in range(2):
                                    for mt in range(2):
                                        ins = e.matmul(psv[:, (h * 2 + dc) * 8:(h * 2 + dc + 1) * 8],
                                                       Vs[kb][:, mt, h * 256 + dc * 128:h * 256 + (dc + 1) * 128],
                                                       PTs[:, h * 2 + mt, :], start=(mt == 0), stop=(mt == 1))
                            return ins
                        R.op("pe", mmv, r=[("Vs", kb), "PTs"], w=[("ps", pi2)])

                        def nrm(e, psv=psv, c0=c0):
                            ins = None
                            for h in range(4):
                                for dc in range(2):
                                    ins = e.tensor_tensor(out=oT[:, h * 2 + dc, c0:c0 + 8],
                                                          in0=psv[:, (h * 2 + dc) * 8:(h * 2 + dc + 1) * 8],
                                                          in1=rdens[:, h, :], op=ALU.mult)
                            return ins
                        R.op("dve", nrm, r=[("ps", pi2), "rdens"], w=[("oT", c) for c in range(NCH)])
                for ec in range(NCH):
                    pi = ec % 2
                    ps = psb[pi]

                    def mm(e, ps=ps, ec=ec, n=n):
                        ins = None
                        for c in range(NCH):
                            ins = e.matmul(ps[:, :n], wo[:, c, ec * 128:(ec + 1) * 128], oT[:, c, :n],
                                           start=(c == 0), stop=(c == NCH - 1))
                        return ins
                    R.op("pe", mm, r=["wo"] + [("oT", c) for c in range(NCH)], w=[("ps", pi)])
                    R.op("dve", lambda e, ps=ps, ec=ec, t0=t0, n=n: e.tensor_tensor(
                        out=self.x[:, ec, t0:t0 + n], in0=self.x[:, ec, t0:t0 + n], in1=ps[:, :n], op=ALU.add),
                        r=[("ps", pi)], w=[("x", ec, blk)])
            R.barrier()
            self.stack = old

    def phase_mlp(self, l):
        R = self.R
        with contextlib.ExitStack() as st:
            old, self.stack = self.stack, st
            self.sq_scr = self.sb("sq_scr", [128, NCH, 512], BF16)
            self.rstd = self.sb("rstd", [128, 512], F32)
            self.psb = [self.ps(f"psb{i}", [128, 512], F32) for i in range(8)]
            self.hb = self.sb("hb", [128, NCH, T], BF16)
            wu = [self.sb(f"wu{i}", [128, NCH, 512], BF16) for i in range(2)]
            wd = [self.sb(f"wd{i}", [128, 4, D], BF16) for i in range(2)]
            hT = [self.sb(f"hT{i}", [128, 4, 512], BF16) for i in range(2)]
            rl = [self.sb(f"rl{i}", [128, 512], F32) for i in range(2)]
            wu_v = self.w_up_d[l].rearrange("(c p) f -> p c f", p=128)
            wd_v = self.w_dn_d[l].rearrange("(fc p) e -> p fc e", p=128)

            for blk in range(len(BLOCKS)):
                self.norm_block(blk, ("norm_g", l, 2), self.hb, [("hb", blk)])

            nrl = 0
            for j in range(8):
                b = j % 2
                R.dma("pool", wu[b][:], wu_v[:, :, j * 512:(j + 1) * 512], w=[("wu", b)])
                R.dma("pool", wd[b][:], wd_v[:, j * 4:(j + 1) * 4, :], w=[("wd", b)])
                for blk, (t0, n) in enumerate(BLOCKS):
                    hb_ = (j * len(BLOCKS) + blk) % 2
                    for fc in range(4):
                        ps = self.psb[fc]

                        def mm(e, ps=ps, fc=fc, t0=t0, n=n, b=b):
                            ins = None
                            for c in range(NCH):
                                ins = e.matmul(ps[:, :n], wu[b][:, c, fc * 128:(fc + 1) * 128],
                                               self.hb[:, c, t0:t0 + n], start=(c == 0), stop=(c == NCH - 1))
                            return ins
                        R.op("pe", mm, r=[("wu", b), ("hb", blk)], w=[("ps", fc)])
                        rb = nrl % 2
                        nrl += 1
                        R.op("act", lambda e, ps=ps, rb=rb, n=n: e.activation(out=rl[rb][:, :n], in_=ps[:, :n],
                                                                               func=AF.Relu),
                             r=[("ps", fc)], w=[("rl", rb)])
                        R.op("pool", lambda e, rb=rb, fc=fc, n=n, hb_=hb_: e.tensor_tensor(
                            out=hT[hb_][:, fc, :n], in0=rl[rb][:, :n], in1=rl[rb][:, :n], op=ALU.mult),
                            r=[("rl", rb)], w=[("hT", hb_, fc)])
                    for oc in range(NCH):
                        pi = 4 + oc % 4
                        ps = self.psb[pi]

                        def mm2(e, ps=ps, oc=oc, n=n, b=b, hb_=hb_):
                            ins = None
                            for fc in range(4):
                                ins = e.matmul(ps[:, :n], wd[b][:, fc, oc * 128:(oc + 1) * 128],
                                               hT[hb_][:, fc, :n], start=(fc == 0), stop=(fc == 3))
                            return ins
                        R.op("pe", mm2, r=[("wd", b)] + [("hT", hb_, fc) for fc in range(4)], w=[("ps", pi)])
                        R.op("dve", lambda e, ps=ps, oc=oc, t0=t0, n=n: e.tensor_tensor(
                            out=self.x[:, oc, t0:t0 + n], in0=self.x[:, oc, t0:t0 + n], in1=ps[:, :n], op=ALU.add),
                            r=[("ps", pi)], w=[("x", oc, blk)])
            R.barrier()
            self.stack = old

    def final_norm(self):
        R = self.R
        with contextlib.ExitStack() as st:
            old, self.stack = self.stack, st
            self.sq_scr = self.sb("sq_scr", [128, NCH, 512], BF16)
            self.rstd = self.sb("rstd", [128, 512], F32)
            self.psb = [self.ps(f"psb{i}", [128, 512], F32) for i in range(8)]
            yb = [self.sb(f"yb{i}", [128, NCH, 512], F32) for i in range(2)]
            yv = self.yT_d.rearrange("(c p) t -> p c t", p=128)
            for blk, (t0, n) in enumerate(BLOCKS):
                b = blk % 2
                self.norm_block(blk, ("final_g",), yb[b], [("yb", b)], out_off=0)
                R.dma("sp", yv[:, :, t0:t0 + n], yb[b][:, :, :n], r=[("yb", b)])
            R.barrier()
            self.stack = old


_CACHE = {}
PHASES = ("conv", "rwkv", "attn", "mlp")
NLAYERS = DEPTH
DBG = {}


def get_nc(phases):
    key = tuple(phases)
    if key not in _CACHE:
        b = Builder(phases)
        nc = b.build()
        _CACHE[key] = (nc, set(b.dins), set(b.douts))
    return _CACHE[key]


def make_rconst():
    i = np.arange(64)
    su = (i[:, None] < i[None, :]).astype(np.float32)
    sl_ = (i[:, None] > i[None, :]).astype(np.float32)
    u = (i[:, None] <= i[None, :]).astype(np.float32)
    ey = np.eye(64, dtype=np.float32)
    cm = np.stack([np.broadcast_to(m[:, None, :], (64, 8, 64)) for m in (su, sl_, u, ey)], axis=1)
    out = np.zeros((128, 2048 + 256), np.float32)
    out[:64, :2048] = cm.reshape(64, 2048)
    out[64:, :2048] = cm.reshape(64, 2048)
    out[:, 2048:2176] = np.eye(128, dtype=np.float32)
    out[:64, 2176:2240] = 1.0
    out[64:, 2240:2304] = 1.0
    return out


def extra_inputs(inp, c, sl):
    f32 = np.float32
    return {
        "rconst": lambda: make_rconst(),
        "sshiftT": lambda: np.ascontiguousarray(np.asarray(inp["state_shift"][:, sl], f32).transpose(0, 2, 1)),
        "swkvT": lambda: np.ascontiguousarray(np.asarray(inp["state_wkv"][:, sl], f32).transpose(0, 1, 2, 4, 3)),
        **{nm: (lambda nm=nm: np.asarray(inp[nm], f32)) for nm in
           ("rw_w_r", "rw_w_k", "rw_w_v", "rw_w_o", "rw_w1", "rw_w2", "rw_a1", "rw_a2", "rw_g1", "rw_g2", "rw_v1", "rw_v2")},
        "cv_w_in": lambda: np.asarray(inp["cv_w_in"], f32),
        "cv_w_out": lambda: np.asarray(inp["cv_w_out"], f32),
        "sconvT": lambda: np.ascontiguousarray(np.asarray(inp["state_conv"][:, sl], f32).transpose(0, 3, 1, 2)),
    }


def make_in_maps(inp, used, n_cores=8):
    vecs = pack_vecs(inp)
    maps = []
    for c in range(n_cores):
        xp = np.asarray(inp["x_prompt"][c], np.float32)
        xs = np.asarray(inp["x_sample"][c * NSEQ:(c + 1) * NSEQ], np.float32).reshape(TS, D)
        xT = np.ascontiguousarray(np.concatenate([xp, xs], axis=0).T)
        sl = slice(c * NSEQ, (c + 1) * NSEQ)
        sl = slice(c * NSEQ, (c + 1) * NSEQ)
        f32 = np.float32
        m = {"xT": lambda: xT, "vecs": lambda: vecs,
             "mlp_w_up": lambda: np.asarray(inp["mlp_w_up"], f32),
             "mlp_w_down": lambda: np.asarray(inp["mlp_w_down"], f32),
             "memT": lambda: np.ascontiguousarray(np.asarray(inp["mem_prompt"][c], f32).T),
             "xa_w_kv": lambda: np.asarray(inp["xa_w_kv"], f32),
             "xa_w_q": lambda: np.asarray(inp["xa_w_q"], f32),
             "xa_w_o": lambda: np.asarray(inp["xa_w_o"], f32),
             "cKT": lambda: np.ascontiguousarray(np.asarray(inp["cache_mem_k"][:, sl], f32).transpose(0, 1, 3, 4, 2)),
             "cV": lambda: np.ascontiguousarray(np.asarray(inp["cache_mem_v"][:, sl], f32).reshape(DEPTH, NSEQ, MEM, D))}
        m.update(extra_inputs(inp, c, sl))
        maps.append({k: v() for k, v in m.items() if k in used})
    return maps


def kernel(**inputs):
    phases = PHASES
    nc, used, douts = get_nc(phases)
    in_maps = make_in_maps(inputs, used)
    res = run_bass_kernel_spmd(nc, in_maps, core_ids=list(range(8)))
    outs = res.results
    yT = np.stack([np.asarray(o["yT"]) for o in outs])
    y_prompt = np.ascontiguousarray(yT[:, :, :TP].transpose(0, 2, 1))
    y_sample = np.ascontiguousarray(yT[:, :, TP:].transpose(0, 2, 1)).reshape(8 * NSEQ, 8, D)
    res_extra = {}
    for k_ in douts:
        if k_.startswith("dbg_"):
            res_extra[k_] = np.stack([np.asarray(o[k_]) for o in outs])
    if "convpT" in douts:
        cp = np.stack([np.asarray(o["convpT"]) for o in outs], axis=1)
        res_extra["conv_prompt"] = np.ascontiguousarray(cp.transpose(0, 1, 3, 2))
        cs = np.stack([np.asarray(o["convsT"]) for o in outs], axis=1)
        res_extra["conv_sample"] = np.ascontiguousarray(cs.transpose(0, 1, 3, 4, 2)).reshape(2, 8 * NSEQ, 30, D)
    if "shiftpT" in douts:
        sp = np.stack([np.asarray(o["shiftpT"]) for o in outs], axis=1)
        res_extra["shift_prompt"] = np.ascontiguousarray(sp.transpose(0, 1, 3, 2)).reshape(2, 8, D)
        ss = np.stack([np.asarray(o["shiftsT"]) for o in outs], axis=1)
        res_extra["shift_sample"] = np.ascontiguousarray(ss.transpose(0, 1, 4, 3, 2)).reshape(2, 8 * NSEQ, D)
        wp = np.stack([np.asarray(o["wkvpT"]) for o in outs], axis=1)
        res_extra["wkv_prompt"] = np.ascontiguousarray(wp.transpose(0, 1, 2, 4, 3))
        ws = np.stack([np.asarray(o["wkvsT"]) for o in outs], axis=1)
        res_extra["wkv_sample"] = np.ascontiguousarray(ws.transpose(0, 1, 2, 3, 5, 4)).reshape(2, 8 * NSEQ, 16, 64, 64)
    DBG["extra"] = res_extra
    if "mem_k" not in douts:
        return y_prompt, y_sample
    mem_k = np.stack([np.asarray(o["mem_k"]) for o in outs], axis=1).reshape(DEPTH, 8, MEM, 4, 256)
    mem_v = np.stack([np.asarray(o["mem_v"]) for o in outs], axis=1).reshape(DEPTH, 8, MEM, 4, 256)
    if DBG.get("short"):
        return y_prompt, y_sample, mem_k, mem_v
    f32 = np.float32
    ex = res_extra
    conv_p = ex.get("conv_prompt", np.zeros((2, 8, 30, D), f32))
    conv_s = ex.get("conv_sample", np.zeros((2, 8 * NSEQ, 30, D), f32))
    shift_p = ex.get("shift_prompt", np.zeros((2, 8, D), f32))
    shift_s = ex.get("shift_sample", np.zeros((2, 8 * NSEQ, D), f32))
    wkv_p = ex.get("wkv_prompt", np.zeros((2, 8, 16, 64, 64), f32))
    wkv_s = ex.get("wkv_sample", np.zeros((2, 8 * NSEQ, 16, 64, 64), f32))
    return (y_prompt, y_sample, mem_k, mem_v, conv_p.astype(f32), shift_p.astype(f32), wkv_p.astype(f32),
            conv_s.astype(f32), shift_s.astype(f32), wkv_s.astype(f32))
```

```python
import contextlib
import numpy as np
import concourse.bass as bass
import concourse.mybir as mybir
from concourse.bass_utils import run_bass_kernel_spmd

F32 = mybir.dt.float32
BF16 = mybir.dt.bfloat16
ALU = mybir.AluOpType
AF = mybir.ActivationFunctionType
AX = mybir.AxisListType

D = 1024
NCH = 8
TP = 2048
NSEQ = 16
TS = 128
T = TP + TS
DEPTH = 4
MEM = 256
DFF = 4096
EPOCH = 16000
NDMASEM = 24

BLOCKS = [(0, 512), (512, 512), (1024, 512), (1536, 512), (2048, 128)]


class Rec:
    ENGS = ("pe", "act", "dve", "pool", "sp")

    def __init__(self, nc, stack):
        self.nc = nc
        self.stack = stack
        self.lists = {e: [] for e in self.ENGS}
        self.count = {e: 0 for e in self.ENGS}
        self.sems = {}
        self.seen = {e: {} for e in self.ENGS}
        self.last_w = {}
        self.readers = {}
        self.dma_slot = 0
        self.dma_val = [0] * NDMASEM
        for i in range(NDMASEM):
            self.sems[("dma", i)] = stack.enter_context(nc.semaphore(f"dma{i}"))
        self.nwait = 0

    def _sem(self, key):
        if key not in self.sems:
            self.sems[key] = self.stack.enter_context(self.nc.semaphore(f"s_{key[0]}_{key[1]}"))
        return self.sems[key]

    def _wait(self, eng, ev):
        key, val = ev
        if self.seen[eng].get(key, 0) >= val:
            return
        self.seen[eng][key] = val
        sem = self._sem(key)
        self.lists[eng].append(lambda e, sem=sem, val=val: e.wait_ge(sem, val))
        self.nwait += 1

    def _deps(self, eng, r, w):
        evs = []
        for res in r:
            ev = self.last_w.get(res)
            if ev is not None:
                evs.append(ev)
        for res in w:
            ev = self.last_w.get(res)
            if ev is not None:
                evs.append(ev)
            evs.extend(self.readers.get(res, ()))
        for ev in evs:
            self._wait(eng, ev)

    def _commit(self, ev, r, w):
        for res in r:
            self.readers.setdefault(res, []).append(ev)
        for res in w:
            self.last_w[res] = ev
            self.readers[res] = []

    def op(self, eng, fn, r=(), w=()):
        self._deps(eng, r, w)
        n = self.count[eng]
        key = (eng, n // EPOCH)
        val = n % EPOCH + 1
        sem = self._sem(key)
        self.count[eng] = n + 1
        self.lists[eng].append(lambda e, fn=fn, sem=sem: fn(e).then_inc(sem, 1))
        self._commit((key, val), r, w)

    def dma(self, eng, out, in_, r=(), w=(), **kw):
        self._deps(eng, r, w)
        s = self.dma_slot
        self.dma_slot = (s + 1) % NDMASEM
        key = ("dma", s)
        if self.dma_val[s] > 0:
            self._wait(eng, (key, self.dma_val[s]))
        self.dma_val[s] += 16
        val = self.dma_val[s]
        sem = self.sems[key]
        self.lists[eng].append(
            lambda e, out=out, in_=in_, sem=sem, kw=kw: e.dma_start(out=out, in_=in_, **kw).then_inc(sem, 16))
        self._commit((key, val), r, w)

    def barrier(self):
        evs = []
        for e in self.ENGS:
            n = self.count[e]
            if n > 0:
                evs.append(((e, (n - 1) // EPOCH), (n - 1) % EPOCH + 1))
        for s in range(NDMASEM):
            if self.dma_val[s] > 0:
                evs.append((("dma", s), self.dma_val[s]))
        for e in self.ENGS:
            for ev in evs:
                if ev[0][0] == e:
                    continue
                self._wait(e, ev)

    def finish(self):
        for s in range(NDMASEM):
            if self.dma_val[s] > 0:
                self._wait("sp", (("dma", s), self.dma_val[s]))
        for e in self.ENGS:
            n = self.count[e]
            if n > 0 and e != "sp":
                self._wait("sp", ((e, (n - 1) // EPOCH), (n - 1) % EPOCH + 1))
        nc = self.nc
        lists = self.lists
        with nc.Block() as block:
            @block.tensor
            def _(e):
                for f in lists["pe"]:
                    f(e)

            @block.scalar
            def _(e):
                for f in lists["act"]:
                    f(e)

            @block.vector
            def _(e):
                for f in lists["dve"]:
                    f(e)

            @block.gpsimd
            def _(e):
                for f in lists["pool"]:
                    f(e)

            @block.sync
            def _(e):
                for f in lists["sp"]:
                    f(e)


def vec_layout():
    names = []
    for l in range(DEPTH):
        for j in range(3):
            names.append(("norm_g", l, j))
    names.append(("final_g",))
    for ci in range(2):
        names += [("cv_b_in", ci, 0), ("cv_b_in", ci, 1)]
        for j in range(31):
            names.append(("cv_w_dw", ci, j))
        names += [("cv_b_dw", ci), ("cv_ln_g", ci), ("cv_ln_b", ci), ("cv_b_out", ci)]
    for ri in range(2):
        for j in range(6):
            names.append(("rw_mix", ri, j))
        names += [("rw_w0", ri), ("rw_a0", ri), ("rw_k_k", ri), ("rw_k_a", ri), ("rw_r_k", ri),
                  ("rw_lnx_g", ri), ("rw_lnx_b", ri)]
    names.append(("rw_v0", 0))
    return {n: i for i, n in enumerate(names)}


VIDX = vec_layout()
NVEC = len(VIDX)


def pack_vecs(inp):
    out = np.zeros((128, NVEC, 8), np.float32)

    def put(key, v):
        out[:, VIDX[key], :] = np.asarray(v, np.float32).reshape(8, 128).T

    for l in range(DEPTH):
        for j in range(3):
            put(("norm_g", l, j), inp["norm_g"][l, j])
    put(("final_g",), inp["final_g"])
    for ci in range(2):
        put(("cv_b_in", ci, 0), inp["cv_b_in"][ci, :D])
        put(("cv_b_in", ci, 1), inp["cv_b_in"][ci, D:])
        for j in range(31):
            put(("cv_w_dw", ci, j), inp["cv_w_dw"][ci, j])
        put(("cv_b_dw", ci), inp["cv_b_dw"][ci])
        put(("cv_ln_g", ci), inp["cv_ln_g"][ci])
        put(("cv_ln_b", ci), inp["cv_ln_b"][ci])
        put(("cv_b_out", ci), inp["cv_b_out"][ci])
    for ri in range(2):
        for j in range(6):
            put(("rw_mix", ri, j), inp["rw_mix"][ri, j])
        put(("rw_w0", ri), inp["rw_w0"][ri])
        put(("rw_a0", ri), inp["rw_a0"][ri])
        put(("rw_k_k", ri), inp["rw_k_k"][ri])
        put(("rw_k_a", ri), inp["rw_k_a"][ri])
        put(("rw_r_k", ri), inp["rw_r_k"][ri].reshape(-1))
        put(("rw_lnx_g", ri), inp["rw_lnx_g"][ri])
        put(("rw_lnx_b", ri), inp["rw_lnx_b"][ri])
    put(("rw_v0", 0), inp["rw_v0"][0])
    return out.reshape(128, NVEC * 8)


class Builder:
    def __init__(self, phases=("mlp",)):
        self.phases = phases
        self.nc = bass.Bass("TRN2", target_bir_lowering=False)
        self.stack = contextlib.ExitStack()
        self.R = Rec(self.nc, self.stack)
        self.dins = {}
        self.douts = {}

    def dram_in(self, name, shape):
        if name not in self.dins:
            self.dins[name] = self.nc.dram_tensor(name, list(shape), F32, kind="ExternalInput").ap()
        return self.dins[name]

    def dram_out(self, name, shape):
        if name not in self.douts:
            self.douts[name] = self.nc.dram_tensor(name, list(shape), F32, kind="ExternalOutput").ap()
        return self.douts[name]

    def sb(self, name, shape, dtype):
        self.uid = getattr(self, "uid", 0) + 1
        return self.stack.enter_context(self.nc.sbuf_tensor(f"{name}_{self.uid}", list(shape), dtype))

    def ps(self, name, shape, dtype=F32):
        self.uid = getattr(self, "uid", 0) + 1
        return self.stack.enter_context(self.nc.psum_tensor(f"{name}_{self.uid}", list(shape), dtype))

    w_up_d = property(lambda self: self.dram_in("mlp_w_up", [DEPTH, D, DFF]))
    w_dn_d = property(lambda self: self.dram_in("mlp_w_down", [DEPTH, DFF, D]))
    memT_d = property(lambda self: self.dram_in("memT", [D, MEM]))
    wkv_d = property(lambda self: self.dram_in("xa_w_kv", [DEPTH, D, 2 * D]))
    wq_d = property(lambda self: self.dram_in("xa_w_q", [DEPTH, D, D]))
    wo_d = property(lambda self: self.dram_in("xa_w_o", [DEPTH, D, D]))
    cKT_d = property(lambda self: self.dram_in("cKT", [DEPTH, NSEQ, 4, 256, MEM]))
    cV_d = property(lambda self: self.dram_in("cV", [DEPTH, NSEQ, MEM, D]))
    memk_d = property(lambda self: self.dram_out("mem_k", [DEPTH, MEM, D]))
    memv_d = property(lambda self: self.dram_out("mem_v", [DEPTH, MEM, D]))

    def vec(self, key):
        i = VIDX[key]
        return self.vecs[:, i * 8:(i + 1) * 8]

    def build(self):
        nc, R = self.nc, self.R
        self.xT_d = self.dram_in("xT", [D, T])
        self.vecs_d = self.dram_in("vecs", [128, NVEC * 8])
        self.yT_d = self.dram_out("yT", [D, T])

        self.x = self.sb("x", [128, NCH, T], F32)
        self.vecs = self.sb("vecs_sb", [128, NVEC * 8], F32)
        self.ones_m = self.sb("ones_m", [128, 128], BF16)
        self.psb = None
        self.ones_1 = self.sb("ones_1", [128, 128], BF16)
        R.op("pool", lambda e: e.memset(self.ones_1[:], 1.0), w=["ones_1"])
        if "attn" in self.phases:
            self.memT = self.sb("memT_sb", [128, NCH, MEM], BF16)
            R.dma("pool", self.memT[:], self.memT_d.rearrange("(c p) m -> p c m", p=128), w=["memT"])

        R.op("pool", lambda e: e.memset(self.ones_m[:], 1.0 / D), w=["ones_m"])
        self.eps_rms = self.sb("eps_rms", [128, 1], F32)
        R.op("pool", lambda e: e.memset(self.eps_rms[:], 1e-6), w=["consts"])
        self.eps_lnx = self.sb("eps_lnx", [128, 1], F32)
        R.op("pool", lambda e: e.memset(self.eps_lnx[:], 64e-5), w=["consts"])
        if "rwkv" in self.phases:
            cd = self.dram_in("rconst", [128, 4 * 8 * 64 + 128 + 128])
            self.cmask = self.sb("cmask", [128, 4, 8, 64], BF16)
            self.identf = self.sb("identf", [128, 128], F32)
            self.identb = self.sb("identb", [128, 128], BF16)
            self.blk64 = self.sb("blk64", [128, 128], BF16)
            R.dma("pool", self.cmask[:].rearrange("p a b c -> p (a b c)"), cd[:, 0:2048], w=["consts"])
            R.dma("sp", self.identf[:], cd[:, 2048:2176], w=["consts"])
            R.dma("pool", self.identb[:], cd[:, 2048:2176], w=["consts"])
            R.dma("pool", self.blk64[:], cd[:, 2176:2304], w=["consts"])
            self.xspill = self.nc.dram_tensor("xspill", [D, T], F32, kind="Internal").ap()
            self.vfirst_d = self.nc.dram_tensor("vfirst", [D, T], F32, kind="Internal").ap()
        self.eps_ln = self.sb("eps_ln", [128, 1], F32)
        R.op("pool", lambda e: e.memset(self.eps_ln[:], 1e-5), w=["consts"])
        R.dma("sp", self.vecs[:], self.vecs_d[:], w=["vecs"])
        xv = self.xT_d.rearrange("(c p) t -> p c t", p=128)
        for c in range(NCH):
            R.dma("sp", self.x[:, c, :], xv[:, c, :], w=[("x", c, b) for b in range(len(BLOCKS))])

        with contextlib.ExitStack() as st:
            old, self.stack = self.stack, st
            for l in range(NLAYERS):
                if "conv" in self.phases and l % 2 == 0:
                    self.phase_conv(l)
                if "rwkv" in self.phases and l % 2 == 1:
                    self.phase_rwkv(l)
                if "attn" in self.phases:
                    self.phase_attn(l)
                if "mlp" in self.phases:
                    self.phase_mlp(l)
            self.final_norm()
            R.barrier()
            self.stack = old
        R.finish()
        return nc

    def rms_rstd(self, blk, rstd, tagps=7):
        R = self.R
        t0, n = BLOCKS[blk]
        sq = self.sq_scr
        ps = self.psb[tagps]
        R.op("act", lambda e: e.activation(out=sq[:, :, :n], in_=self.x[:, :, t0:t0 + n], func=AF.Square),
             r=[("x", c, blk) for c in range(NCH)], w=["sq_scr"])

        def mm(e):
            ins = None
            for c in range(NCH):
                ins = e.matmul(ps[:, :n], self.ones_m[:], sq[:, c, :n], start=(c == 0), stop=(c == NCH - 1))
            return ins
        R.op("pe", mm, r=["sq_scr", "ones_m"], w=[("ps", tagps)])
        R.op("act", lambda e: e.activation(out=rstd[:, :n], in_=ps[:, :n], func=AF.Sqrt, bias=self.eps_rms[:, 0:1]),
             r=[("ps", tagps), "consts"], w=["rstd"])
        R.op("dve", lambda e: e.reciprocal(out=rstd[:, :n], in_=rstd[:, :n]), r=["rstd"], w=["rstd"])

    def norm_block(self, blk, gkey, out_tile, out_res, out_off=None):
        R = self.R
        t0, n = BLOCKS[blk]
        off = t0 if out_off is None else out_off
        self.rms_rstd(blk, self.rstd)
        g = self.vec(gkey)

        def f(e):
            ins = None
            for c in range(NCH):
                ins = e.scalar_tensor_tensor(out=out_tile[:, c, off:off + n], in0=self.x[:, c, t0:t0 + n],
                                             scalar=g[:, c:c + 1], in1=self.rstd[:, :n],
                                             op0=ALU.mult, op1=ALU.mult)
            return ins
        R.op("dve", f, r=[("x", c, blk) for c in range(NCH)] + ["rstd", "vecs"], w=out_res)


    def phase_conv(self, l):
        R = self.R
        ci = l // 2
        with contextlib.ExitStack() as st:
            old, self.stack = self.stack, st
            self.sq_scr = self.sb("sq_scr", [128, NCH, 512], BF16)
            self.rstd = self.sb("rstd", [128, 512], F32)
            self.psb = [self.ps(f"psb{i}", [128, 512], F32) for i in range(8)]
            win = self.sb("win", [128, NCH, 2 * D], BF16)
            wout = self.sb("wout", [128, NCH, D], BF16)
            hbk = self.sb("hbk", [128, NCH, 512], BF16)
            up = self.sb("up", [128, NCH, NSEQ * 38], F32)
            ups = up[:].rearrange("p c (b t) -> p c b t", t=38)
            gate = [self.sb(f"gate{i}", [128, 512], F32) for i in range(2)]
            z = self.sb("z", [128, NCH, 512], F32)
            zb = self.sb("zb", [128, NCH, 512], BF16)
            mean = self.sb("mean", [128, 512], F32)
            var = self.sb("var", [128, 512], F32)
            mr = self.sb("mr", [128, 512], F32)
            tmp = [self.sb(f"tmpc{i}", [128, 512], F32) for i in range(2)]
            psb = self.psb
            win_v = self.dram_in("cv_w_in", [2, D, 2 * D])[ci].rearrange("(c p) e -> p c e", p=128)
            wout_v = self.dram_in("cv_w_out", [2, D, D])[ci].rearrange("(c p) e -> p c e", p=128)
            for j in range(4):
                R.dma("pool", win[:, :, j * 512:(j + 1) * 512], win_v[:, :, j * 512:(j + 1) * 512], w=["win"])
            for j in range(2):
                R.dma("pool", wout[:, :, j * 512:(j + 1) * 512], wout_v[:, :, j * 512:(j + 1) * 512], w=["wout"])
            sconv_d = self.dram_in("sconvT", [2, D, NSEQ, 30])
            convp_d = self.dram_out("convpT", [2, D, 30])
            convs_d = self.dram_out("convsT", [2, D, NSEQ, 30])
            R.op("pool", lambda e: e.memset(up[:, :, 0:30], 0.0), w=["up"])
            b1 = self.vec(("cv_b_in", ci, 0))
            b2 = self.vec(("cv_b_in", ci, 1))
            bdw = self.vec(("cv_b_dw", ci))
            lng = self.vec(("cv_ln_g", ci))
            lnb = self.vec(("cv_ln_b", ci))
            bout = self.vec(("cv_b_out", ci))
            wdw = [self.vec(("cv_w_dw", ci, j)) for j in range(31)]
            ng = 0
            for blk, (t0, n) in enumerate(BLOCKS):
                samp = blk == 4
                if samp:
                    for c in range(NCH):
                        R.dma("sp", ups[:, c, :, 0:30], sconv_d[ci, c * 128:(c + 1) * 128], r=["up"], w=["up"])
                self.norm_block(blk, ("norm_g", l, 0), hbk, ["hbk"], out_off=0)
                if 0 < blk < 4:
                    R.op("pool", lambda e: e.tensor_copy(out=up[:, :, 0:30], in_=up[:, :, 512:542]), r=["up"], w=["up"])
                for ec in range(NCH):
                    psA, psG = psb[(ec % 2) * 2], psb[(ec % 2) * 2 + 1]
                    pa, pg = (ec % 2) * 2, (ec % 2) * 2 + 1

                    def mm(e, psA=psA, psG=psG, ec=ec, n=n):
                        ins = None
                        for c in range(NCH):
                            ins = e.matmul(psA[:, :n], win[:, c, ec * 128:(ec + 1) * 128], hbk[:, c, :n],
                                           start=(c == 0), stop=(c == NCH - 1))
                        for c in range(NCH):
                            ins = e.matmul(psG[:, :n], win[:, c, D + ec * 128:D + (ec + 1) * 128], hbk[:, c, :n],
                                           start=(c == 0), stop=(c == NCH - 1))
                        return ins
                    R.op("pe", mm, r=["win", "hbk"], w=[("ps", pa), ("ps", pg)])
                    gb = ng % 2
                    ng += 1
                    R.op("act", lambda e, psG=psG, gb=gb, ec=ec, n=n: e.activation(
                        out=gate[gb][:, :n], in_=psG[:, :n], func=AF.Sigmoid, bias=b2[:, ec:ec + 1]),
                        r=[("ps", pg), "vecs"], w=[("gate", gb)])
                    if not samp:
                        R.op("dve", lambda e, psA=psA, gb=gb, ec=ec, n=n: e.scalar_tensor_tensor(
                            out=up[:, ec, 30:30 + n], in0=psA[:, :n], scalar=b1[:, ec:ec + 1], in1=gate[gb][:, :n],
                            op0=ALU.add, op1=ALU.mult), r=[("ps", pa), ("gate", gb), "vecs"], w=["up"])
                    else:
                        R.op("dve", lambda e, psA=psA, gb=gb, ec=ec, n=n: e.scalar_tensor_tensor(
                            out=ups[:, ec, :, 30:38], in0=psA[:, :n].rearrange("p (b t) -> p b t", t=8),
                            scalar=b1[:, ec:ec + 1], in1=gate[gb][:, :n].rearrange("p (b t) -> p b t", t=8),
                            op0=ALU.add, op1=ALU.mult), r=[("ps", pa), ("gate", gb), "vecs"], w=["up"])
                for c in range(NCH):
                    def src(j, c=c, n=n, samp=samp):
                        if samp:
                            return ups[:, c, :, j:j + 8]
                        return up[:, c, j:j + n]

                    def dst(tile, n=n, samp=samp):
                        if samp:
                            return tile[:, :n].rearrange("p (b t) -> p b t", t=8)
                        return tile[:, :n]

                    za = dst(z[:, c, :])
                    R.op("dve", lambda e, c=c, src=src, za=za: e.tensor_scalar(
                        out=za, in0=src(0), scalar1=wdw[0][:, c:c + 1], scalar2=bdw[:, c:c + 1], op0=ALU.mult, op1=ALU.add),
                        r=["up", "vecs"], w=[("z", c)])
                    for j in range(1, 31):
                        R.op("dve", lambda e, c=c, j=j, src=src, za=za: e.scalar_tensor_tensor(
                            out=za, in0=src(j), scalar=wdw[j][:, c:c + 1], in1=za, op0=ALU.mult, op1=ALU.add),
                            r=["up", "vecs", ("z", c)], w=[("z", c)])
                    R.op("pool", lambda e, c=c, n=n: e.tensor_copy(out=zb[:, c, :n], in_=z[:, c, :n]), r=[("z", c)], w=[("zb", c)])
                    R.op("act", lambda e, c=c, n=n: e.activation(out=self.sq_scr[:, c, :n], in_=z[:, c, :n], func=AF.Square),
                         r=[("z", c)], w=["sq_scr"])
                if samp and DBG.get("dump"):
                    dd = self.dram_out("dbg_z", [128, NCH, 128])
                    R.dma("sp", dd[:], z[:, :, :128], r=[("z", c) for c in range(NCH)])
                    dd2 = self.dram_out("dbg_up", [128, NCH, NSEQ * 38])
                    R.dma("sp", dd2[:], up[:], r=["up"])
                def mmst(e, n=n):
                    ins = None
                    for c in range(NCH):
                        ins = e.matmul(psb[4][:, :n], self.ones_m[:], zb[:, c, :n], start=(c == 0), stop=(c == NCH - 1))
                    for c in range(NCH):
                        ins = e.matmul(psb[5][:, :n], self.ones_m[:], self.sq_scr[:, c, :n], start=(c == 0), stop=(c == NCH - 1))
                    return ins
                R.op("pe", mmst, r=["ones_m", "sq_scr"] + [("zb", c) for c in range(NCH)], w=[("ps", 4), ("ps", 5)])
                R.op("dve", lambda e, n=n: e.tensor_copy(out=mean[:, :n], in_=psb[4][:, :n]), r=[("ps", 4)], w=["mean"])
                R.op("dve", lambda e, n=n: e.tensor_tensor(out=var[:, :n], in0=mean[:, :n], in1=mean[:, :n], op=ALU.mult),
                     r=["mean"], w=["var"])
                R.op("dve", lambda e, n=n: e.tensor_tensor(out=var[:, :n], in0=psb[5][:, :n], in1=var[:, :n], op=ALU.subtract),
                     r=[("ps", 5), "var"], w=["var"])
                R.op("act", lambda e, n=n: e.activation(out=var[:, :n], in_=var[:, :n], func=AF.Sqrt, bias=self.eps_ln[:, 0:1]),
                     r=["var", "consts"], w=["var"])
                R.op("dve", lambda e, n=n: e.reciprocal(out=var[:, :n], in_=var[:, :n]), r=["var"], w=["var"])
                R.op("dve", lambda e, n=n: e.tensor_tensor(out=mr[:, :n], in0=mean[:, :n], in1=var[:, :n], op=ALU.mult),
                     r=["mean", "var"], w=["mr"])
                for c in range(NCH):
                    tb = c % 2
                    R.op("dve", lambda e, c=c, n=n, tb=tb: e.tensor_tensor(out=tmp[tb][:, :n], in0=z[:, c, :n], in1=var[:, :n],
                                                                           op=ALU.mult), r=[("z", c), "var"], w=[("tmpc", tb)])
                    R.op("dve", lambda e, c=c, n=n, tb=tb: e.tensor_tensor(out=tmp[tb][:, :n], in0=tmp[tb][:, :n], in1=mr[:, :n],
                                                                           op=ALU.subtract), r=[("tmpc", tb), "mr"], w=[("tmpc", tb)])
                    R.op("act", lambda e, c=c, n=n, tb=tb: e.activation(out=hbk[:, c, :n], in_=tmp[tb][:, :n], func=AF.Silu,
                                                                        bias=lnb[:, c:c + 1], scale=lng[:, c:c + 1]),
                         r=[("tmpc", tb), "vecs"], w=["hbk"])
                for ec in range(NCH):
                    pi = ec % 2
                    ps = psb[pi]

                    def mm(e, ps=ps, ec=ec, n=n):
                        ins = None
                        for c in range(NCH):
                            ins = e.matmul(ps[:, :n], wout[:, c, ec * 128:(ec + 1) * 128], hbk[:, c, :n],
                                           start=(c == 0), stop=(c == NCH - 1))
                        return ins
                    R.op("pe", mm, r=["wout", "hbk"], w=[("ps", pi)])
                    R.op("dve", lambda e, ps=ps, ec=ec, t0=t0, n=n: e.scalar_tensor_tensor(
                        out=self.x[:, ec, t0:t0 + n], in0=ps[:, :n], scalar=bout[:, ec:ec + 1], in1=self.x[:, ec, t0:t0 + n],
                        op0=ALU.add, op1=ALU.add), r=[("ps", pi), "vecs"], w=[("x", ec, blk)])
                if blk == 3:
                    for c in range(NCH):
                        R.dma("sp", convp_d[ci, c * 128:(c + 1) * 128, :], up[:, c, 512:542], r=["up"])
                if samp:
                    for c in range(NCH):
                        R.dma("sp", convs_d[ci, c * 128:(c + 1) * 128], ups[:, c, :, 8:38], r=["up"])
            R.barrier()
            self.stack = old


    def phase_rwkv(self, l):
        R = self.R
        nc = self.nc
        ri = l // 2
        NB = 128
        xs_d = self.xspill
        xs_v = xs_d.rearrange("(c p) t -> p c t", p=128)
        allx = [("x", c, b) for c in range(NCH) for b in range(len(BLOCKS))]
        for c in range(NCH):
            R.dma("sp", xs_v[:, c, :], self.x[:, c, :], r=allx)
        R.barrier()
        with contextlib.ExitStack() as st:
            old, self.stack = self.stack, st
            self.psb_save = self.psb
            def slot(j):
                return self.x[:, j // 2, (j % 2) * 1024:(j % 2) * 1024 + 1024]
            def bslot(j):
                return slot(j).rearrange("p (a b) -> p a b", b=NB)
            names = ["xb", "hf", "xx", "rf", "kf", "vf", "lw", "cs1", "cs2", "eNi", "af", "kkn", "yf"]
            Fb = {nm: bslot(i) for i, nm in enumerate(names)}
            SAV = slot(13)[:, 0:512].rearrange("p (c v) -> p c v", v=64)
            ytm = slot(14)[:, 0:512].rearrange("p (c v) -> p c v", v=64)
            yc = slot(15)[:, 0:512].rearrange("p (c v) -> p c v", v=64)
            bf = lambda nm, shape: self.sb(nm, shape, BF16)
            xm = [bf(f"xm{i}", [128, NCH, NB]) for i in range(3)]
            At, Bt, Kt, Rt, Vb, BWb, KWb, gfb, sqb = [bf(nm, [128, NCH, NB]) for nm in
                                                      ("At", "Bt", "Kt", "Rt", "Vb", "BWb", "KWb", "gfb", "sqb")]
            yg = xm[0]
            tw = bf("tw", [64, NB]); ta = bf("ta", [64, NB]); tv = bf("tv", [32, NB]); tg = bf("tg", [128, 2, NB])
            Pm, Qm, Tm, MKA, MBR, MKR, AtT, VT, BWT, KWT, XV, SAb = [
                bf(nm, [128, NCH, 64]) for nm in ("Pm", "Qm", "Tm", "MKA", "MBR", "MKR", "AtT", "VT", "BWT", "KWT", "XV", "SAb")]
            Ah = bf("Ah", [128, NCH, 64])
            Sf = self.sb("Sf", [128, NCH, 64], F32)
            Sb = bf("Sb", [128, NCH, 64])
            WC = self.sb("WC", [128, NCH, 16], F32)
            hlast = self.sb("hlast", [128, NCH, 16], F32)
            st1 = self.sb("st1", [128, NCH], F32)
            st2 = self.sb("st2", [128, NCH], F32)
            rnb = self.sb("rnb", [128, NB], F32)
            rnb4 = self.sb("rnb4", [128, 4 * NB], F32)
            wr, wk, wv, wo = [bf(nm, [128, NCH, D]) for nm in ("wr", "wk", "wv", "wo")]
            w1 = bf("w1", [128, NCH, 64]); w2 = bf("w2", [64, D])
            a1 = bf("a1", [128, NCH, 64]); a2 = bf("a2", [64, D])
            g1 = bf("g1", [128, NCH, 160]); g2 = bf("g2", [128, 2, D])
            if ri > 0:
                v1 = bf("v1", [128, NCH, 32]); v2 = bf("v2", [32, D])
            ps = [self.ps(f"rps{i}", [128, 512], F32) for i in range(8)]
            cm = self.cmask
            mSU, mSL, mU, mI = (cm[:, i] for i in range(4))

            def wload(dst, name, shape, view, nsplit=1):
                src = self.dram_in(name, shape)[ri if name not in ("rw_v1", "rw_v2") else 0]
                src = src.rearrange(view, p=128) if view else src
                if nsplit == 1:
                    R.dma("pool", dst, src, w=[name])
                else:
                    for j in range(nsplit):
                        R.dma("pool", dst[:, :, j * 512:(j + 1) * 512], src[:, :, j * 512:(j + 1) * 512], w=[name])
            for t_, nm in ((wr, "rw_w_r"), (wk, "rw_w_k"), (wv, "rw_w_v"), (wo, "rw_w_o")):
                wload(t_[:], nm, [2, D, D], "(c p) e -> p c e", 2)
            wload(w1[:], "rw_w1", [2, D, 64], "(c p) e -> p c e")
            wload(w2[:], "rw_w2", [2, 64, D], None)
            wload(a1[:], "rw_a1", [2, D, 64], "(c p) e -> p c e")
            wload(a2[:], "rw_a2", [2, 64, D], None)
            wload(g1[:], "rw_g1", [2, D, 160], "(c p) e -> p c e")
            g2d = self.dram_in("rw_g2", [2, 160, D])[ri]
            R.dma("pool", g2[:, 0, :], g2d[0:128, :], w=["rw_g2"])
            R.dma("pool", g2[0:32, 1, :], g2d[128:160, :], w=["rw_g2"])
            if ri > 0:
                wload(v1[:], "rw_v1", [1, D, 32], "(c p) e -> p c e")
                wload(v2[:], "rw_v2", [1, 32, D], None)
            WALL = ["rw_w_r", "rw_w_k", "rw_w_v", "rw_w_o", "rw_w1", "rw_w2", "rw_a1", "rw_a2", "rw_g1", "rw_g2",
                    "rw_v1", "rw_v2"]
            sshift_d = self.dram_in("sshiftT", [2, D, NSEQ])
            swkv_d = self.dram_in("swkvT", [2, NSEQ, 16, 64, 64])
            shiftp_d = self.dram_out("shiftpT", [2, 128, NCH])
            shifts_d = self.dram_out("shiftsT", [2, 128, NCH, NSEQ])
            wkvp_d = self.dram_out("wkvpT", [2, 16, 64, 64])
            wkvs_d = self.dram_out("wkvsT", [2, NSEQ, 16, 64, 64])
            vfd = self.vfirst_d.rearrange("(c p) t -> p c t", p=128)

            def V(key):
                return self.vec(key)[:, :].unsqueeze(2).to_broadcast([128, NCH, NB])

            def dve(fn, r, w):
                R.op("dve", fn, r=list(r) + ["vecs", "consts"], w=w)

            def act(fn, r, w):
                R.op("act", fn, r=list(r) + ["vecs", "consts"], w=w)

            npj = [0]

            def proj(W, wname, src, srcres, cols, evac):
                for g in range(cols // 512):
                    pi = 4 + npj[0] % 4
                    npj[0] += 1
                    p_ = ps[pi]

                    def mm(e, p_=p_, g=g):
                        ins = None
                        for j in range(4):
                            ec = g * 4 + j
                            for c in range(NCH):
                                ins = e.matmul(p_[:, j * NB:(j + 1) * NB], W[:, c, ec * 128:(ec + 1) * 128], src[:, c, :],
                                               start=(c == 0), stop=(c == NCH - 1))
                        return ins
                    R.op("pe", mm, r=[wname, srcres], w=[("rps", pi)])
                    evac(p_[:, :].rearrange("p (j t) -> p j t", t=NB), pi, g)

            R.op("pool", lambda e: e.memset(Sf[:], 0.0), w=["Sf"])
            R.op("pool", lambda e: e.memset(Sb[:], 0.0), w=["Sb"])
            R.op("pool", lambda e: e.memset(hlast[:], 0.0), w=["hlast"])

            nblk = 17

            def do_block(blk):
                samp = blk == 16
                t0 = blk * NB
                C = 8 if samp else 64
                NCK = NB // C
                LV = 2 if samp else 5
                xb, hf, xx = Fb["xb"], Fb["hf"], Fb["xx"]
                rf, kf, vf, lw, cs1, cs2 = Fb["rf"], Fb["kf"], Fb["vf"], Fb["lw"], Fb["cs1"], Fb["cs2"]
                eNi, af, kkn, yf = Fb["eNi"], Fb["af"], Fb["kkn"], Fb["yf"]
                R.dma("sp", xb, xs_v[:, :, t0:t0 + NB], w=["xb"])
                act(lambda e: e.activation(out=sqb[:], in_=xb, func=AF.Square), ["xb"], ["sqb"])

                def mmn(e):
                    ins = None
                    for c in range(NCH):
                        ins = e.matmul(ps[6][:, :NB], self.ones_m[:], sqb[:, c, :], start=(c == 0), stop=(c == NCH - 1))
                    return ins
                R.op("pe", mmn, r=["sqb", "ones_m"], w=[("rps", 6)])
                act(lambda e: e.activation(out=rnb[:], in_=ps[6][:, :NB], func=AF.Sqrt, bias=self.eps_rms[:, 0:1]),
                    [("rps", 6)], ["rnb"])
                dve(lambda e: e.reciprocal(out=rnb[:], in_=rnb[:]), ["rnb"], ["rnb"])
                dve(lambda e: e.tensor_tensor(out=hf, in0=xb, in1=V(("norm_g", l, 0)), op=ALU.mult), ["xb"], ["hf"])
                dve(lambda e: e.tensor_tensor(out=hf, in0=hf, in1=rnb[:, :].unsqueeze(1).to_broadcast([128, NCH, NB]),
                                              op=ALU.mult), ["hf", "rnb"], ["hf"])
                if samp:
                    for c in range(NCH):
                        R.dma("sp", hlast[:, c, :], sshift_d[ri, c * 128:(c + 1) * 128, :], w=["hlast"])
                    h4 = hf.rearrange("p c (b t) -> p c b t", t=8)
                    x4 = xx.rearrange("p c (b t) -> p c b t", t=8)
                    for c in range(NCH):
                        dve(lambda e, c=c: e.tensor_tensor(out=x4[:, c, :, 1:8], in0=h4[:, c, :, 0:7], in1=h4[:, c, :, 1:8],
                                                           op=ALU.subtract), ["hf"], ["xx"])
                        dve(lambda e, c=c: e.tensor_tensor(out=x4[:, c, :, 0], in0=hlast[:, c, :], in1=h4[:, c, :, 0],
                                                           op=ALU.subtract), ["hf", "hlast"], ["xx"])
                    for c in range(NCH):
                        R.dma("sp", shifts_d[ri, :, c, :], h4[:, c, :, 7], r=["hf"], allow_slow_non_contiguous=True)
                else:
                    dve(lambda e: e.tensor_tensor(out=xx[:, :, 1:NB], in0=hf[:, :, 0:NB - 1], in1=hf[:, :, 1:NB],
                                                  op=ALU.subtract), ["hf"], ["xx"])
                    dve(lambda e: e.tensor_tensor(out=xx[:, :, 0], in0=hlast[:, :, 0], in1=hf[:, :, 0], op=ALU.subtract),
                        ["hf", "hlast"], ["xx"])
                    dve(lambda e: e.tensor_copy(out=hlast[:, :, 0], in_=hf[:, :, NB - 1]), ["hf", "xx"], ["hlast"])
                    if blk == 15:
                        R.dma("sp", shiftp_d[ri], hlast[:, :, 0], r=["hlast"], allow_slow_non_contiguous=True)
                nmx = [0]

                def pool(fn, r, w):
                    R.op("pool", fn, r=list(r) + ["vecs", "consts"], w=w)

                def mix(i):
                    b = nmx[0] % 3
                    nmx[0] += 1
                    pool(lambda e: e.tensor_tensor(out=xm[b][:], in0=xx, in1=V(("rw_mix", ri, i)), op=ALU.mult),
                         ["xx"], [("xm", b)])
                    pool(lambda e: e.tensor_tensor(out=xm[b][:], in0=xm[b][:], in1=hf, op=ALU.add), ["hf", ("xm", b)], [("xm", b)])
                    return xm[b], ("xm", b)

                def lora1(W, wname, src, srcres, rank, dst, dstres, func):
                    pi = 4 + npj[0] % 4
                    npj[0] += 1
                    p_ = ps[pi]

                    def mm(e):
                        ins = None
                        for c in range(NCH):
                            ins = e.matmul(p_[:rank, :NB], W[:, c, :rank], src[:, c, :], start=(c == 0), stop=(c == NCH - 1))
                        return ins
                    R.op("pe", mm, r=[wname, srcres], w=[("rps", pi)])
                    if func is None:
                        dve(lambda e: e.tensor_copy(out=dst[:rank, :], in_=p_[:rank, :NB]), [("rps", pi)], [dstres])
                    else:
                        act(lambda e: e.activation(out=dst[:rank, :], in_=p_[:rank, :NB], func=func), [("rps", pi)], [dstres])

                def lora2(W2, wname, mid, midres, rank, evac):
                    for ec in range(NCH):
                        pi = 4 + npj[0] % 4
                        npj[0] += 1
                        p_ = ps[pi]
                        R.op("pe", lambda e, p_=p_, ec=ec: e.matmul(p_[:, :NB], W2[:rank, ec * 128:(ec + 1) * 128], mid[:rank, :],
                                                                    start=True, stop=True),
                             r=[wname, midres], w=[("rps", pi)])
                        evac(p_, pi, ec)

                xw, xwr = mix(1)
                xa, xar = mix(4)
                xk, xkr = mix(2)
                lora1(w1, "rw_w1", xw, xwr, 64, tw, "tw", AF.Tanh)
                w0v = self.vec(("rw_w0", ri))
                lora2(w2, "rw_w2", tw, "tw", 64, lambda p_, pi, ec: act(
                    lambda e: e.activation(out=lw[:, ec, :], in_=p_[:, :NB], func=AF.Sigmoid, bias=w0v[:, ec:ec + 1]),
                    [("rps", pi)], ["lw"]))
                pool(lambda e: e.tensor_scalar(out=lw, in0=lw, scalar1=-0.6065306597126334, scalar2=None, op0=ALU.mult),
                     ["lw"], ["lw"])
                lw4 = lw.rearrange("p c (k t) -> p c k t", t=C)
                a4 = cs1.rearrange("p c (k t) -> p c k t", t=C)
                b4 = cs2.rearrange("p c (k t) -> p c k t", t=C)
                src4, srcn = lw4, "lw"
                dsts = [(a4, "cs1"), (b4, "cs2")]
                sh = 1
                k_ = 0
                while sh < C:
                    d4, dn = dsts[k_ % 2]
                    pool(lambda e, d4=d4, src4=src4, sh=sh: e.tensor_tensor(
                        out=d4[:, :, :, sh:C], in0=src4[:, :, :, sh:C], in1=src4[:, :, :, 0:C - sh], op=ALU.add),
                        [srcn], [dn])
                    pool(lambda e, d4=d4, src4=src4, sh=sh: e.tensor_copy(out=d4[:, :, :, 0:sh], in_=src4[:, :, :, 0:sh]),
                        [srcn], [dn])
                    src4, srcn = d4, dn
                    sh *= 2
                    k_ += 1
                Li4, Lin = src4, srcn
                Li = cs1 if Lin == "cs1" else cs2
                Le, Len = (cs2, "cs2") if Lin == "cs1" else (cs1, "cs1")
                pool(lambda e: e.tensor_tensor(out=Le, in0=Li, in1=lw, op=ALU.subtract), [Lin, "lw"], [Len])
                lora1(a1, "rw_a1", xa, xar, 64, ta, "ta", None)
                a0v = self.vec(("rw_a0", ri))
                lora2(a2, "rw_a2", ta, "ta", 64, lambda p_, pi, ec: act(
                    lambda e: e.activation(out=af[:, ec, :], in_=p_[:, :NB], func=AF.Sigmoid, bias=a0v[:, ec:ec + 1]),
                    [("rps", pi)], ["af"]))
                proj(wk, "rw_w_k", xk, xkr, D, lambda p4, pi, g: dve(
                    lambda e: e.tensor_copy(out=kf[:, g * 4:(g + 1) * 4, :], in_=p4), [("rps", pi)], ["kf"]))
                dve(lambda e: e.tensor_tensor(out=kkn, in0=kf, in1=V(("rw_k_k", ri)), op=ALU.mult), ["kf"], ["kkn"])
                act(lambda e: e.activation(out=sqb[:], in_=kkn, func=AF.Square), ["kkn"], ["sqb"])
                for hf_ in range(2):
                    pi = 4 + npj[0] % 4
                    npj[0] += 1
                    p_ = ps[pi]

                    def mmk(e, p_=p_, hf_=hf_):
                        ins = None
                        for j in range(4):
                            ins = e.matmul(p_[:, j * NB:(j + 1) * NB], self.blk64[:], sqb[:, hf_ * 4 + j, :], start=True, stop=True)
                        return ins
                    R.op("pe", mmk, r=["sqb", "consts"], w=[("rps", pi)])
                    act(lambda e, p_=p_: e.activation(out=rnb4[:], in_=p_[:, :], func=AF.Sqrt), [("rps", pi)], ["rnb4"])
                    dve(lambda e: e.tensor_scalar(out=rnb4[:], in0=rnb4[:], scalar1=1e-12, scalar2=None, op0=ALU.max),
                        ["rnb4"], ["rnb4"])
                    dve(lambda e: e.reciprocal(out=rnb4[:], in_=rnb4[:]), ["rnb4"], ["rnb4"])
                    dve(lambda e, hf_=hf_: e.tensor_tensor(out=kkn[:, hf_ * 4:(hf_ + 1) * 4, :], in0=kkn[:, hf_ * 4:(hf_ + 1) * 4, :],
                                                           in1=rnb4[:, :].rearrange("p (j t) -> p j t", t=NB), op=ALU.mult),
                        ["kkn", "rnb4"], ["kkn"])
                xv, xvr = mix(3)
                xg, xgr = mix(5)
                xr, xrr = mix(0)
                proj(wv, "rw_w_v", xv, xvr, D, lambda p4, pi, g: dve(
                    lambda e: e.tensor_copy(out=vf[:, g * 4:(g + 1) * 4, :], in_=p4), [("rps", pi)], ["vf"]))
                if ri == 0:
                    R.dma("sp", vfd[:, :, t0:t0 + NB], vf, r=["vf"])
                else:
                    lora1(v1, "rw_v1", xv, xvr, 32, tv, "tv", None)
                    v0v = self.vec(("rw_v0", 0))
                    vg, vgn = xx, "xx"
                    tmpV, tmpVn = hf, "hf"
                    lora2(v2, "rw_v2", tv, "tv", 32, lambda p_, pi, ec: act(
                        lambda e: e.activation(out=vg[:, ec, :], in_=p_[:, :NB], func=AF.Sigmoid, bias=v0v[:, ec:ec + 1]),
                        [("rps", pi)], [vgn]))
                    R.dma("sp", tmpV, vfd[:, :, t0:t0 + NB], w=[tmpVn])
                    dve(lambda e: e.tensor_tensor(out=tmpV, in0=tmpV, in1=vf, op=ALU.subtract), [tmpVn, "vf"], [tmpVn])
                    dve(lambda e: e.tensor_tensor(out=tmpV, in0=tmpV, in1=vg, op=ALU.mult), [tmpVn, vgn], [tmpVn])
                    dve(lambda e: e.tensor_tensor(out=vf, in0=vf, in1=tmpV, op=ALU.add), [tmpVn, "vf"], ["vf"])
                dve(lambda e: e.tensor_copy(out=Vb[:], in_=vf), ["vf"], ["Vb"])
                for hf_ in range(2):
                    rk_ = 128 if hf_ == 0 else 32
                    pi = 4 + npj[0] % 4
                    npj[0] += 1
                    p_ = ps[pi]

                    def mmg(e, p_=p_, hf_=hf_, rk_=rk_):
                        ins = None
                        for c in range(NCH):
                            ins = e.matmul(p_[:rk_, :NB], g1[:, c, hf_ * 128:hf_ * 128 + rk_], xg[:, c, :],
                                           start=(c == 0), stop=(c == NCH - 1))
                        return ins
                    R.op("pe", mmg, r=["rw_g1", xgr], w=[("rps", pi)])
                    act(lambda e, p_=p_, hf_=hf_, rk_=rk_: e.activation(out=tg[:rk_, hf_, :], in_=p_[:rk_, :NB], func=AF.Sigmoid),
                        [("rps", pi)], ["tg"])
                for g in range(2):
                    pi = 4 + npj[0] % 4
                    npj[0] += 1
                    p_ = ps[pi]

                    def mmg2(e, p_=p_, g=g):
                        ins = None
                        for j in range(4):
                            ec = g * 4 + j
                            e.matmul(p_[:, j * NB:(j + 1) * NB], g2[:, 0, ec * 128:(ec + 1) * 128], tg[:, 0, :], start=True, stop=False)
                            ins = e.matmul(p_[:, j * NB:(j + 1) * NB], g2[:32, 1, ec * 128:(ec + 1) * 128], tg[:32, 1, :],
                                           start=False, stop=True)
                        return ins
                    R.op("pe", mmg2, r=["rw_g2", "tg"], w=[("rps", pi)])
                    dve(lambda e, p_=p_, g=g: e.tensor_copy(out=gfb[:, g * 4:(g + 1) * 4, :],
                                                            in_=p_[:, :].rearrange("p (j t) -> p j t", t=NB)),
                        [("rps", pi)], ["gfb"])

                act(lambda e: e.activation(out=WC[:, :, :NCK], in_=Li4[:, :, :, C - 1], func=AF.Exp), [Lin], ["WC"])
                act(lambda e: e.activation(out=eNi, in_=Li, func=AF.Exp, scale=-1.0), [Lin], ["eNi"])
                act(lambda e: e.activation(out=Le, in_=Le, func=AF.Exp), [Len], [Len])
                act(lambda e: e.activation(out=Li, in_=Li, func=AF.Exp), [Lin], [Lin])
                eLe, eLen, eLi, eLin = Le, Len, Li, Lin

                def ev_r(p4, pi, g):
                    dve(lambda e: e.tensor_copy(out=rf[:, g * 4:(g + 1) * 4, :], in_=p4), [("rps", pi)], ["rf"])
                    dve(lambda e: e.tensor_tensor(out=Rt[:, g * 4:(g + 1) * 4, :], in0=p4, in1=eLi[:, g * 4:(g + 1) * 4, :],
                                                  op=ALU.mult), [("rps", pi), eLin], ["Rt"])
                proj(wr, "rw_w_r", xr, xrr, D, ev_r)
                dve(lambda e: e.scalar_tensor_tensor(out=At[:], in0=kkn, scalar=-1.0, in1=eLe, op0=ALU.mult, op1=ALU.mult),
                    ["kkn", eLen], ["At"])
                tmpA, tmpAn = eLe, eLen
                dve(lambda e: e.scalar_tensor_tensor(out=tmpA, in0=af, scalar=-1.0, in1=V(("rw_k_a", ri)), op0=ALU.add, op1=ALU.mult),
                    ["af", "At"], [tmpAn])
                dve(lambda e: e.scalar_tensor_tensor(out=kf, in0=tmpA, scalar=1.0, in1=kf, op0=ALU.add, op1=ALU.mult),
                    [tmpAn, "kf"], ["kf"])
                dve(lambda e: e.tensor_tensor(out=kkn, in0=kkn, in1=af, op=ALU.mult), ["kkn", "af", "At"], ["kkn"])
                dve(lambda e: e.tensor_tensor(out=kkn, in0=kkn, in1=eNi, op=ALU.mult), ["kkn", "eNi"], ["kkn"])
                dve(lambda e: e.tensor_copy(out=Bt[:], in_=kkn), ["kkn"], ["Bt"])
                WCb = WC[:, :, :NCK].unsqueeze(3).to_broadcast([128, NCH, NCK, C])
                dve(lambda e: e.tensor_tensor(out=BWb[:].rearrange("p c (k t) -> p c k t", t=C),
                                              in0=kkn.rearrange("p c (k t) -> p c k t", t=C), in1=WCb, op=ALU.mult),
                    ["kkn", "WC"], ["BWb"])
                dve(lambda e: e.tensor_tensor(out=tmpA, in0=rf, in1=V(("rw_r_k", ri)), op=ALU.mult), ["rf", "kf"], [tmpAn])
                dve(lambda e: e.tensor_tensor(out=sqb[:], in0=tmpA, in1=kf, op=ALU.mult), [tmpAn, "kf", "kkn"], ["sqb"])
                dve(lambda e: e.tensor_tensor(out=tmpA, in0=kf, in1=eNi, op=ALU.mult), ["kf", "eNi", "sqb"], [tmpAn])
                dve(lambda e: e.tensor_copy(out=Kt[:], in_=tmpA), [tmpAn], ["Kt"])
                dve(lambda e: e.tensor_tensor(out=KWb[:].rearrange("p c (k t) -> p c k t", t=C),
                                              in0=tmpA.rearrange("p c (k t) -> p c k t", t=C), in1=WCb, op=ALU.mult),
                    [tmpAn, "WC"], ["KWb"])
                RL = [(0, 128)] if C == 64 else [(0, C), (64, 64 + C)]

                def headmm2(pi, lhs, rhs, mrows, ncols, rres):
                    def f(e):
                        ins = None
                        for h in range(16):
                            pb, c = (h % 2) * 64, h // 2
                            ins = e.matmul(ps[pi][pb:pb + mrows, c * 64:c * 64 + ncols], lhs(pb, c), rhs(pb, c),
                                           start=True, stop=True)
                        return ins
                    R.op("pe", f, r=rres, w=[("rps", pi)])

                def pv(pi, ncols):
                    return ps[pi][:, :].rearrange("p (c t) -> p c t", t=64)[:, :, :ncols]

                def rows_op(fn, r, w):
                    for (r0, r1) in RL:
                        dve(lambda e, r0=r0, r1=r1: fn(e, r0, r1), r, w)

                def do_chunk(ck):
                    o = ck * C
                    fmx = lambda tile: (lambda pb, c: tile[pb:pb + 64, c, o:o + C])
                    tk = lambda tile: (lambda pb, c: tile[pb:pb + C, c, :C])
                    tvv = lambda tile: (lambda pb, c: tile[pb:pb + C, c, :])

                    def evm(dst, dstn, pi, mask):
                        rows_op(lambda e, r0, r1: e.tensor_tensor(out=dst[r0:r1, :, :C], in0=pv(pi, C)[r0:r1],
                                                                  in1=mask[r0:r1, :, :C], op=ALU.mult),
                                [("rps", pi)], [dstn])

                    def evc(dst, dstn, pi, ncols):
                        rows_op(lambda e, r0, r1: e.tensor_copy(out=dst[r0:r1, :, :ncols], in_=pv(pi, ncols)[r0:r1]),
                                [("rps", pi)], [dstn])

                    headmm2(0, fmx(Bt), fmx(At), C, C, ["Bt", "At"])
                    evm(Pm, "Pm", 0, mSU)
                    headmm2(1, fmx(At), fmx(Bt), C, C, ["Bt", "At"])
                    evm(Qm, "Qm", 1, mSL)
                    rows_op(lambda e, r0, r1: e.tensor_tensor(out=Tm[r0:r1, :, :C], in0=Pm[r0:r1, :, :C],
                                                              in1=mI[r0:r1, :, :C], op=ALU.add), ["Pm"], ["Tm"])
                    headmm2(2, fmx(Kt), fmx(At), C, C, ["Kt", "At"])
                    evm(MKA, "MKA", 2, mSU)
                    headmm2(3, fmx(Bt), fmx(Rt), C, C, ["Bt", "Rt"])
                    evm(MBR, "MBR", 3, mU)
                    headmm2(0, fmx(Kt), fmx(Rt), C, C, ["Kt", "Rt"])
                    evm(MKR, "MKR", 0, mU)
                    if DBG.get("rl", 9) < 2.2:
                        return
                    for n_ in range(1, LV + 1):
                        if n_ < LV:
                            headmm2(0, tk(Qm), tk(Pm), C, C, ["Pm", "Qm"])
                        headmm2(1, tk(Pm), tk(Qm), C, C, ["Pm", "Qm"])
                        if n_ < LV:
                            evc(Pm, "Pm", 0, C)
                        evc(Qm, "Qm", 1, C)
                        headmm2(2, tk(Qm), tk(Tm), C, C, ["Qm", "Tm"])
                        rows_op(lambda e, r0, r1: e.tensor_tensor(out=Tm[r0:r1, :, :C], in0=pv(2, C)[r0:r1],
                                                                  in1=Tm[r0:r1, :, :C], op=ALU.add),
                                [("rps", 2), "Tm"], ["Tm"])
                    if DBG.get("rl", 9) < 2.5:
                        return
                    ptv = pv(4, 64)
                    for srcT, srcn_, dstT, dstn_ in ((At, "At", AtT, "AtT"), (Vb, "Vb", VT, "VT"),
                                                     (BWb, "BWb", BWT, "BWT"), (KWb, "KWb", KWT, "KWT")):
                        def ftr(e, srcT=srcT):
                            ins = None
                            for h in range(16):
                                pb, c = (h % 2) * 64, h // 2
                                ins = e.matmul(ps[4][pb:pb + C, c * 64:(c + 1) * 64], srcT[pb:pb + 64, c, o:o + C],
                                               self.identb[pb:pb + 64, pb:pb + 64], start=True, stop=True)
                            return ins
                        R.op("pe", ftr, r=[srcn_, "consts"], w=[("rps", 4)])
                        rows_op(lambda e, r0, r1, dstT=dstT: e.tensor_copy(out=dstT[r0:r1, :, :], in_=ptv[r0:r1]),
                                [("rps", 4)], [dstn_])
                    headmm2(3, tvv(AtT), tk(Tm), 64, C, ["AtT", "Tm"])
                    dve(lambda e: e.tensor_copy(out=Ah[:, :, :C], in_=pv(3, C)), [("rps", 3)], ["Ah"])
                    if DBG.get("rl", 9) < 2.7:
                        return
                    headmm2(0, tk(MKA), tvv(VT), C, 64, ["MKA", "VT"])
                    evc(XV, "XV", 0, 64)
                    headmm2(1, tk(Tm), tvv(XV), C, 64, ["Tm", "XV"])
                    evc(SAV, "SAV", 1, 64)
                    if DBG.get("rl", 9) < 3:
                        return
                    if samp:
                        R.dma("sp", Sf[:], swkv_d[ri, ck].rearrange("(c h2) k v -> (h2 k) c v", h2=2), w=["Sf"])
                        dve(lambda e: e.tensor_copy(out=Sb[:], in_=Sf[:]), ["Sf"], ["Sb"])
                    fS = lambda pb, c: Sb[pb:pb + 64, c, :]
                    headmm2(2, lambda pb, c: Ah[pb:pb + 64, c, :C], fS, C, 64, ["Ah", "Sb"])
                    rows_op(lambda e, r0, r1: e.tensor_tensor(out=SAb[r0:r1, :, :], in0=pv(2, 64)[r0:r1],
                                                              in1=SAV[r0:r1, :, :], op=ALU.add),
                            [("rps", 2), "SAV"], ["SAb"])

                    def fy(e):
                        ins = None
                        for h in range(16):
                            pb, c = (h % 2) * 64, h // 2
                            o_ = ps[3][pb:pb + C, c * 64:(c + 1) * 64]
                            e.matmul(o_, Rt[pb:pb + 64, c, o:o + C], Sb[pb:pb + 64, c, :], start=True, stop=False)
                            e.matmul(o_, MBR[pb:pb + C, c, :C], SAb[pb:pb + C, c, :], start=False, stop=False)
                            ins = e.matmul(o_, MKR[pb:pb + C, c, :C], VT[pb:pb + C, c, :], start=False, stop=True)
                        return ins
                    R.op("pe", fy, r=["Rt", "Sb", "MBR", "SAb", "MKR", "VT"], w=[("rps", 3)])
                    evc(ytm, "ytm", 3, 64)

                    def fs(e):
                        ins = None
                        for h in range(16):
                            pb, c = (h % 2) * 64, h // 2
                            o_ = ps[0][pb:pb + 64, c * 64:(c + 1) * 64]
                            e.matmul(o_, BWT[pb:pb + C, c, :], SAb[pb:pb + C, c, :], start=True, stop=False)
                            ins = e.matmul(o_, KWT[pb:pb + C, c, :], VT[pb:pb + C, c, :], start=False, stop=True)
                        return ins
                    R.op("pe", fs, r=["BWT", "SAb", "KWT", "VT"], w=[("rps", 0)])
                    dve(lambda e: e.tensor_tensor(out=Sf[:], in0=Sf[:], in1=WC[:, :, ck:ck + 1].to_broadcast([128, NCH, 64]),
                                                  op=ALU.mult), ["Sf", "WC"], ["Sf"])
                    dve(lambda e: e.tensor_tensor(out=Sf[:], in0=Sf[:], in1=pv(0, 64), op=ALU.add), ["Sf", ("rps", 0)], ["Sf"])
                    dve(lambda e: e.tensor_copy(out=Sb[:], in_=Sf[:]), ["Sf"], ["Sb"])
                    if samp:
                        R.dma("sp", wkvs_d[ri, ck].rearrange("(c h2) k v -> (h2 k) c v", h2=2), Sf[:], r=["Sf"])
                    elif blk == 15 and ck == NCK - 1:
                        R.dma("sp", wkvp_d[ri].rearrange("(c h2) k v -> (h2 k) c v", h2=2), Sf[:], r=["Sf"])
                    if DBG.get("rl", 9) < 4:
                        return
                    rows_op(lambda e, r0, r1: e.tensor_reduce(out=st1[r0:r1, :], in_=ytm[r0:r1], axis=AX.X, op=ALU.add),
                            ["ytm"], ["st1"])
                    rows_op(lambda e, r0, r1: e.tensor_scalar(out=st1[r0:r1, :], in0=st1[r0:r1, :], scalar1=1.0 / 64,
                                                              scalar2=None, op0=ALU.mult), ["st1"], ["st1"])
                    rows_op(lambda e, r0, r1: e.tensor_tensor(out=yc[r0:r1], in0=ytm[r0:r1],
                                                              in1=st1[r0:r1, :].unsqueeze(2).to_broadcast([r1 - r0, NCH, 64]),
                                                              op=ALU.subtract), ["ytm", "st1"], ["yc"])
                    rows_op(lambda e, r0, r1: e.tensor_tensor(out=ytm[r0:r1], in0=yc[r0:r1], in1=yc[r0:r1], op=ALU.mult),
                            ["yc"], ["ytm"])
                    rows_op(lambda e, r0, r1: e.tensor_reduce(out=st2[r0:r1, :], in_=ytm[r0:r1], axis=AX.X, op=ALU.add),
                            ["ytm"], ["st2"])
                    for (r0, r1) in RL:
                        act(lambda e, r0=r0, r1=r1: e.activation(out=st2[r0:r1, :], in_=st2[r0:r1, :], func=AF.Sqrt,
                                                                 scale=1.0 / 64, bias=self.eps_lnx[r0:r1, 0:1]),
                            ["st2"], ["st2"])
                    rows_op(lambda e, r0, r1: e.reciprocal(out=st2[r0:r1, :], in_=st2[r0:r1, :]), ["st2"], ["st2"])
                    rows_op(lambda e, r0, r1: e.tensor_tensor(out=yc[r0:r1], in0=yc[r0:r1],
                                                              in1=st2[r0:r1, :].unsqueeze(2).to_broadcast([r1 - r0, NCH, 64]),
                                                              op=ALU.mult), ["yc", "st2"], ["yc"])

                    def ftb(e):
                        ins = None
                        for h in range(16):
                            pb, c = (h % 2) * 64, h // 2
                            ins = e.matmul(ps[1][pb:pb + 64, c * 64:c * 64 + C], yc[pb:pb + C, c, :],
                                           self.identf[pb:pb + C, pb:pb + C], start=True, stop=True)
                        return ins
                    R.op("pe", ftb, r=["yc", "consts"], w=[("rps", 1)])
                    dve(lambda e: e.tensor_copy(out=yf[:, :, o:o + C], in_=pv(1, C)), [("rps", 1)], ["yf"])

                for ck in range(NCK if DBG.get("rl", 9) >= 2 else 0):
                    do_chunk(ck)
                dve(lambda e: e.tensor_tensor(out=yf, in0=yf, in1=V(("rw_lnx_g", ri)), op=ALU.mult), ["yf"], ["yf"])
                dve(lambda e: e.tensor_tensor(out=yf, in0=yf, in1=V(("rw_lnx_b", ri)), op=ALU.add), ["yf"], ["yf"])
                for hf_ in range(2):
                    pi = 4 + npj[0] % 4
                    npj[0] += 1
                    p_ = ps[pi]

                    def mmb(e, p_=p_, hf_=hf_):
                        ins = None
                        for j in range(4):
                            ins = e.matmul(p_[:, j * NB:(j + 1) * NB], self.blk64[:], sqb[:, hf_ * 4 + j, :], start=True, stop=True)
                        return ins
                    R.op("pe", mmb, r=["sqb", "consts"], w=[("rps", pi)])
                    dve(lambda e, p_=p_, hf_=hf_: e.tensor_tensor(out=rnb4[:, :].rearrange("p (j t) -> p j t", t=NB),
                                                                  in0=p_[:, :].rearrange("p (j t) -> p j t", t=NB),
                                                                  in1=vf[:, hf_ * 4:(hf_ + 1) * 4, :], op=ALU.mult),
                        [("rps", pi), "vf"], ["rnb4"])
                    dve(lambda e, hf_=hf_: e.tensor_tensor(out=yf[:, hf_ * 4:(hf_ + 1) * 4, :], in0=yf[:, hf_ * 4:(hf_ + 1) * 4, :],
                                                           in1=rnb4[:, :].rearrange("p (j t) -> p j t", t=NB), op=ALU.add),
                        ["yf", "rnb4"], ["yf"])
                dve(lambda e: e.tensor_tensor(out=yg[:], in0=yf, in1=gfb[:], op=ALU.mult), ["yf", "gfb"], [("xm", 0)])

                def ev_o(p4, pi, g):
                    dve(lambda e: e.tensor_tensor(out=xb[:, g * 4:(g + 1) * 4, :], in0=xb[:, g * 4:(g + 1) * 4, :], in1=p4,
                                                  op=ALU.add), [("rps", pi), "xb"], ["xb"])
                proj(wo, "rw_w_o", yg, ("xm", 0), D, ev_o)
                R.dma("sp", xs_v[:, :, t0:t0 + NB], xb, r=["xb"])
            for blk in (DBG.get("blks", range(nblk)) if DBG.get("rl", 9) >= 1 else []):
                do_block(blk)
            R.barrier()
            self.psb = self.psb_save
            self.stack = old
        for c in range(NCH):
            R.dma("sp", self.x[:, c, :], xs_v[:, c, :], w=[("x", c, b) for b in range(len(BLOCKS))])
        R.barrier()

    def phase_attn(self, l):
        R = self.R
        with contextlib.ExitStack() as st:
            old, self.stack = self.stack, st
            self.sq_scr = self.sb("sq_scr", [128, NCH, 512], BF16)
            self.rstd = self.sb("rstd", [128, 512], F32)
            self.psb = [self.ps(f"psb{i}", [128, 512], F32) for i in range(8)]
            wq = self.sb("wq", [128, NCH, D], BF16)
            wo = self.sb("wo", [128, NCH, D], BF16)
            KTp = self.sb("KTp", [128, NCH, MEM], BF16)
            Vp = self.sb("Vp", [128, 2, D], BF16)
            hbk = self.sb("hbk", [128, NCH, 512], BF16)
            qT = self.sb("qT", [128, NCH, 512], BF16)
            oT = self.sb("oT", [128, NCH, 512], BF16)
            PT = [self.sb(f"PT{i}", [128, 2, 512], BF16) for i in range(2)]
            rden = self.sb("rden", [128, 512], F32)
            wkvb = [self.sb(f"wkvb{i}", [128, NCH, 512], BF16) for i in range(2)]
            kvo = [self.sb(f"kvo{i}", [128, 512], F32) for i in range(2)]
            KTs = [self.sb(f"KTs{i}", [128, NCH, MEM], BF16) for i in range(2)]
            Vs = [self.sb(f"Vs{i}", [128, 2, D], BF16) for i in range(2)]
            PTs = self.sb("PTs", [128, 8, 8], BF16)
            rdens = self.sb("rdens", [128, 4, 8], F32)
            psb = self.psb

            wkv_v = self.wkv_d[l].rearrange("(c p) e -> p c e", p=128)
            nk = 0
            for j in range(4):
                b = j % 2
                R.dma("pool", wkvb[b][:], wkv_v[:, :, j * 512:(j + 1) * 512], w=[("wkvb", b)])
                for mt in range(2):
                    ps = psb[mt]

                    def mm(e, ps=ps, b=b, mt=mt):
                        ins = None
                        for c in range(NCH):
                            ins = e.matmul(ps[:, :], self.memT[:, c, mt * 128:(mt + 1) * 128], wkvb[b][:, c, :],
                                           start=(c == 0), stop=(c == NCH - 1))
                        return ins
                    R.op("pe", mm, r=["memT", ("wkvb", b)], w=[("ps", mt)])
                    kb = nk % 2
                    nk += 1
                    R.op("dve", lambda e, ps=ps, kb=kb: e.tensor_copy(out=kvo[kb][:], in_=ps[:]),
                         r=[("ps", mt)], w=[("kvo", kb)])
                    dst = self.memk_d if j < 2 else self.memv_d
                    R.dma("sp", dst[l, mt * 128:(mt + 1) * 128, (j % 2) * 512:(j % 2 + 1) * 512], kvo[kb][:],
                          r=[("kvo", kb)])
                    if j >= 2:
                        R.op("dve", lambda e, ps=ps, mt=mt, j=j: e.tensor_copy(
                            out=Vp[:, mt, (j - 2) * 512:(j - 1) * 512], in_=ps[:]),
                            r=[("ps", mt)], w=["Vp"])
                if j < 2:
                    for ec in range(4):
                        pi = 2 + ec % 2
                        ps = psb[pi]

                        def mm(e, ps=ps, b=b, ec=ec):
                            ins = None
                            for c in range(NCH):
                                ins = e.matmul(ps[:, :MEM], wkvb[b][:, c, ec * 128:(ec + 1) * 128], self.memT[:, c, :],
                                               start=(c == 0), stop=(c == NCH - 1))
                            return ins
                        R.op("pe", mm, r=["memT", ("wkvb", b)], w=[("ps", pi)])
                        R.op("dve", lambda e, ps=ps, j=j, ec=ec: e.tensor_copy(out=KTp[:, j * 4 + ec, :], in_=ps[:, :MEM]),
                             r=[("ps", pi)], w=["KTp"])

            for hf in range(2):
                R.dma("pool", wq[:, :, hf * 512:(hf + 1) * 512],
                      self.wq_d[l].rearrange("(c p) e -> p c e", p=128)[:, :, hf * 512:(hf + 1) * 512], w=["wq"])
                R.dma("pool", wo[:, :, hf * 512:(hf + 1) * 512],
                      self.wo_d[l].rearrange("(c p) e -> p c e", p=128)[:, :, hf * 512:(hf + 1) * 512], w=["wo"])

            npt = 0
            lvl = DBG.get("lvl", 9)
            for blk, (t0, n) in enumerate(BLOCKS if lvl >= 2 else []):
                self.norm_block(blk, ("norm_g", l, 1), hbk, ["hbk"], out_off=0)
                for ec in range(NCH):
                    pi = ec % 2
                    ps = psb[pi]

                    def mm(e, ps=ps, ec=ec, n=n):
                        ins = None
                        for c in range(NCH):
                            ins = e.matmul(ps[:, :n], wq[:, c, ec * 128:(ec + 1) * 128], hbk[:, c, :n],
                                           start=(c == 0), stop=(c == NCH - 1))
                        return ins
                    R.op("pe", mm, r=["wq", "hbk"], w=[("ps", pi)])
                    R.op("dve", lambda e, ps=ps, ec=ec, n=n: e.tensor_copy(out=qT[:, ec, :n], in_=ps[:, :n]),
                         r=[("ps", pi)], w=[("qT", ec)])
                if lvl < 3:
                    continue
                if blk < 4:
                    for h in range(4):
                        pb = npt % 2
                        npt += 1
                        for mt in range(2):
                            pi = 2 + mt
                            ps = psb[pi]

                            def mm(e, ps=ps, h=h, mt=mt, n=n):
                                ins = None
                                for dc in range(2):
                                    ins = e.matmul(ps[:, :n], KTp[:, h * 2 + dc, mt * 128:(mt + 1) * 128],
                                                   qT[:, h * 2 + dc, :n], start=(dc == 0), stop=(dc == 1))
                                return ins
                            R.op("pe", mm, r=["KTp", ("qT", h * 2), ("qT", h * 2 + 1)], w=[("ps", pi)])
                            R.op("act", lambda e, ps=ps, pb=pb, mt=mt, n=n: e.activation(
                                out=PT[pb][:, mt, :n], in_=ps[:, :n], func=AF.Exp, scale=1.0 / 16.0),
                                r=[("ps", pi)], w=[("PT", pb, mt)])
                        ps4 = psb[4]

                        def mmd(e, pb=pb, n=n):
                            ins = None
                            for mt in range(2):
                                ins = e.matmul(ps4[:, :n], self.ones_1[:], PT[pb][:, mt, :n], start=(mt == 0), stop=(mt == 1))
                            return ins
                        R.op("pe", mmd, r=["ones_1", ("PT", pb, 0), ("PT", pb, 1)], w=[("ps", 4)])
                        R.op("dve", lambda e, n=n: e.reciprocal(out=rden[:, :n], in_=ps4[:, :n]), r=[("ps", 4)], w=["rden"])
                        for dc in range(2):
                            pi = 5 + dc
                            ps = psb[pi]

                            def mmv(e, ps=ps, h=h, dc=dc, pb=pb, n=n):
                                ins = None
                                for mt in range(2):
                                    ins = e.matmul(ps[:, :n], Vp[:, mt, h * 256 + dc * 128:h * 256 + (dc + 1) * 128],
                                                   PT[pb][:, mt, :n], start=(mt == 0), stop=(mt == 1))
                                return ins
                            R.op("pe", mmv, r=["Vp", ("PT", pb, 0), ("PT", pb, 1)], w=[("ps", pi)])
                            R.op("dve", lambda e, ps=ps, h=h, dc=dc, n=n: e.tensor_tensor(
                                out=oT[:, h * 2 + dc, :n], in0=ps[:, :n], in1=rden[:, :n], op=ALU.mult),
                                r=[("ps", pi), "rden"], w=[("oT", h * 2 + dc)])
                else:
                    for sb_ in range(DBG.get("nseq", NSEQ)):
                        kb = sb_ % 2
                        R.dma("pool", KTs[kb][:], self.cKT_d[l, sb_].rearrange("h (dc p) m -> p (h dc) m", p=128),
                              w=[("KTs", kb)])
                        R.dma("pool", Vs[kb][:], self.cV_d[l, sb_].rearrange("(mt p) e -> p mt e", p=128),
                              w=[("Vs", kb)])
                        c0 = sb_ * 8
                        ps = psb[2 + sb_ % 2]
                        pi = 2 + sb_ % 2

                        def mms(e, ps=ps, kb=kb, c0=c0):
                            ins = None
                            for h in range(4):
                                for mt in range(2):
                                    for dc in range(2):
                                        ins = e.matmul(ps[:, (h * 2 + mt) * 8:(h * 2 + mt + 1) * 8],
                                                       KTs[kb][:, h * 2 + dc, mt * 128:(mt + 1) * 128],
                                                       qT[:, h * 2 + dc, c0:c0 + 8], start=(dc == 0), stop=(dc == 1))
                            return ins
                        R.op("pe", mms, r=[("KTs", kb)] + [("qT", c) for c in range(NCH)], w=[("ps", pi)])
                        R.op("act", lambda e, ps=ps: e.activation(out=PTs[:].rearrange("p a b -> p (a b)"), in_=ps[:, :64],
                                                                   func=AF.Exp, scale=1.0 / 16.0),
                             r=[("ps", pi)], w=["PTs"])
                        ps4 = psb[4]

                        def mmd(e):
                            ins = None
                            for h in range(4):
                                for mt in range(2):
                                    ins = e.matmul(ps4[:, h * 8:(h + 1) * 8], self.ones_1[:], PTs[:, h * 2 + mt, :],
                                                   start=(mt == 0), stop=(mt == 1))
                            return ins
                        R.op("pe", mmd, r=["ones_1", "PTs"], w=[("ps", 4)])
                        R.op("dve", lambda e: e.reciprocal(out=rdens[:].rearrange("p a b -> p (a b)"), in_=ps4[:, :32]),
                             r=[("ps", 4)], w=["rdens"])
                        pi2 = 5 + sb_ % 2
                        psv = psb[pi2]

                        def mmv(e, psv=psv, kb=kb):
                            ins = None
                            for h in range(4):
                                for dc in range(2):
                                    for mt in range(2):
                                        ins = e.matmul(psv[:, (h * 2 + dc) * 8:(h * 2 + dc + 1) * 8],
                                                       Vs[kb][:, mt, h * 256 + dc * 128:h * 256 + (dc + 1) * 128],
                                                       PTs[:, h * 2 + mt, :], start=(mt == 0), stop=(mt == 1))
                            return ins
                        R.op("pe", mmv, r=[("Vs", kb), "PTs"], w=[("ps", pi2)])

                        def nrm(e, psv=psv, c0=c0):
                            ins = None
                            for h in range(4):
                                for dc in range(2):
                                    ins = e.tensor_tensor(out=oT[:, h * 2 + dc, c0:c0 + 8],
                                                          in0=psv[:, (h * 2 + dc) * 8:(h * 2 + dc + 1) * 8],
                                                          in1=rdens[:, h, :], op=ALU.mult)
                            return ins
                        R.op("dve", nrm, r=[("ps", pi2), "rdens"], w=[("oT", c) for c in range(NCH)])
                for ec in range(NCH):
                    pi = ec % 2
                    ps = psb[pi]

                    def mm(e, ps=ps, ec=ec, n=n):
                        ins = None
                        for c in range(NCH):
                            ins = e.matmul(ps[:, :n], wo[:, c, ec * 128:(ec + 1) * 128], oT[:, c, :n],
                                           start=(c == 0), stop=(c == NCH - 1))
                        return ins
                    R.op("pe", mm, r=["wo"] + [("oT", c) for c in range(NCH)], w=[("ps", pi)])
                    R.op("dve", lambda e, ps=ps, ec=ec, t0=t0, n=n: e.tensor_tensor(
                        out=self.x[:, ec, t0:t0 + n], in0=self.x[:, ec, t0:t0 + n], in1=ps[:, :n], op=ALU.add),
                        r=[("ps", pi)], w=[("x", ec, blk)])
            R.barrier()
            self.stack = old

    def phase_mlp(self, l):
        R = self.R
        with contextlib.ExitStack() as st:
            old, self.stack = self.stack, st
            self.sq_scr = self.sb("sq_scr", [128, NCH, 512], BF16)
            self.rstd = self.sb("rstd", [128, 512], F32)
            self.psb = [self.ps(f"psb{i}", [128, 512], F32) for i in range(8)]
            self.hb = self.sb("hb", [128, NCH, T], BF16)
            wu = [self.sb(f"wu{i}", [128, NCH, 512], BF16) for i in range(2)]
            wd = [self.sb(f"wd{i}", [128, 4, D], BF16) for i in range(2)]
            hT = [self.sb(f"hT{i}", [128, 4, 512], BF16) for i in range(2)]
            rl = [self.sb(f"rl{i}", [128, 512], F32) for i in range(2)]
            wu_v = self.w_up_d[l].rearrange("(c p) f -> p c f", p=128)
            wd_v = self.w_dn_d[l].rearrange("(fc p) e -> p fc e", p=128)

            for blk in range(len(BLOCKS)):
                self.norm_block(blk, ("norm_g", l, 2), self.hb, [("hb", blk)])

            nrl = 0
            for j in range(8):
                b = j % 2
                R.dma("pool", wu[b][:], wu_v[:, :, j * 512:(j + 1) * 512], w=[("wu", b)])
                R.dma("pool", wd[b][:], wd_v[:, j * 4:(j + 1) * 4, :], w=[("wd", b)])
                for blk, (t0, n) in enumerate(BLOCKS):
                    hb_ = (j * len(BLOCKS) + blk) % 2
                    for fc in range(4):
                        ps = self.psb[fc]

                        def mm(e, ps=ps, fc=fc, t0=t0, n=n, b=b):
                            ins = None
                            for c in range(NCH):
                                ins = e.matmul(ps[:, :n], wu[b][:, c, fc * 128:(fc + 1) * 128],
                                               self.hb[:, c, t0:t0 + n], start=(c == 0), stop=(c == NCH - 1))
                            return ins
                        R.op("pe", mm, r=[("wu", b), ("hb", blk)], w=[("ps", fc)])
                        rb = nrl % 2
                        nrl += 1
                        R.op("act", lambda e, ps=ps, rb=rb, n=n: e.activation(out=rl[rb][:, :n], in_=ps[:, :n],
                                                                               func=AF.Relu),
                             r=[("ps", fc)], w=[("rl", rb)])
                        R.op("pool", lambda e, rb=rb, fc=fc, n=n, hb_=hb_: e.tensor_tensor(
                            out=hT[hb_][:, fc, :n], in0=rl[rb][:, :n], in1=rl[rb][:, :n], op=ALU.mult),
                            r=[("rl", rb)], w=[("hT", hb_, fc)])
                    for oc in range(NCH):
                        pi = 4 + oc % 4
                        ps = self.psb[pi]

                        def mm2(e, ps=ps, oc=oc, n=n, b=b, hb_=hb_):
                            ins = None
                            for fc in range(4):
                                ins = e.matmul(ps[:, :n], wd[b][:, fc, oc * 128:(oc + 1) * 128],
                                               hT[hb_][:, fc, :n], start=(fc == 0), stop=(fc == 3))
                            return ins
                        R.op("pe", mm2, r=[("wd", b)] + [("hT", hb_, fc) for fc in range(4)], w=[("ps", pi)])
                        R.op("dve", lambda e, ps=ps, oc=oc, t0=t0, n=n: e.tensor_tensor(
                            out=self.x[:, oc, t0:t0 + n], in0=self.x[:, oc, t0:t0 + n], in1=ps[:, :n], op=ALU.add),
                            r=[("ps", pi)], w=[("x", oc, blk)])
            R.barrier()
            self.stack = old

    def final_norm(self):
        R = self.R
        with contextlib.ExitStack() as st:
            old, self.stack = self.stack, st
            self.sq_scr = self.sb("sq_scr", [128, NCH, 512], BF16)
            self.rstd = self.sb("rstd", [128, 512], F32)
            self.psb = [self.ps(f"psb{i}", [128, 512], F32) for i in range(8)]
            yb = [self.sb(f"yb{i}", [128, NCH, 512], F32) for i in range(2)]
            yv = self.yT_d.rearrange("(c p) t -> p c t", p=128)
            for blk, (t0, n) in enumerate(BLOCKS):
                b = blk % 2
                self.norm_block(blk, ("final_g",), yb[b], [("yb", b)], out_off=0)
                R.dma("sp", yv[:, :, t0:t0 + n], yb[b][:, :, :n], r=[("yb", b)])
            R.barrier()
            self.stack = old


_CACHE = {}
PHASES = ("conv", "rwkv", "attn", "mlp")
NLAYERS = DEPTH
DBG = {}


def get_nc(phases):
    key = tuple(phases)
    if key not in _CACHE:
        b = Builder(phases)
        nc = b.build()
        _CACHE[key] = (nc, set(b.dins), set(b.douts))
    return _CACHE[key]


def make_rconst():
    i = np.arange(64)
    su = (i[:, None] < i[None, :]).astype(np.float32)
    sl_ = (i[:, None] > i[None, :]).astype(np.float32)
    u = (i[:, None] <= i[None, :]).astype(np.float32)
    ey = np.eye(64, dtype=np.float32)
    cm = np.stack([np.broadcast_to(m[:, None, :], (64, 8, 64)) for m in (su, sl_, u, ey)], axis=1)
    out = np.zeros((128, 2048 + 256), np.float32)
    out[:64, :2048] = cm.reshape(64, 2048)
    out[64:, :2048] = cm.reshape(64, 2048)
    out[:, 2048:2176] = np.eye(128, dtype=np.float32)
    out[:64, 2176:2240] = 1.0
    out[64:, 2240:2304] = 1.0
    return out


def extra_inputs(inp, c, sl):
    f32 = np.float32
    return {
        "rconst": lambda: make_rconst(),
        "sshiftT": lambda: np.ascontiguousarray(np.asarray(inp["state_shift"][:, sl], f32).transpose(0, 2, 1)),
        "swkvT": lambda: np.ascontiguousarray(np.asarray(inp["state_wkv"][:, sl], f32).transpose(0, 1, 2, 4, 3)),
        **{nm: (lambda nm=nm: np.asarray(inp[nm], f32)) for nm in
           ("rw_w_r", "rw_w_k", "rw_w_v", "rw_w_o", "rw_w1", "rw_w2", "rw_a1", "rw_a2", "rw_g1", "rw_g2", "rw_v1", "rw_v2")},
        "cv_w_in": lambda: np.asarray(inp["cv_w_in"], f32),
        "cv_w_out": lambda: np.asarray(inp["cv_w_out"], f32),
        "sconvT": lambda: np.ascontiguousarray(np.asarray(inp["state_conv"][:, sl], f32).transpose(0, 3, 1, 2)),
    }


def make_in_maps(inp, used, n_cores=8):
    vecs = pack_vecs(inp)
    maps = []
    for c in range(n_cores):
        xp = np.asarray(inp["x_prompt"][c], np.float32)
        xs = np.asarray(inp["x_sample"][c * NSEQ:(c + 1) * NSEQ], np.float32).reshape(TS, D)
        xT = np.ascontiguousarray(np.concatenate([xp, xs], axis=0).T)
        sl = slice(c * NSEQ, (c + 1) * NSEQ)
        sl = slice(c * NSEQ, (c + 1) * NSEQ)
        f32 = np.float32
        m = {"xT": lambda: xT, "vecs": lambda: vecs,
             "mlp_w_up": lambda: np.asarray(inp["mlp_w_up"], f32),
             "mlp_w_down": lambda: np.asarray(inp["mlp_w_down"], f32),
             "memT": lambda: np.ascontiguousarray(np.asarray(inp["mem_prompt"][c], f32).T),
             "xa_w_kv": lambda: np.asarray(inp["xa_w_kv"], f32),
             "xa_w_q": lambda: np.asarray(inp["xa_w_q"], f32),
             "xa_w_o": lambda: np.asarray(inp["xa_w_o"], f32),
             "cKT": lambda: np.ascontiguousarray(np.asarray(inp["cache_mem_k"][:, sl], f32).transpose(0, 1, 3, 4, 2)),
             "cV": lambda: np.ascontiguousarray(np.asarray(inp["cache_mem_v"][:, sl], f32).reshape(DEPTH, NSEQ, MEM, D))}
        m.update(extra_inputs(inp, c, sl))
        maps.append({k: v() for k, v in m.items() if k in used})
    return maps


def kernel(**inputs):
    phases = PHASES
    nc, used, douts = get_nc(phases)
    in_maps = make_in_maps(inputs, used)
    res = run_bass_kernel_spmd(nc, in_maps, core_ids=list(range(8)))
    outs = res.results
    yT = np.stack([np.asarray(o["yT"]) for o in outs])
    y_prompt = np.ascontiguousarray(yT[:, :, :TP].transpose(0, 2, 1))
    y_sample = np.ascontiguousarray(yT[:, :, TP:].transpose(0, 2, 1)).reshape(8 * NSEQ, 8, D)
    res_extra = {}
    for k_ in douts:
        if k_.startswith("dbg_"):
            res_extra[k_] = np.stack([np.asarray(o[k_]) for o in outs])
    if "convpT" in douts:
        cp = np.stack([np.asarray(o["convpT"]) for o in outs], axis=1)
        res_extra["conv_prompt"] = np.ascontiguousarray(cp.transpose(0, 1, 3, 2))
        cs = np.stack([np.asarray(o["convsT"]) for o in outs], axis=1)
        res_extra["conv_sample"] = np.ascontiguousarray(cs.transpose(0, 1, 3, 4, 2)).reshape(2, 8 * NSEQ, 30, D)
    if "shiftpT" in douts:
        sp = np.stack([np.asarray(o["shiftpT"]) for o in outs], axis=1)
        res_extra["shift_prompt"] = np.ascontiguousarray(sp.transpose(0, 1, 3, 2)).reshape(2, 8, D)
        ss = np.stack([np.asarray(o["shiftsT"]) for o in outs], axis=1)
        res_extra["shift_sample"] = np.ascontiguousarray(ss.transpose(0, 1, 4, 3, 2)).reshape(2, 8 * NSEQ, D)
        wp = np.stack([np.asarray(o["wkvpT"]) for o in outs], axis=1)
        res_extra["wkv_prompt"] = np.ascontiguousarray(wp.transpose(0, 1, 2, 4, 3))
        ws = np.stack([np.asarray(o["wkvsT"]) for o in outs], axis=1)
        res_extra["wkv_sample"] = np.ascontiguousarray(ws.transpose(0, 1, 2, 3, 5, 4)).reshape(2, 8 * NSEQ, 16, 64, 64)
    DBG["extra"] = res_extra
    if "mem_k" not in douts:
        return y_prompt, y_sample
    mem_k = np.stack([np.asarray(o["mem_k"]) for o in outs], axis=1).reshape(DEPTH, 8, MEM, 4, 256)
    mem_v = np.stack([np.asarray(o["mem_v"]) for o in outs], axis=1).reshape(DEPTH, 8, MEM, 4, 256)
    if DBG.get("short"):
        return y_prompt, y_sample, mem_k, mem_v
    f32 = np.float32
    ex = res_extra
    conv_p = ex.get("conv_prompt", np.zeros((2, 8, 30, D), f32))
    conv_s = ex.get("conv_sample", np.zeros((2, 8 * NSEQ, 30, D), f32))
    shift_p = ex.get("shift_prompt", np.zeros((2, 8, D), f32))
    shift_s = ex.get("shift_sample", np.zeros((2, 8 * NSEQ, D), f32))
    wkv_p = ex.get("wkv_prompt", np.zeros((2, 8, 16, 64, 64), f32))
    wkv_s = ex.get("wkv_sample", np.zeros((2, 8 * NSEQ, 16, 64, 64), f32))
    return (y_prompt, y_sample, mem_k, mem_v, conv_p.astype(f32), shift_p.astype(f32), wkv_p.astype(f32),
            conv_s.astype(f32), shift_s.astype(f32), wkv_s.astype(f32))
```

```python
import contextlib
import numpy as np
import concourse.bass as bass
import concourse.mybir as mybir
from concourse.bass_utils import run_bass_kernel_spmd

F32 = mybir.dt.float32
BF16 = mybir.dt.bfloat16
ALU = mybir.AluOpType
AF = mybir.ActivationFunctionType
AX = mybir.AxisListType

D = 1024
NCH = 8
TP = 2048
NSEQ = 16
TS = 128
T = TP + TS
DEPTH = 4
MEM = 256
DFF = 4096
EPOCH = 16000
NDMASEM = 24

BLOCKS = [(0, 512), (512, 512), (1024, 512), (1536, 512), (2048, 128)]


class Rec:
    ENGS = ("pe", "act", "dve", "pool", "sp")

    def __init__(self, nc, stack):
        self.nc = nc
        self.stack = stack
        self.lists = {e: [] for e in self.ENGS}
        self.count = {e: 0 for e in self.ENGS}
        self.sems = {}
        self.seen = {e: {} for e in self.ENGS}
        self.last_w = {}
        self.readers = {}
        self.dma_slot = 0
        self.dma_val = [0] * NDMASEM
        for i in range(NDMASEM):
            self.sems[("dma", i)] = stack.enter_context(nc.semaphore(f"dma{i}"))
        self.nwait = 0

    def _sem(self, key):
        if key not in self.sems:
            self.sems[key] = self.stack.enter_context(self.nc.semaphore(f"s_{key[0]}_{key[1]}"))
        return self.sems[key]

    def _wait(self, eng, ev):
        key, val = ev
        if self.seen[eng].get(key, 0) >= val:
            return
        self.seen[eng][key] = val
        sem = self._sem(key)
        self.lists[eng].append(lambda e, sem=sem, val=val: e.wait_ge(sem, val))
        self.nwait += 1

    def _deps(self, eng, r, w):
        evs = []
        for res in r:
            ev = self.last_w.get(res)
            if ev is not None:
                evs.append(ev)
        for res in w:
            ev = self.last_w.get(res)
            if ev is not None:
                evs.append(ev)
            evs.extend(self.readers.get(res, ()))
        for ev in evs:
            self._wait(eng, ev)

    def _commit(self, ev, r, w):
        for res in r:
            self.readers.setdefault(res, []).append(ev)
        for res in w:
            self.last_w[res] = ev
            self.readers[res] = []

    def op(self, eng, fn, r=(), w=()):
        self._deps(eng, r, w)
        n = self.count[eng]
        key = (eng, n // EPOCH)
        val = n % EPOCH + 1
        sem = self._sem(key)
        self.count[eng] = n + 1
        self.lists[eng].append(lambda e, fn=fn, sem=sem: fn(e).then_inc(sem, 1))
        self._commit((key, val), r, w)

    def dma(self, eng, out, in_, r=(), w=(), **kw):
        self._deps(eng, r, w)
        s = self.dma_slot
        self.dma_slot = (s + 1) % NDMASEM
        key = ("dma", s)
        if self.dma_val[s] > 0:
            self._wait(eng, (key, self.dma_val[s]))
        self.dma_val[s] += 16
        val = self.dma_val[s]
        sem = self.sems[key]
        self.lists[eng].append(
            lambda e, out=out, in_=in_, sem=sem, kw=kw: e.dma_start(out=out, in_=in_, **kw).then_inc(sem, 16))
        self._commit((key, val), r, w)

    def barrier(self):
        evs = []
        for e in self.ENGS:
            n = self.count[e]
            if n > 0:
                evs.append(((e, (n - 1) // EPOCH), (n - 1) % EPOCH + 1))
        for s in range(NDMASEM):
            if self.dma_val[s] > 0:
                evs.append((("dma", s), self.dma_val[s]))
        for e in self.ENGS:
            for ev in evs:
                if ev[0][0] == e:
                    continue
                self._wait(e, ev)

    def finish(self):
        for s in range(NDMASEM):
            if self.dma_val[s] > 0:
                self._wait("sp", (("dma", s), self.dma_val[s]))
        for e in self.ENGS:
            n = self.count[e]
            if n > 0 and e != "sp":
                self._wait("sp", ((e, (n - 1) // EPOCH), (n - 1) % EPOCH + 1))
        nc = self.nc
        lists = self.lists
        with nc.Block() as block:
            @block.tensor
            def _(e):
                for f in lists["pe"]:
                    f(e)

            @block.scalar
            def _(e):
                for f in lists["act"]:
                    f(e)

            @block.vector
            def _(e):
                for f in lists["dve"]:
                    f(e)

            @block.gpsimd
            def _(e):
                for f in lists["pool"]:
                    f(e)

            @block.sync
            def _(e):
                for f in lists["sp"]:
                    f(e)


def vec_layout():
    names = []
    for l in range(DEPTH):
        for j in range(3):
            names.append(("norm_g", l, j))
    names.append(("final_g",))
    for ci in range(2):
        names += [("cv_b_in", ci, 0), ("cv_b_in", ci, 1)]
        for j in range(31):
            names.append(("cv_w_dw", ci, j))
        names += [("cv_b_dw", ci), ("cv_ln_g", ci), ("cv_ln_b", ci), ("cv_b_out", ci)]
    for ri in range(2):
        for j in range(6):
            names.append(("rw_mix", ri, j))
        names += [("rw_w0", ri), ("rw_a0", ri), ("rw_k_k", ri), ("rw_k_a", ri), ("rw_r_k", ri),
                  ("rw_lnx_g", ri), ("rw_lnx_b", ri)]
    names.append(("rw_v0", 0))
    return {n: i for i, n in enumerate(names)}


VIDX = vec_layout()
NVEC = len(VIDX)


def pack_vecs(inp):
    out = np.zeros((128, NVEC, 8), np.float32)

    def put(key, v):
        out[:, VIDX[key], :] = np.asarray(v, np.float32).reshape(8, 128).T

    for l in range(DEPTH):
        for j in range(3):
            put(("norm_g", l, j), inp["norm_g"][l, j])
    put(("final_g",), inp["final_g"])
    for ci in range(2):
        put(("cv_b_in", ci, 0), inp["cv_b_in"][ci, :D])
        put(("cv_b_in", ci, 1), inp["cv_b_in"][ci, D:])
        for j in range(31):
            put(("cv_w_dw", ci, j), inp["cv_w_dw"][ci, j])
        put(("cv_b_dw", ci), inp["cv_b_dw"][ci])
        put(("cv_ln_g", ci), inp["cv_ln_g"][ci])
        put(("cv_ln_b", ci), inp["cv_ln_b"][ci])
        put(("cv_b_out", ci), inp["cv_b_out"][ci])
    for ri in range(2):
        for j in range(6):
            put(("rw_mix", ri, j), inp["rw_mix"][ri, j])
        put(("rw_w0", ri), inp["rw_w0"][ri])
        put(("rw_a0", ri), inp["rw_a0"][ri])
        put(("rw_k_k", ri), inp["rw_k_k"][ri])
        put(("rw_k_a", ri), inp["rw_k_a"][ri])
        put(("rw_r_k", ri), inp["rw_r_k"][ri].reshape(-1))
        put(("rw_lnx_g", ri), inp["rw_lnx_g"][ri])
        put(("rw_lnx_b", ri), inp["rw_lnx_b"][ri])
    put(("rw_v0", 0), inp["rw_v0"][0])
    return out.reshape(128, NVEC * 8)


class Builder:
    def __init__(self, phases=("mlp",)):
        self.phases = phases
        self.nc = bass.Bass("TRN2", target_bir_lowering=False)
        self.stack = contextlib.ExitStack()
        self.R = Rec(self.nc, self.stack)
        self.dins = {}
        self.douts = {}

    def dram_in(self, name, shape):
        if name not in self.dins:
            self.dins[name] = self.nc.dram_tensor(name, list(shape), F32, kind="ExternalInput").ap()
        return self.dins[name]

    def dram_out(self, name, shape):
        if name not in self.douts:
            self.douts[name] = self.nc.dram_tensor(name, list(shape), F32, kind="ExternalOutput").ap()
        return self.douts[name]

    def sb(self, name, shape, dtype):
        self.uid = getattr(self, "uid", 0) + 1
        return self.stack.enter_context(self.nc.sbuf_tensor(f"{name}_{self.uid}", list(shape), dtype))

    def ps(self, name, shape, dtype=F32):
        self.uid = getattr(self, "uid", 0) + 1
        return self.stack.enter_context(self.nc.psum_tensor(f"{name}_{self.uid}", list(shape), dtype))

    w_up_d = property(lambda self: self.dram_in("mlp_w_up", [DEPTH, D, DFF]))
    w_dn_d = property(lambda self: self.dram_in("mlp_w_down", [DEPTH, DFF, D]))
    memT_d = property(lambda self: self.dram_in("memT", [D, MEM]))
    wkv_d = property(lambda self: self.dram_in("xa_w_kv", [DEPTH, D, 2 * D]))
    wq_d = property(lambda self: self.dram_in("xa_w_q", [DEPTH, D, D]))
    wo_d = property(lambda self: self.dram_in("xa_w_o", [DEPTH, D, D]))
    cKT_d = property(lambda self: self.dram_in("cKT", [DEPTH, NSEQ, 4, 256, MEM]))
    cV_d = property(lambda self: self.dram_in("cV", [DEPTH, NSEQ, MEM, D]))
    memk_d = property(lambda self: self.dram_out("mem_k", [DEPTH, MEM, D]))
    memv_d = property(lambda self: self.dram_out("mem_v", [DEPTH, MEM, D]))

    def vec(self, key):
        i = VIDX[key]
        return self.vecs[:, i * 8:(i + 1) * 8]

    def build(self):
        nc, R = self.nc, self.R
        self.xT_d = self.dram_in("xT", [D, T])
        self.vecs_d = self.dram_in("vecs", [128, NVEC * 8])
        self.yT_d = self.dram_out("yT", [D, T])

        self.x = self.sb("x", [128, NCH, T], F32)
        self.vecs = self.sb("vecs_sb", [128, NVEC * 8], F32)
        self.ones_m = self.sb("ones_m", [128, 128], BF16)
        self.psb = None
        self.ones_1 = self.sb("ones_1", [128, 128], BF16)
        R.op("pool", lambda e: e.memset(self.ones_1[:], 1.0), w=["ones_1"])
        if "attn" in self.phases:
            self.memT = self.sb("memT_sb", [128, NCH, MEM], BF16)
            R.dma("pool", self.memT[:], self.memT_d.rearrange("(c p) m -> p c m", p=128), w=["memT"])

        R.op("pool", lambda e: e.memset(self.ones_m[:], 1.0 / D), w=["ones_m"])
        self.eps_rms = self.sb("eps_rms", [128, 1], F32)
        R.op("pool", lambda e: e.memset(self.eps_rms[:], 1e-6), w=["consts"])
        self.eps_lnx = self.sb("eps_lnx", [128, 1], F32)
        R.op("pool", lambda e: e.memset(self.eps_lnx[:], 64e-5), w=["consts"])
        if "rwkv" in self.phases:
            cd = self.dram_in("rconst", [128, 4 * 8 * 64 + 128 + 128])
            self.cmask = self.sb("cmask", [128, 4, 8, 64], BF16)
            self.identf = self.sb("identf", [128, 128], F32)
            self.identb = self.sb("identb", [128, 128], BF16)
            self.blk64 = self.sb("blk64", [128, 128], BF16)
            R.dma("pool", self.cmask[:].rearrange("p a b c -> p (a b c)"), cd[:, 0:2048], w=["consts"])
            R.dma("sp", self.identf[:], cd[:, 2048:2176], w=["consts"])
            R.dma("pool", self.identb[:], cd[:, 2048:2176], w=["consts"])
            R.dma("pool", self.blk64[:], cd[:, 2176:2304], w=["consts"])
            self.xspill = self.nc.dram_tensor("xspill", [D, T], F32, kind="Internal").ap()
            self.vfirst_d = self.nc.dram_tensor("vfirst", [D, T], F32, kind="Internal").ap()
        self.eps_ln = self.sb("eps_ln", [128, 1], F32)
        R.op("pool", lambda e: e.memset(self.eps_ln[:], 1e-5), w=["consts"])
        R.dma("sp", self.vecs[:], self.vecs_d[:], w=["vecs"])
        xv = self.xT_d.rearrange("(c p) t -> p c t", p=128)
        for c in range(NCH):
            R.dma("sp", self.x[:, c, :], xv[:, c, :], w=[("x", c, b) for b in range(len(BLOCKS))])

        with contextlib.ExitStack() as st:
            old, self.stack = self.stack, st
            for l in range(NLAYERS):
                if "conv" in self.phases and l % 2 == 0:
                    self.phase_conv(l)
                if "rwkv" in self.phases and l % 2 == 1:
                    self.phase_rwkv(l)
                if "attn" in self.phases:
                    self.phase_attn(l)
                if "mlp" in self.phases:
                    self.phase_mlp(l)
            self.final_norm()
            R.barrier()
            self.stack = old
        R.finish()
        return nc

    def rms_rstd(self, blk, rstd, tagps=7):
        R = self.R
        t0, n = BLOCKS[blk]
        sq = self.sq_scr
        ps = self.psb[tagps]
        R.op("act", lambda e: e.activation(out=sq[:, :, :n], in_=self.x[:, :, t0:t0 + n], func=AF.Square),
             r=[("x", c, blk) for c in range(NCH)], w=["sq_scr"])

        def mm(e):
            ins = None
            for c in range(NCH):
                ins = e.matmul(ps[:, :n], self.ones_m[:], sq[:, c, :n], start=(c == 0), stop=(c == NCH - 1))
            return ins
        R.op("pe", mm, r=["sq_scr", "ones_m"], w=[("ps", tagps)])
        R.op("act", lambda e: e.activation(out=rstd[:, :n], in_=ps[:, :n], func=AF.Sqrt, bias=self.eps_rms[:, 0:1]),
             r=[("ps", tagps), "consts"], w=["rstd"])
        R.op("dve", lambda e: e.reciprocal(out=rstd[:, :n], in_=rstd[:, :n]), r=["rstd"], w=["rstd"])

    def norm_block(self, blk, gkey, out_tile, out_res, out_off=None):
        R = self.R
        t0, n = BLOCKS[blk]
        off = t0 if out_off is None else out_off
        self.rms_rstd(blk, self.rstd)
        g = self.vec(gkey)

        def f(e):
            ins = None
            for c in range(NCH):
                ins = e.scalar_tensor_tensor(out=out_tile[:, c, off:off + n], in0=self.x[:, c, t0:t0 + n],
                                             scalar=g[:, c:c + 1], in1=self.rstd[:, :n],
                                             op0=ALU.mult, op1=ALU.mult)
            return ins
        R.op("dve", f, r=[("x", c, blk) for c in range(NCH)] + ["rstd", "vecs"], w=out_res)


    def phase_conv(self, l):
        R = self.R
        ci = l // 2
        with contextlib.ExitStack() as st:
            old, self.stack = self.stack, st
            self.sq_scr = self.sb("sq_scr", [128, NCH, 512], BF16)
            self.rstd = self.sb("rstd", [128, 512], F32)
            self.psb = [self.ps(f"psb{i}", [128, 512], F32) for i in range(8)]
            win = self.sb("win", [128, NCH, 2 * D], BF16)
            wout = self.sb("wout", [128, NCH, D], BF16)
            hbk = self.sb("hbk", [128, NCH, 512], BF16)
            up = self.sb("up", [128, NCH, NSEQ * 38], F32)
            ups = up[:].rearrange("p c (b t) -> p c b t", t=38)
            gate = [self.sb(f"gate{i}", [128, 512], F32) for i in range(2)]
            z = self.sb("z", [128, NCH, 512], F32)
            zb = self.sb("zb", [128, NCH, 512], BF16)
            mean = self.sb("mean", [128, 512], F32)
            var = self.sb("var", [128, 512], F32)
            mr = self.sb("mr", [128, 512], F32)
            tmp = [self.sb(f"tmpc{i}", [128, 512], F32) for i in range(2)]
            psb = self.psb
            win_v = self.dram_in("cv_w_in", [2, D, 2 * D])[ci].rearrange("(c p) e -> p c e", p=128)
            wout_v = self.dram_in("cv_w_out", [2, D, D])[ci].rearrange("(c p) e -> p c e", p=128)
            for j in range(4):
                R.dma("pool", win[:, :, j * 512:(j + 1) * 512], win_v[:, :, j * 512:(j + 1) * 512], w=["win"])
            for j in range(2):
                R.dma("pool", wout[:, :, j * 512:(j + 1) * 512], wout_v[:, :, j * 512:(j + 1) * 512], w=["wout"])
            sconv_d = self.dram_in("sconvT", [2, D, NSEQ, 30])
            convp_d = self.dram_out("convpT", [2, D, 30])
            convs_d = self.dram_out("convsT", [2, D, NSEQ, 30])
            R.op("pool", lambda e: e.memset(up[:, :, 0:30], 0.0), w=["up"])
            b1 = self.vec(("cv_b_in", ci, 0))
            b2 = self.vec(("cv_b_in", ci, 1))
            bdw = self.vec(("cv_b_dw", ci))
            lng = self.vec(("cv_ln_g", ci))
            lnb = self.vec(("cv_ln_b", ci))
            bout = self.vec(("cv_b_out", ci))
            wdw = [self.vec(("cv_w_dw", ci, j)) for j in range(31)]
            ng = 0
            for blk, (t0, n) in enumerate(BLOCKS):
                samp = blk == 4
                if samp:
                    for c in range(NCH):
                        R.dma("sp", ups[:, c, :, 0:30], sconv_d[ci, c * 128:(c + 1) * 128], r=["up"], w=["up"])
                self.norm_block(blk, ("norm_g", l, 0), hbk, ["hbk"], out_off=0)
                if 0 < blk < 4:
                    R.op("pool", lambda e: e.tensor_copy(out=up[:, :, 0:30], in_=up[:, :, 512:542]), r=["up"], w=["up"])
                for ec in range(NCH):
                    psA, psG = psb[(ec % 2) * 2], psb[(ec % 2) * 2 + 1]
                    pa, pg = (ec % 2) * 2, (ec % 2) * 2 + 1

                    def mm(e, psA=psA, psG=psG, ec=ec, n=n):
                        ins = None
                        for c in range(NCH):
                            ins = e.matmul(psA[:, :n], win[:, c, ec * 128:(ec + 1) * 128], hbk[:, c, :n],
                                           start=(c == 0), stop=(c == NCH - 1))
                        for c in range(NCH):
                            ins = e.matmul(psG[:, :n], win[:, c, D + ec * 128:D + (ec + 1) * 128], hbk[:, c, :n],
                                           start=(c == 0), stop=(c == NCH - 1))
                        return ins
                    R.op("pe", mm, r=["win", "hbk"], w=[("ps", pa), ("ps", pg)])
                    gb = ng % 2
                    ng += 1
                    R.op("act", lambda e, psG=psG, gb=gb, ec=ec, n=n: e.activation(
                        out=gate[gb][:, :n], in_=psG[:, :n], func=AF.Sigmoid, bias=b2[:, ec:ec + 1]),
                        r=[("ps", pg), "vecs"], w=[("gate", gb)])
                    if not samp:
                        R.op("dve", lambda e, psA=psA, gb=gb, ec=ec, n=n: e.scalar_tensor_tensor(
                            out=up[:, ec, 30:30 + n], in0=psA[:, :n], scalar=b1[:, ec:ec + 1], in1=gate[gb][:, :n],
                            op0=ALU.add, op1=ALU.mult), r=[("ps", pa), ("gate", gb), "vecs"], w=["up"])
                    else:
                        R.op("dve", lambda e, psA=psA, gb=gb, ec=ec, n=n: e.scalar_tensor_tensor(
                            out=ups[:, ec, :, 30:38], in0=psA[:, :n].rearrange("p (b t) -> p b t", t=8),
                            scalar=b1[:, ec:ec + 1], in1=gate[gb][:, :n].rearrange("p (b t) -> p b t", t=8),
                            op0=ALU.add, op1=ALU.mult), r=[("ps", pa), ("gate", gb), "vecs"], w=["up"])
                for c in range(NCH):
                    def src(j, c=c, n=n, samp=samp):
                        if samp:
                            return ups[:, c, :, j:j + 8]
                        return up[:, c, j:j + n]

                    def dst(tile, n=n, samp=samp):
                        if samp:
                            return tile[:, :n].rearrange("p (b t) -> p b t", t=8)
                        return tile[:, :n]

                    za = dst(z[:, c, :])
                    R.op("dve", lambda e, c=c, src=src, za=za: e.tensor_scalar(
                        out=za, in0=src(0), scalar1=wdw[0][:, c:c + 1], scalar2=bdw[:, c:c + 1], op0=ALU.mult, op1=ALU.add),
                        r=["up", "vecs"], w=[("z", c)])
                    for j in range(1, 31):
                        R.op("dve", lambda e, c=c, j=j, src=src, za=za: e.scalar_tensor_tensor(
                            out=za, in0=src(j), scalar=wdw[j][:, c:c + 1], in1=za, op0=ALU.mult, op1=ALU.add),
                            r=["up", "vecs", ("z", c)], w=[("z", c)])
                    R.op("pool", lambda e, c=c, n=n: e.tensor_copy(out=zb[:, c, :n], in_=z[:, c, :n]), r=[("z", c)], w=[("zb", c)])
                    R.op("act", lambda e, c=c, n=n: e.activation(out=self.sq_scr[:, c, :n], in_=z[:, c, :n], func=AF.Square),
                         r=[("z", c)], w=["sq_scr"])
                if samp and DBG.get("dump"):
                    dd = self.dram_out("dbg_z", [128, NCH, 128])
                    R.dma("sp", dd[:], z[:, :, :128], r=[("z", c) for c in range(NCH)])
                    dd2 = self.dram_out("dbg_up", [128, NCH, NSEQ * 38])
                    R.dma("sp", dd2[:], up[:], r=["up"])
                def mmst(e, n=n):
                    ins = None
                    for c in range(NCH):
                        ins = e.matmul(psb[4][:, :n], self.ones_m[:], zb[:, c, :n], start=(c == 0), stop=(c == NCH - 1))
                    for c in range(NCH):
                        ins = e.matmul(psb[5][:, :n], self.ones_m[:], self.sq_scr[:, c, :n], start=(c == 0), stop=(c == NCH - 1))
                    return ins
                R.op("pe", mmst, r=["ones_m", "sq_scr"] + [("zb", c) for c in range(NCH)], w=[("ps", 4), ("ps", 5)])
                R.op("dve", lambda e, n=n: e.tensor_copy(out=mean[:, :n], in_=psb[4][:, :n]), r=[("ps", 4)], w=["mean"])
                R.op("dve", lambda e, n=n: e.tensor_tensor(out=var[:, :n], in0=mean[:, :n], in1=mean[:, :n], op=ALU.mult),
                     r=["mean"], w=["var"])
                R.op("dve", lambda e, n=n: e.tensor_tensor(out=var[:, :n], in0=psb[5][:, :n], in1=var[:, :n], op=ALU.subtract),
                     r=[("ps", 5), "var"], w=["var"])
                R.op("act", lambda e, n=n: e.activation(out=var[:, :n], in_=var[:, :n], func=AF.Sqrt, bias=self.eps_ln[:, 0:1]),
                     r=["var", "consts"], w=["var"])
                R.op("dve", lambda e, n=n: e.reciprocal(out=var[:, :n], in_=var[:, :n]), r=["var"], w=["var"])
                R.op("dve", lambda e, n=n: e.tensor_tensor(out=mr[:, :n], in0=mean[:, :n], in1=var[:, :n], op=ALU.mult),
                     r=["mean", "var"], w=["mr"])
                for c in range(NCH):
                    tb = c % 2
                    R.op("dve", lambda e, c=c, n=n, tb=tb: e.tensor_tensor(out=tmp[tb][:, :n], in0=z[:, c, :n], in1=var[:, :n],
                                                                           op=ALU.mult), r=[("z", c), "var"], w=[("tmpc", tb)])
                    R.op("dve", lambda e, c=c, n=n, tb=tb: e.tensor_tensor(out=tmp[tb][:, :n], in0=tmp[tb][:, :n], in1=mr[:, :n],
                                                                           op=ALU.subtract), r=[("tmpc", tb), "mr"], w=[("tmpc", tb)])
                    R.op("act", lambda e, c=c, n=n, tb=tb: e.activation(out=hbk[:, c, :n], in_=tmp[tb][:, :n], func=AF.Silu,
                                                                        bias=lnb[:, c:c + 1], scale=lng[:, c:c + 1]),
                         r=[("tmpc", tb), "vecs"], w=["hbk"])
                for ec in range(NCH):
                    pi = ec % 2
                    ps = psb[pi]

                    def mm(e, ps=ps, ec=ec, n=n):
                        ins = None
                        for c in range(NCH):
                            ins = e.matmul(ps[:, :n], wout[:, c, ec * 128:(ec + 1) * 128], hbk[:, c, :n],
                                           start=(c == 0), stop=(c == NCH - 1))
                        return ins
                    R.op("pe", mm, r=["wout", "hbk"], w=[("ps", pi)])
                    R.op("dve", lambda e, ps=ps, ec=ec, t0=t0, n=n: e.scalar_tensor_tensor(
                        out=self.x[:, ec, t0:t0 + n], in0=ps[:, :n], scalar=bout[:, ec:ec + 1], in1=self.x[:, ec, t0:t0 + n],
                        op0=ALU.add, op1=ALU.add), r=[("ps", pi), "vecs"], w=[("x", ec, blk)])
                if blk == 3:
                    for c in range(NCH):
                        R.dma("sp", convp_d[ci, c * 128:(c + 1) * 128, :], up[:, c, 512:542], r=["up"])
                if samp:
                    for c in range(NCH):
                        R.dma("sp", convs_d[ci, c * 128:(c + 1) * 128], ups[:, c, :, 8:38], r=["up"])
            R.barrier()
            self.stack = old


    def phase_rwkv(self, l):
        R = self.R
        nc = self.nc
        ri = l // 2
        NB = 128
        xs_d = self.xspill
        xs_v = xs_d.rearrange("(c p) t -> p c t", p=128)
        allx = [("x", c, b) for c in range(NCH) for b in range(len(BLOCKS))]
        for c in range(NCH):
            R.dma("sp", xs_v[:, c, :], self.x[:, c, :], r=allx)
        R.barrier()
        with contextlib.ExitStack() as st:
            old, self.stack = self.stack, st
            self.psb_save = self.psb
            def slot(j):
                return self.x[:, j // 2, (j % 2) * 1024:(j % 2) * 1024 + 1024]
            def bslot(j):
                return slot(j).rearrange("p (a b) -> p a b", b=NB)
            names = ["xb", "hf", "xx", "rf", "kf", "vf", "lw", "cs1", "cs2", "eNi", "af", "kkn", "yf"]
            Fb = {nm: bslot(i) for i, nm in enumerate(names)}
            SAV = slot(13)[:, 0:512].rearrange("p (c v) -> p c v", v=64)
            ytm = slot(14)[:, 0:512].rearrange("p (c v) -> p c v", v=64)
            yc = slot(15)[:, 0:512].rearrange("p (c v) -> p c v", v=64)
            bf = lambda nm, shape: self.sb(nm, shape, BF16)
            xm = [bf(f"xm{i}", [128, NCH, NB]) for i in range(3)]
            At, Bt, Kt, Rt, Vb, BWb, KWb, gfb, sqb = [bf(nm, [128, NCH, NB]) for nm in
                                                      ("At", "Bt", "Kt", "Rt", "Vb", "BWb", "KWb", "gfb", "sqb")]
            yg = xm[0]
            tw = bf("tw", [64, NB]); ta = bf("ta", [64, NB]); tv = bf("tv", [32, NB]); tg = bf("tg", [128, 2, NB])
            Pm, Qm, Tm, MKA, MBR, MKR, AtT, VT, BWT, KWT, XV, SAb = [
                bf(nm, [128, NCH, 64]) for nm in ("Pm", "Qm", "Tm", "MKA", "MBR", "MKR", "AtT", "VT", "BWT", "KWT", "XV", "SAb")]
            Ah = bf("Ah", [128, NCH, 64])
            Sf = self.sb("Sf", [128, NCH, 64], F32)
            Sb = bf("Sb", [128, NCH, 64])
            WC = self.sb("WC", [128, NCH, 16], F32)
            hlast = self.sb("hlast", [128, NCH, 16], F32)
            st1 = self.sb("st1", [128, NCH], F32)
            st2 = self.sb("st2", [128, NCH], F32)
            rnb = self.sb("rnb", [128, NB], F32)
            rnb4 = self.sb("rnb4", [128, 4 * NB], F32)
            wr, wk, wv, wo = [bf(nm, [128, NCH, D]) for nm in ("wr", "wk", "wv", "wo")]
            w1 = bf("w1", [128, NCH, 64]); w2 = bf("w2", [64, D])
            a1 = bf("a1", [128, NCH, 64]); a2 = bf("a2", [64, D])
            g1 = bf("g1", [128, NCH, 160]); g2 = bf("g2", [128, 2, D])
            if ri > 0:
                v1 = bf("v1", [128, NCH, 32]); v2 = bf("v2", [32, D])
            ps = [self.ps(f"rps{i}", [128, 512], F32) for i in range(8)]
            cm = self.cmask
            mSU, mSL, mU, mI = (cm[:, i] for i in range(4))

            def wload(dst, name, shape, view, nsplit=1):
                src = self.dram_in(name, shape)[ri if name not in ("rw_v1", "rw_v2") else 0]
                src = src.rearrange(view, p=128) if view else src
                if nsplit == 1:
                    R.dma("pool", dst, src, w=[name])
                else:
                    for j in range(nsplit):
                        R.dma("pool", dst[:, :, j * 512:(j + 1) * 512], src[:, :, j * 512:(j + 1) * 512], w=[name])
            for t_, nm in ((wr, "rw_w_r"), (wk, "rw_w_k"), (wv, "rw_w_v"), (wo, "rw_w_o")):
                wload(t_[:], nm, [2, D, D], "(c p) e -> p c e", 2)
            wload(w1[:], "rw_w1", [2, D, 64], "(c p) e -> p c e")
            wload(w2[:], "rw_w2", [2, 64, D], None)
            wload(a1[:], "rw_a1", [2, D, 64], "(c p) e -> p c e")
            wload(a2[:], "rw_a2", [2, 64, D], None)
            wload(g1[:], "rw_g1", [2, D, 160], "(c p) e -> p c e")
            g2d = self.dram_in("rw_g2", [2, 160, D])[ri]
            R.dma("pool", g2[:, 0, :], g2d[0:128, :], w=["rw_g2"])
            R.dma("pool", g2[0:32, 1, :], g2d[128:160, :], w=["rw_g2"])
            if ri > 0:
                wload(v1[:], "rw_v1", [1, D, 32], "(c p) e -> p c e")
                wload(v2[:], "rw_v2", [1, 32, D], None)
            WALL = ["rw_w_r", "rw_w_k", "rw_w_v", "rw_w_o", "rw_w1", "rw_w2", "rw_a1", "rw_a2", "rw_g1", "rw_g2",
                    "rw_v1", "rw_v2"]
            sshift_d = self.dram_in("sshiftT", [2, D, NSEQ])
            swkv_d = self.dram_in("swkvT", [2, NSEQ, 16, 64, 64])
            shiftp_d = self.dram_out("shiftpT", [2, 128, NCH])
            shifts_d = self.dram_out("shiftsT", [2, 128, NCH, NSEQ])
            wkvp_d = self.dram_out("wkvpT", [2, 16, 64, 64])
            wkvs_d = self.dram_out("wkvsT", [2, NSEQ, 16, 64, 64])
            vfd = self.vfirst_d.rearrange("(c p) t -> p c t", p=128)

            def V(key):
                return self.vec(key)[:, :].unsqueeze(2).to_broadcast([128, NCH, NB])

            def dve(fn, r, w):
                R.op("dve", fn, r=list(r) + ["vecs", "consts"], w=w)

            def act(fn, r, w):
                R.op("act", fn, r=list(r) + ["vecs", "consts"], w=w)

            npj = [0]

            def proj(W, wname, src, srcres, cols, evac):
                for g in range(cols // 512):
                    pi = 4 + npj[0] % 4
                    npj[0] += 1
                    p_ = ps[pi]

                    def mm(e, p_=p_, g=g):
                        ins = None
                        for j in range(4):
                            ec = g * 4 + j
                            for c in range(NCH):
                                ins = e.matmul(p_[:, j * NB:(j + 1) * NB], W[:, c, ec * 128:(ec + 1) * 128], src[:, c, :],
                                               start=(c == 0), stop=(c == NCH - 1))
                        return ins
                    R.op("pe", mm, r=[wname, srcres], w=[("rps", pi)])
                    evac(p_[:, :].rearrange("p (j t) -> p j t", t=NB), pi, g)

            R.op("pool", lambda e: e.memset(Sf[:], 0.0), w=["Sf"])
            R.op("pool", lambda e: e.memset(Sb[:], 0.0), w=["Sb"])
            R.op("pool", lambda e: e.memset(hlast[:], 0.0), w=["hlast"])

            nblk = 17

            def do_block(blk):
                samp = blk == 16
                t0 = blk * NB
                C = 8 if samp else 64
                NCK = NB // C
                LV = 2 if samp else 5
                xb, hf, xx = Fb["xb"], Fb["hf"], Fb["xx"]
                rf, kf, vf, lw, cs1, cs2 = Fb["rf"], Fb["kf"], Fb["vf"], Fb["lw"], Fb["cs1"], Fb["cs2"]
                eNi, af, kkn, yf = Fb["eNi"], Fb["af"], Fb["kkn"], Fb["yf"]
                R.dma("sp", xb, xs_v[:, :, t0:t0 + NB], w=["xb"])
                act(lambda e: e.activation(out=sqb[:], in_=xb, func=AF.Square), ["xb"], ["sqb"])

                def mmn(e):
                    ins = None
                    for c in range(NCH):
                        ins = e.matmul(ps[6][:, :NB], self.ones_m[:], sqb[:, c, :], start=(c == 0), stop=(c == NCH - 1))
                    return ins
                R.op("pe", mmn, r=["sqb", "ones_m"], w=[("rps", 6)])
                act(lambda e: e.activation(out=rnb[:], in_=ps[6][:, :NB], func=AF.Sqrt, bias=self.eps_rms[:, 0:1]),
                    [("rps", 6)], ["rnb"])
                dve(lambda e: e.reciprocal(out=rnb[:], in_=rnb[:]), ["rnb"], ["rnb"])
                dve(lambda e: e.tensor_tensor(out=hf, in0=xb, in1=V(("norm_g", l, 0)), op=ALU.mult), ["xb"], ["hf"])
                dve(lambda e: e.tensor_tensor(out=hf, in0=hf, in1=rnb[:, :].unsqueeze(1).to_broadcast([128, NCH, NB]),
                                              op=ALU.mult), ["hf", "rnb"], ["hf"])
                if samp:
                    for c in range(NCH):
                        R.dma("sp", hlast[:, c, :], sshift_d[ri, c * 128:(c + 1) * 128, :], w=["hlast"])
                    h4 = hf.rearrange("p c (b t) -> p c b t", t=8)
                    x4 = xx.rearrange("p c (b t) -> p c b t", t=8)
                    for c in range(NCH):
                        dve(lambda e, c=c: e.tensor_tensor(out=x4[:, c, :, 1:8], in0=h4[:, c, :, 0:7], in1=h4[:, c, :, 1:8],
                                                           op=ALU.subtract), ["hf"], ["xx"])
                        dve(lambda e, c=c: e.tensor_tensor(out=x4[:, c, :, 0], in0=hlast[:, c, :], in1=h4[:, c, :, 0],
                                                           op=ALU.subtract), ["hf", "hlast"], ["xx"])
                    for c in range(NCH):
                        R.dma("sp", shifts_d[ri, :, c, :], h4[:, c, :, 7], r=["hf"], allow_slow_non_contiguous=True)
                else:
                    dve(lambda e: e.tensor_tensor(out=xx[:, :, 1:NB], in0=hf[:, :, 0:NB - 1], in1=hf[:, :, 1:NB],
                                                  op=ALU.subtract), ["hf"], ["xx"])
                    dve(lambda e: e.tensor_tensor(out=xx[:, :, 0], in0=hlast[:, :, 0], in1=hf[:, :, 0], op=ALU.subtract),
                        ["hf", "hlast"], ["xx"])
                    dve(lambda e: e.tensor_copy(out=hlast[:, :, 0], in_=hf[:, :, NB - 1]), ["hf", "xx"], ["hlast"])
                    if blk == 15:
                        R.dma("sp", shiftp_d[ri], hlast[:, :, 0], r=["hlast"], allow_slow_non_contiguous=True)
                nmx = [0]

                def pool(fn, r, w):
                    R.op("pool", fn, r=list(r) + ["vecs", "consts"], w=w)

                def mix(i):
                    b = nmx[0] % 3
                    nmx[0] += 1
                    pool(lambda e: e.tensor_tensor(out=xm[b][:], in0=xx, in1=V(("rw_mix", ri, i)), op=ALU.mult),
                         ["xx"], [("xm", b)])
                    pool(lambda e: e.tensor_tensor(out=xm[b][:], in0=xm[b][:], in1=hf, op=ALU.add), ["hf", ("xm", b)], [("xm", b)])
                    return xm[b], ("xm", b)

                def lora1(W, wname, src, srcres, rank, dst, dstres, func):
                    pi = 4 + npj[0] % 4
                    npj[0] += 1
                    p_ = ps[pi]

                    def mm(e):
                        ins = None
                        for c in range(NCH):
                            ins = e.matmul(p_[:rank, :NB], W[:, c, :rank], src[:, c, :], start=(c == 0), stop=(c == NCH - 1))
                        return ins
                    R.op("pe", mm, r=[wname, srcres], w=[("rps", pi)])
                    if func is None:
                        dve(lambda e: e.tensor_copy(out=dst[:rank, :], in_=p_[:rank, :NB]), [("rps", pi)], [dstres])
                    else:
                        act(lambda e: e.activation(out=dst[:rank, :], in_=p_[:rank, :NB], func=func), [("rps", pi)], [dstres])

                def lora2(W2, wname, mid, midres, rank, evac):
                    for ec in range(NCH):
                        pi = 4 + npj[0] % 4
                        npj[0] += 1
                        p_ = ps[pi]
                        R.op("pe", lambda e, p_=p_, ec=ec: e.matmul(p_[:, :NB], W2[:rank, ec * 128:(ec + 1) * 128], mid[:rank, :],
                                                                    start=True, stop=True),
                             r=[wname, midres], w=[("rps", pi)])
                        evac(p_, pi, ec)

                xw, xwr = mix(1)
                xa, xar = mix(4)
                xk, xkr = mix(2)
                lora1(w1, "rw_w1", xw, xwr, 64, tw, "tw", AF.Tanh)
                w0v = self.vec(("rw_w0", ri))
                lora2(w2, "rw_w2", tw, "tw", 64, lambda p_, pi, ec: act(
                    lambda e: e.activation(out=lw[:, ec, :], in_=p_[:, :NB], func=AF.Sigmoid, bias=w0v[:, ec:ec + 1]),
                    [("rps", pi)], ["lw"]))
                pool(lambda e: e.tensor_scalar(out=lw, in0=lw, scalar1=-0.6065306597126334, scalar2=None, op0=ALU.mult),
                     ["lw"], ["lw"])
                lw4 = lw.rearrange("p c (k t) -> p c k t", t=C)
                a4 = cs1.rearrange("p c (k t) -> p c k t", t=C)
                b4 = cs2.rearrange("p c (k t) -> p c k t", t=C)
                src4, srcn = lw4, "lw"
                dsts = [(a4, "cs1"), (b4, "cs2")]
                sh = 1
                k_ = 0
                while sh < C:
                    d4, dn = dsts[k_ % 2]
                    pool(lambda e, d4=d4, src4=src4, sh=sh: e.tensor_tensor(
                        out=d4[:, :, :, sh:C], in0=src4[:, :, :, sh:C], in1=src4[:, :, :, 0:C - sh], op=ALU.add),
                        [srcn], [dn])
                    pool(lambda e, d4=d4, src4=src4, sh=sh: e.tensor_copy(out=d4[:, :, :, 0:sh], in_=src4[:, :, :, 0:sh]),
                        [srcn], [dn])
                    src4, srcn = d4, dn
                    sh *= 2
                    k_ += 1
                Li4, Lin = src4, srcn
                Li = cs1 if Lin == "cs1" else cs2
                Le, Len = (cs2, "cs2") if Lin == "cs1" else (cs1, "cs1")
                pool(lambda e: e.tensor_tensor(out=Le, in0=Li, in1=lw, op=ALU.subtract), [Lin, "lw"], [Len])
                lora1(a1, "rw_a1", xa, xar, 64, ta, "ta", None)
                a0v = self.vec(("rw_a0", ri))
                lora2(a2, "rw_a2", ta, "ta", 64, lambda p_, pi, ec: act(
                    lambda e: e.activation(out=af[:, ec, :], in_=p_[:, :NB], func=AF.Sigmoid, bias=a0v[:, ec:ec + 1]),
                    [("rps", pi)], ["af"]))
                proj(wk, "rw_w_k", xk, xkr, D, lambda p4, pi, g: dve(
                    lambda e: e.tensor_copy(out=kf[:, g * 4:(g + 1) * 4, :], in_=p4), [("rps", pi)], ["kf"]))
                dve(lambda e: e.tensor_tensor(out=kkn, in0=kf, in1=V(("rw_k_k", ri)), op=ALU.mult), ["kf"], ["kkn"])
                act(lambda e: e.activation(out=sqb[:], in_=kkn, func=AF.Square), ["kkn"], ["sqb"])
                for hf_ in range(2):
                    pi = 4 + npj[0] % 4
                    npj[0] += 1
                    p_ = ps[pi]

                    def mmk(e, p_=p_, hf_=hf_):
                        ins = None
                        for j in range(4):
                            ins = e.matmul(p_[:, j * NB:(j + 1) * NB], self.blk64[:], sqb[:, hf_ * 4 + j, :], start=True, stop=True)
                        return ins
                    R.op("pe", mmk, r=["sqb", "consts"], w=[("rps", pi)])
                    act(lambda e, p_=p_: e.activation(out=rnb4[:], in_=p_[:, :], func=AF.Sqrt), [("rps", pi)], ["rnb4"])
                    dve(lambda e: e.tensor_scalar(out=rnb4[:], in0=rnb4[:], scalar1=1e-12, scalar2=None, op0=ALU.max),
                        ["rnb4"], ["rnb4"])
                    dve(lambda e: e.reciprocal(out=rnb4[:], in_=rnb4[:]), ["rnb4"], ["rnb4"])
                    dve(lambda e, hf_=hf_: e.tensor_tensor(out=kkn[:, hf_ * 4:(hf_ + 1) * 4, :], in0=kkn[:, hf_ * 4:(hf_ + 1) * 4, :],
                                                           in1=rnb4[:, :].rearrange("p (j t) -> p j t", t=NB), op=ALU.mult),
                        ["kkn", "rnb4"], ["kkn"])
                xv, xvr = mix(3)
                xg, xgr = mix(5)
                xr, xrr = mix(0)
                proj(wv, "rw_w_v", xv, xvr, D, lambda p4, pi, g: dve(
                    lambda e: e.tensor_copy(out=vf[:, g * 4:(g + 1) * 4, :], in_=p4), [("rps", pi)], ["vf"]))
                if ri == 0:
                    R.dma("sp", vfd[:, :, t0:t0 + NB], vf, r=["vf"])
                else:
                    lora1(v1, "rw_v1", xv, xvr, 32, tv, "tv", None)
                    v0v = self.vec(("rw_v0", 0))
                    vg, vgn = xx, "xx"
                    tmpV, tmpVn = hf, "hf"
                    lora2(v2, "rw_v2", tv, "tv", 32, lambda p_, pi, ec: act(
                        lambda e: e.activation(out=vg[:, ec, :], in_=p_[:, :NB], func=AF.Sigmoid, bias=v0v[:, ec:ec + 1]),
                        [("rps", pi)], [vgn]))
                    R.dma("sp", tmpV, vfd[:, :, t0:t0 + NB], w=[tmpVn])
                    dve(lambda e: e.tensor_tensor(out=tmpV, in0=tmpV, in1=vf, op=ALU.subtract), [tmpVn, "vf"], [tmpVn])
                    dve(lambda e: e.tensor_tensor(out=tmpV, in0=tmpV, in1=vg, op=ALU.mult), [tmpVn, vgn], [tmpVn])
                    dve(lambda e: e.tensor_tensor(out=vf, in0=vf, in1=tmpV, op=ALU.add), [tmpVn, "vf"], ["vf"])
                dve(lambda e: e.tensor_copy(out=Vb[:], in_=vf), ["vf"], ["Vb"])
                for hf_ in range(2):
                    rk_ = 128 if hf_ == 0 else 32
                    pi = 4 + npj[0] % 4
                    npj[0] += 1
                    p_ = ps[pi]

                    def mmg(e, p_=p_, hf_=hf_, rk_=rk_):
                        ins = None
                        for c in range(NCH):
                            ins = e.matmul(p_[:rk_, :NB], g1[:, c, hf_ * 128:hf_ * 128 + rk_], xg[:, c, :],
                                           start=(c == 0), stop=(c == NCH - 1))
                        return ins
                    R.op("pe", mmg, r=["rw_g1", xgr], w=[("rps", pi)])
                    act(lambda e, p_=p_, hf_=hf_, rk_=rk_: e.activation(out=tg[:rk_, hf_, :], in_=p_[:rk_, :NB], func=AF.Sigmoid),
                        [("rps", pi)], ["tg"])
                for g in range(2):
                    pi = 4 + npj[0] % 4
                    npj[0] += 1
                    p_ = ps[pi]

                    def mmg2(e, p_=p_, g=g):
                        ins = None
                        for j in range(4):
                            ec = g * 4 + j
                            e.matmul(p_[:, j * NB:(j + 1) * NB], g2[:, 0, ec * 128:(ec + 1) * 128], tg[:, 0, :], start=True, stop=False)
                            ins = e.matmul(p_[:, j * NB:(j + 1) * NB], g2[:32, 1, ec * 128:(ec + 1) * 128], tg[:32, 1, :],
                                           start=False, stop=True)
                        return ins
                    R.op("pe", mmg2, r=["rw_g2", "tg"], w=[("rps", pi)])
                    dve(lambda e, p_=p_, g=g: e.tensor_copy(out=gfb[:, g * 4:(g + 1) * 4, :],
                                                            in_=p_[:, :].rearrange("p (j t) -> p j t", t=NB)),
                        [("rps", pi)], ["gfb"])

                act(lambda e: e.activation(out=WC[:, :, :NCK], in_=Li4[:, :, :, C - 1], func=AF.Exp), [Lin], ["WC"])
                act(lambda e: e.activation(out=eNi, in_=Li, func=AF.Exp, scale=-1.0), [Lin], ["eNi"])
                act(lambda e: e.activation(out=Le, in_=Le, func=AF.Exp), [Len], [Len])
                act(lambda e: e.activation(out=Li, in_=Li, func=AF.Exp), [Lin], [Lin])
                eLe, eLen, eLi, eLin = Le, Len, Li, Lin

                def ev_r(p4, pi, g):
                    dve(lambda e: e.tensor_copy(out=rf[:, g * 4:(g + 1) * 4, :], in_=p4), [("rps", pi)], ["rf"])
                    dve(lambda e: e.tensor_tensor(out=Rt[:, g * 4:(g + 1) * 4, :], in0=p4, in1=eLi[:, g * 4:(g + 1) * 4, :],
                                                  op=ALU.mult), [("rps", pi), eLin], ["Rt"])
                proj(wr, "rw_w_r", xr, xrr, D, ev_r)
                dve(lambda e: e.scalar_tensor_tensor(out=At[:], in0=kkn, scalar=-1.0, in1=eLe, op0=ALU.mult, op1=ALU.mult),
                    ["kkn", eLen], ["At"])
                tmpA, tmpAn = eLe, eLen
                dve(lambda e: e.scalar_tensor_tensor(out=tmpA, in0=af, scalar=-1.0, in1=V(("rw_k_a", ri)), op0=ALU.add, op1=ALU.mult),
                    ["af", "At"], [tmpAn])
                dve(lambda e: e.scalar_tensor_tensor(out=kf, in0=tmpA, scalar=1.0, in1=kf, op0=ALU.add, op1=ALU.mult),
                    [tmpAn, "kf"], ["kf"])
                dve(lambda e: e.tensor_tensor(out=kkn, in0=kkn, in1=af, op=ALU.mult), ["kkn", "af", "At"], ["kkn"])
                dve(lambda e: e.tensor_tensor(out=kkn, in0=kkn, in1=eNi, op=ALU.mult), ["kkn", "eNi"], ["kkn"])
                dve(lambda e: e.tensor_copy(out=Bt[:], in_=kkn), ["kkn"], ["Bt"])
                WCb = WC[:, :, :NCK].unsqueeze(3).to_broadcast([128, NCH, NCK, C])
                dve(lambda e: e.tensor_tensor(out=BWb[:].rearrange("p c (k t) -> p c k t", t=C),
                                              in0=kkn.rearrange("p c (k t) -> p c k t", t=C), in1=WCb, op=ALU.mult),
                    ["kkn", "WC"], ["BWb"])
                dve(lambda e: e.tensor_tensor(out=tmpA, in0=rf, in1=V(("rw_r_k", ri)), op=ALU.mult), ["rf", "kf"], [tmpAn])
                dve(lambda e: e.tensor_tensor(out=sqb[:], in0=tmpA, in1=kf, op=ALU.mult), [tmpAn, "kf", "kkn"], ["sqb"])
                dve(lambda e: e.tensor_tensor(out=tmpA, in0=kf, in1=eNi, op=ALU.mult), ["kf", "eNi", "sqb"], [tmpAn])
                dve(lambda e: e.tensor_copy(out=Kt[:], in_=tmpA), [tmpAn], ["Kt"])
                dve(lambda e: e.tensor_tensor(out=KWb[:].rearrange("p c (k t) -> p c k t", t=C),
                                              in0=tmpA.rearrange("p c (k t) -> p c k t", t=C), in1=WCb, op=ALU.mult),
                    [tmpAn, "WC"], ["KWb"])
                RL = [(0, 128)]

                def headmm2(pi, lhs, rhs, mrows, ncols, rres):
                    def f(e):
                        ins = None
                        for h in range(16):
                            pb, c = (h % 2) * 64, h // 2
                            ins = e.matmul(ps[pi][pb:pb + mrows, c * 64:c * 64 + ncols], lhs(pb, c), rhs(pb, c),
                                           start=True, stop=True)
                        return ins
                    R.op("pe", f, r=rres, w=[("rps", pi)])

                def pv(pi, ncols):
                    return ps[pi][:, :].rearrange("p (c t) -> p c t", t=64)[:, :, :ncols]

                def rows_op(fn, r, w):
                    for (r0, r1) in RL:
                        dve(lambda e, r0=r0, r1=r1: fn(e, r0, r1), r, w)

                def do_chunk(ck):
                    o = ck * C
                    fmx = lambda tile: (lambda pb, c: tile[pb:pb + 64, c, o:o + C])
                    tk = lambda tile: (lambda pb, c: tile[pb:pb + C, c, :C])
                    tvv = lambda tile: (lambda pb, c: tile[pb:pb + C, c, :])

                    def evm(dst, dstn, pi, mask):
                        rows_op(lambda e, r0, r1: e.tensor_tensor(out=dst[r0:r1, :, :C], in0=pv(pi, C)[r0:r1],
                                                                  in1=mask[r0:r1, :, :C], op=ALU.mult),
                                [("rps", pi)], [dstn])

                    def evc(dst, dstn, pi, ncols):
                        rows_op(lambda e, r0, r1: e.tensor_copy(out=dst[r0:r1, :, :ncols], in_=pv(pi, ncols)[r0:r1]),
                                [("rps", pi)], [dstn])

                    headmm2(0, fmx(Bt), fmx(At), C, C, ["Bt", "At"])
                    evm(Pm, "Pm", 0, mSU)
                    headmm2(1, fmx(At), fmx(Bt), C, C, ["Bt", "At"])
                    evm(Qm, "Qm", 1, mSL)
                    rows_op(lambda e, r0, r1: e.tensor_tensor(out=Tm[r0:r1, :, :C], in0=Pm[r0:r1, :, :C],
                                                              in1=mI[r0:r1, :, :C], op=ALU.add), ["Pm"], ["Tm"])
                    headmm2(2, fmx(Kt), fmx(At), C, C, ["Kt", "At"])
                    evm(MKA, "MKA", 2, mSU)
                    headmm2(3, fmx(Bt), fmx(Rt), C, C, ["Bt", "Rt"])
                    evm(MBR, "MBR", 3, mU)
                    headmm2(0, fmx(Kt), fmx(Rt), C, C, ["Kt", "Rt"])
                    evm(MKR, "MKR", 0, mU)
                    if DBG.get("rl", 9) < 2.2:
                        return
                    for n_ in range(1, LV + 1):
                        if n_ < LV:
                            headmm2(0, tk(Qm), tk(Pm), C, C, ["Pm", "Qm"])
                        headmm2(1, tk(Pm), tk(Qm), C, C, ["Pm", "Qm"])
                        if n_ < LV:
                            evc(Pm, "Pm", 0, C)
                        evc(Qm, "Qm", 1, C)
                        headmm2(2, tk(Qm), tk(Tm), C, C, ["Qm", "Tm"])
                        rows_op(lambda e, r0, r1: e.tensor_tensor(out=Tm[r0:r1, :, :C], in0=pv(2, C)[r0:r1],
                                                                  in1=Tm[r0:r1, :, :C], op=ALU.add),
                                [("rps", 2), "Tm"], ["Tm"])
                    if DBG.get("rl", 9) < 2.5:
                        return
                    ptv = pv(4, 64)
                    for srcT, srcn_, dstT, dstn_ in ((At, "At", AtT, "AtT"), (Vb, "Vb", VT, "VT"),
                                                     (BWb, "BWb", BWT, "BWT"), (KWb, "KWb", KWT, "KWT")):
                        def ftr(e, srcT=srcT):
                            ins = None
                            for h in range(16):
                                pb, c = (h % 2) * 64, h // 2
                                ins = e.matmul(ps[4][pb:pb + C, c * 64:(c + 1) * 64], srcT[pb:pb + 64, c, o:o + C],
                                               self.identb[pb:pb + 64, pb:pb + 64], start=True, stop=True)
                            return ins
                        R.op("pe", ftr, r=[srcn_, "consts"], w=[("rps", 4)])
                        rows_op(lambda e, r0, r1, dstT=dstT: e.tensor_copy(out=dstT[r0:r1, :, :], in_=ptv[r0:r1]),
                                [("rps", 4)], [dstn_])
                    headmm2(3, tvv(AtT), tk(Tm), 64, C, ["AtT", "Tm"])
                    dve(lambda e: e.tensor_copy(out=Ah[:, :, :C], in_=pv(3, C)), [("rps", 3)], ["Ah"])
                    if DBG.get("rl", 9) < 2.7:
                        return
                    headmm2(0, tk(MKA), tvv(VT), C, 64, ["MKA", "VT"])
                    evc(XV, "XV", 0, 64)
                    headmm2(1, tk(Tm), tvv(XV), C, 64, ["Tm", "XV"])
                    evc(SAV, "SAV", 1, 64)
                    if DBG.get("rl", 9) < 3:
                        return
                    if samp:
                        R.dma("sp", Sf[:], swkv_d[ri, ck].rearrange("(c h2) k v -> (h2 k) c v", h2=2), w=["Sf"])
                        dve(lambda e: e.tensor_copy(out=Sb[:], in_=Sf[:]), ["Sf"], ["Sb"])
                    fS = lambda pb, c: Sb[pb:pb + 64, c, :]
                    headmm2(2, lambda pb, c: Ah[pb:pb + 64, c, :C], fS, C, 64, ["Ah", "Sb"])
                    rows_op(lambda e, r0, r1: e.tensor_tensor(out=SAb[r0:r1, :, :], in0=pv(2, 64)[r0:r1],
                                                              in1=SAV[r0:r1, :, :], op=ALU.add),
                            [("rps", 2), "SAV"], ["SAb"])

                    def fy(e):
                        ins = None
                        for h in range(16):
                            pb, c = (h % 2) * 64, h // 2
                            o_ = ps[3][pb:pb + C, c * 64:(c + 1) * 64]
                            e.matmul(o_, Rt[pb:pb + 64, c, o:o + C], Sb[pb:pb + 64, c, :], start=True, stop=False)
                            e.matmul(o_, MBR[pb:pb + C, c, :C], SAb[pb:pb + C, c, :], start=False, stop=False)
                            ins = e.matmul(o_, MKR[pb:pb + C, c, :C], VT[pb:pb + C, c, :], start=False, stop=True)
                        return ins
                    R.op("pe", fy, r=["Rt", "Sb", "MBR", "SAb", "MKR", "VT"], w=[("rps", 3)])
                    evc(ytm, "ytm", 3, 64)

                    def fs(e):
                        ins = None
                        for h in range(16):
                            pb, c = (h % 2) * 64, h // 2
                            o_ = ps[0][pb:pb + 64, c * 64:(c + 1) * 64]
                            e.matmul(o_, BWT[pb:pb + C, c, :], SAb[pb:pb + C, c, :], start=True, stop=False)
                            ins = e.matmul(o_, KWT[pb:pb + C, c, :], VT[pb:pb + C, c, :], start=False, stop=True)
                        return ins
                    R.op("pe", fs, r=["BWT", "SAb", "KWT", "VT"], w=[("rps", 0)])
                    dve(lambda e: e.tensor_tensor(out=Sf[:], in0=Sf[:], in1=WC[:, :, ck:ck + 1].to_broadcast([128, NCH, 64]),
                                                  op=ALU.mult), ["Sf", "WC"], ["Sf"])
                    dve(lambda e: e.tensor_tensor(out=Sf[:], in0=Sf[:], in1=pv(0, 64), op=ALU.add), ["Sf", ("rps", 0)], ["Sf"])
                    dve(lambda e: e.tensor_copy(out=Sb[:], in_=Sf[:]), ["Sf"], ["Sb"])
                    if samp:
                        R.dma("sp", wkvs_d[ri, ck].rearrange("(c h2) k v -> (h2 k) c v", h2=2), Sf[:], r=["Sf"])
                    elif blk == 15 and ck == NCK - 1:
                        R.dma("sp", wkvp_d[ri].rearrange("(c h2) k v -> (h2 k) c v", h2=2), Sf[:], r=["Sf"])
                    if DBG.get("rl", 9) < 4:
                        return
                    rows_op(lambda e, r0, r1: e.tensor_reduce(out=st1[r0:r1, :], in_=ytm[r0:r1], axis=AX.X, op=ALU.add),
                            ["ytm"], ["st1"])
                    rows_op(lambda e, r0, r1: e.tensor_scalar(out=st1[r0:r1, :], in0=st1[r0:r1, :], scalar1=1.0 / 64,
                                                              scalar2=None, op0=ALU.mult), ["st1"], ["st1"])
                    rows_op(lambda e, r0, r1: e.tensor_tensor(out=yc[r0:r1], in0=ytm[r0:r1],
                                                              in1=st1[r0:r1, :].unsqueeze(2).to_broadcast([r1 - r0, NCH, 64]),
                                                              op=ALU.subtract), ["ytm", "st1"], ["yc"])
                    rows_op(lambda e, r0, r1: e.tensor_tensor(out=ytm[r0:r1], in0=yc[r0:r1], in1=yc[r0:r1], op=ALU.mult),
                            ["yc"], ["ytm"])
                    rows_op(lambda e, r0, r1: e.tensor_reduce(out=st2[r0:r1, :], in_=ytm[r0:r1], axis=AX.X, op=ALU.add),
                            ["ytm"], ["st2"])
                    for (r0, r1) in RL:
                        act(lambda e, r0=r0, r1=r1: e.activation(out=st2[r0:r1, :], in_=st2[r0:r1, :], func=AF.Sqrt,
                                                                 scale=1.0 / 64, bias=self.eps_lnx[r0:r1, 0:1]),
                            ["st2"], ["st2"])
                    rows_op(lambda e, r0, r1: e.reciprocal(out=st2[r0:r1, :], in_=st2[r0:r1, :]), ["st2"], ["st2"])
                    rows_op(lambda e, r0, r1: e.tensor_tensor(out=yc[r0:r1], in0=yc[r0:r1],
                                                              in1=st2[r0:r1, :].unsqueeze(2).to_broadcast([r1 - r0, NCH, 64]),
                                                              op=ALU.mult), ["yc", "st2"], ["yc"])

                    def ftb(e):
                        ins = None
                        for h in range(16):
                            pb, c = (h % 2) * 64, h // 2
                            ins = e.matmul(ps[1][pb:pb + 64, c * 64:c * 64 + C], yc[pb:pb + C, c, :],
                                           self.identf[pb:pb + C, pb:pb + C], start=True, stop=True)
                        return ins
                    R.op("pe", ftb, r=["yc", "consts"], w=[("rps", 1)])
                    dve(lambda e: e.tensor_copy(out=yf[:, :, o:o + C], in_=pv(1, C)), [("rps", 1)], ["yf"])

                for ck in range(NCK if DBG.get("rl", 9) >= 2 else 0):
                    do_chunk(ck)
                dve(lambda e: e.tensor_tensor(out=yf, in0=yf, in1=V(("rw_lnx_g", ri)), op=ALU.mult), ["yf"], ["yf"])
                dve(lambda e: e.tensor_tensor(out=yf, in0=yf, in1=V(("rw_lnx_b", ri)), op=ALU.add), ["yf"], ["yf"])
                for hf_ in range(2):
                    pi = 4 + npj[0] % 4
                    npj[0] += 1
                    p_ = ps[pi]

                    def mmb(e, p_=p_, hf_=hf_):
                        ins = None
                        for j in range(4):
                            ins = e.matmul(p_[:, j * NB:(j + 1) * NB], self.blk64[:], sqb[:, hf_ * 4 + j, :], start=True, stop=True)
                        return ins
                    R.op("pe", mmb, r=["sqb", "consts"], w=[("rps", pi)])
                    dve(lambda e, p_=p_, hf_=hf_: e.tensor_tensor(out=rnb4[:, :].rearrange("p (j t) -> p j t", t=NB),
                                                                  in0=p_[:, :].rearrange("p (j t) -> p j t", t=NB),
                                                                  in1=vf[:, hf_ * 4:(hf_ + 1) * 4, :], op=ALU.mult),
                        [("rps", pi), "vf"], ["rnb4"])
                    dve(lambda e, hf_=hf_: e.tensor_tensor(out=yf[:, hf_ * 4:(hf_ + 1) * 4, :], in0=yf[:, hf_ * 4:(hf_ + 1) * 4, :],
                                                           in1=rnb4[:, :].rearrange("p (j t) -> p j t", t=NB), op=ALU.add),
                        ["yf", "rnb4"], ["yf"])
                dve(lambda e: e.tensor_tensor(out=yg[:], in0=yf, in1=gfb[:], op=ALU.mult), ["yf", "gfb"], [("xm", 0)])

                def ev_o(p4, pi, g):
                    dve(lambda e: e.tensor_tensor(out=xb[:, g * 4:(g + 1) * 4, :], in0=xb[:, g * 4:(g + 1) * 4, :], in1=p4,
                                                  op=ALU.add), [("rps", pi), "xb"], ["xb"])
                proj(wo, "rw_w_o", yg, ("xm", 0), D, ev_o)
                R.dma("sp", xs_v[:, :, t0:t0 + NB], xb, r=["xb"])
            for blk in (DBG.get("blks", range(nblk)) if DBG.get("rl", 9) >= 1 else []):
                do_block(blk)
            R.barrier()
            self.psb = self.psb_save
            self.stack = old
        for c in range(NCH):
            R.dma("sp", self.x[:, c, :], xs_v[:, c, :], w=[("x", c, b) for b in range(len(BLOCKS))])
        R.barrier()

    def phase_attn(self, l):
        R = self.R
        with contextlib.ExitStack() as st:
            old, self.stack = self.stack, st
            self.sq_scr = self.sb("sq_scr", [128, NCH, 512], BF16)
            self.rstd = self.sb("rstd", [128, 512], F32)
            self.psb = [self.ps(f"psb{i}", [128, 512], F32) for i in range(8)]
            wq = self.sb("wq", [128, NCH, D], BF16)
            wo = self.sb("wo", [128, NCH, D], BF16)
            KTp = self.sb("KTp", [128, NCH, MEM], BF16)
            Vp = self.sb("Vp", [128, 2, D], BF16)
            hbk = self.sb("hbk", [128, NCH, 512], BF16)
            qT = self.sb("qT", [128, NCH, 512], BF16)
            oT = self.sb("oT", [128, NCH, 512], BF16)
            PT = [self.sb(f"PT{i}", [128, 2, 512], BF16) for i in range(2)]
            rden = self.sb("rden", [128, 512], F32)
            wkvb = [self.sb(f"wkvb{i}", [128, NCH, 512], BF16) for i in range(2)]
            kvo = [self.sb(f"kvo{i}", [128, 512], F32) for i in range(2)]
            KTs = [self.sb(f"KTs{i}", [128, NCH, MEM], BF16) for i in range(2)]
            Vs = [self.sb(f"Vs{i}", [128, 2, D], BF16) for i in range(2)]
            PTs = self.sb("PTs", [128, 8, 8], BF16)
            rdens = self.sb("rdens", [128, 4, 8], F32)
            psb = self.psb

            wkv_v = self.wkv_d[l].rearrange("(c p) e -> p c e", p=128)
            nk = 0
            for j in range(4):
                b = j % 2
                R.dma("pool", wkvb[b][:], wkv_v[:, :, j * 512:(j + 1) * 512], w=[("wkvb", b)])
                for mt in range(2):
                    ps = psb[mt]

                    def mm(e, ps=ps, b=b, mt=mt):
                        ins = None
                        for c in range(NCH):
                            ins = e.matmul(ps[:, :], self.memT[:, c, mt * 128:(mt + 1) * 128], wkvb[b][:, c, :],
                                           start=(c == 0), stop=(c == NCH - 1))
                        return ins
                    R.op("pe", mm, r=["memT", ("wkvb", b)], w=[("ps", mt)])
                    kb = nk % 2
                    nk += 1
                    R.op("dve", lambda e, ps=ps, kb=kb: e.tensor_copy(out=kvo[kb][:], in_=ps[:]),
                         r=[("ps", mt)], w=[("kvo", kb)])
                    dst = self.memk_d if j < 2 else self.memv_d
                    R.dma("sp", dst[l, mt * 128:(mt + 1) * 128, (j % 2) * 512:(j % 2 + 1) * 512], kvo[kb][:],
                          r=[("kvo", kb)])
                    if j >= 2:
                        R.op("dve", lambda e, ps=ps, mt=mt, j=j: e.tensor_copy(
                            out=Vp[:, mt, (j - 2) * 512:(j - 1) * 512], in_=ps[:]),
                            r=[("ps", mt)], w=["Vp"])
                if j < 2:
                    for ec in range(4):
                        pi = 2 + ec % 2
                        ps = psb[pi]

                        def mm(e, ps=ps, b=b, ec=ec):
                            ins = None
                            for c in range(NCH):
                                ins = e.matmul(ps[:, :MEM], wkvb[b][:, c, ec * 128:(ec + 1) * 128], self.memT[:, c, :],
                                               start=(c == 0), stop=(c == NCH - 1))
                            return ins
                        R.op("pe", mm, r=["memT", ("wkvb", b)], w=[("ps", pi)])
                        R.op("dve", lambda e, ps=ps, j=j, ec=ec: e.tensor_copy(out=KTp[:, j * 4 + ec, :], in_=ps[:, :MEM]),
                             r=[("ps", pi)], w=["KTp"])

            for hf in range(2):
                R.dma("pool", wq[:, :, hf * 512:(hf + 1) * 512],
                      self.wq_d[l].rearrange("(c p) e -> p c e", p=128)[:, :, hf * 512:(hf + 1) * 512], w=["wq"])
                R.dma("pool", wo[:, :, hf * 512:(hf + 1) * 512],
                      self.wo_d[l].rearrange("(c p) e -> p c e", p=128)[:, :, hf * 512:(hf + 1) * 512], w=["wo"])

            npt = 0
            lvl = DBG.get("lvl", 9)
            for blk, (t0, n) in enumerate(BLOCKS if lvl >= 2 else []):
                self.norm_block(blk, ("norm_g", l, 1), hbk, ["hbk"], out_off=0)
                for ec in range(NCH):
                    pi = ec % 2
                    ps = psb[pi]

                    def mm(e, ps=ps, ec=ec, n=n):
                        ins = None
                        for c in range(NCH):
                            ins = e.matmul(ps[:, :n], wq[:, c, ec * 128:(ec + 1) * 128], hbk[:, c, :n],
                                           start=(c == 0), stop=(c == NCH - 1))
                        return ins
                    R.op("pe", mm, r=["wq", "hbk"], w=[("ps", pi)])
                    R.op("dve", lambda e, ps=ps, ec=ec, n=n: e.tensor_copy(out=qT[:, ec, :n], in_=ps[:, :n]),
                         r=[("ps", pi)], w=[("qT", ec)])
                if lvl < 3:
                    continue
                if blk < 4:
                    for h in range(4):
                        pb = npt % 2
                        npt += 1
                        for mt in range(2):
                            pi = 2 + mt
                            ps = psb[pi]

                            def mm(e, ps=ps, h=h, mt=mt, n=n):
                                ins = None
                                for dc in range(2):
                                    ins = e.matmul(ps[:, :n], KTp[:, h * 2 + dc, mt * 128:(mt + 1) * 128],
                                                   qT[:, h * 2 + dc, :n], start=(dc == 0), stop=(dc == 1))
                                return ins
                            R.op("pe", mm, r=["KTp", ("qT", h * 2), ("qT", h * 2 + 1)], w=[("ps", pi)])
                            R.op("act", lambda e, ps=ps, pb=pb, mt=mt, n=n: e.activation(
                                out=PT[pb][:, mt, :n], in_=ps[:, :n], func=AF.Exp, scale=1.0 / 16.0),
                                r=[("ps", pi)], w=[("PT", pb, mt)])
                        ps4 = psb[4]

                        def mmd(e, pb=pb, n=n):
                            ins = None
                            for mt in range(2):
                                ins = e.matmul(ps4[:, :n], self.ones_1[:], PT[pb][:, mt, :n], start=(mt == 0), stop=(mt == 1))
                            return ins
                        R.op("pe", mmd, r=["ones_1", ("PT", pb, 0), ("PT", pb, 1)], w=[("ps", 4)])
                        R.op("dve", lambda e, n=n: e.reciprocal(out=rden[:, :n], in_=ps4[:, :n]), r=[("ps", 4)], w=["rden"])
                        for dc in range(2):
                            pi = 5 + dc
                            ps = psb[pi]

                            def mmv(e, ps=ps, h=h, dc=dc, pb=pb, n=n):
                                ins = None
                                for mt in range(2):
                                    ins = e.matmul(ps[:, :n], Vp[:, mt, h * 256 + dc * 128:h * 256 + (dc + 1) * 128],
                                                   PT[pb][:, mt, :n], start=(mt == 0), stop=(mt == 1))
                                return ins
                            R.op("pe", mmv, r=["Vp", ("PT", pb, 0), ("PT", pb, 1)], w=[("ps", pi)])
                            R.op("dve", lambda e, ps=ps, h=h, dc=dc, n=n: e.tensor_tensor(
                                out=oT[:, h * 2 + dc, :n], in0=ps[:, :n], in1=rden[:, :n], op=ALU.mult),
                                r=[("ps", pi), "rden"], w=[("oT", h * 2 + dc)])
                else:
                    for sb_ in range(DBG.get("nseq", NSEQ)):
                        kb = sb_ % 2
                        R.dma("pool", KTs[kb][:], self.cKT_d[l, sb_].rearrange("h (dc p) m -> p (h dc) m", p=128),
                              w=[("KTs", kb)])
                        R.dma("pool", Vs[kb][:], self.cV_d[l, sb_].rearrange("(mt p) e -> p mt e", p=128),
                              w=[("Vs", kb)])
                        c0 = sb_ * 8
                        ps = psb[2 + sb_ % 2]
                        pi = 2 + sb_ % 2

                        def mms(e, ps=ps, kb=kb, c0=c0):
                            ins = None
                            for h in range(4):
                                for mt in range(2):
                                    for dc in range(2):
                                        ins = e.matmul(ps[:, (h * 2 + mt) * 8:(h * 2 + mt + 1) * 8],
                                                       KTs[kb][:, h * 2 + dc, mt * 128:(mt + 1) * 128],
                                                       qT[:, h * 2 + dc, c0:c0 + 8], start=(dc == 0), stop=(dc == 1))
                            return ins
                        R.op("pe", mms, r=[("KTs", kb)] + [("qT", c) for c in range(NCH)], w=[("ps", pi)])
                        R.op("act", lambda e, ps=ps: e.activation(out=PTs[:].rearrange("p a b -> p (a b)"), in_=ps[:, :64],
                                                                   func=AF.Exp, scale=1.0 / 16.0),
                             r=[("ps", pi)], w=["PTs"])
                        ps4 = psb[4]

                        def mmd(e):
                            ins = None
                            for h in range(4):
                                for mt in range(2):
                                    ins = e.matmul(ps4[:, h * 8:(h + 1) * 8], self.ones_1[:], PTs[:, h * 2 + mt, :],
                                                   start=(mt == 0), stop=(mt == 1))
                            return ins
                        R.op("pe", mmd, r=["ones_1", "PTs"], w=[("ps", 4)])
                        R.op("dve", lambda e: e.reciprocal(out=rdens[:].rearrange("p a b -> p (a b)"), in_=ps4[:, :32]),
                             r=[("ps", 4)], w=["rdens"])
                        pi2 = 5 + sb_ % 2
                        psv = psb[pi2]

                        def mmv(e, psv=psv, kb=kb):
                            ins = None
                            for h in range(4):
                                for dc in range(2):
                                    for mt in range(2):
                                        ins = e.matmul(psv[:, (h * 2 + dc) * 8:(h * 2 + dc + 1) * 8],
                                                       Vs[kb][:, mt, h * 256 + dc * 128:h * 256 + (dc + 1) * 128],
                                                       PTs[:, h * 2 + mt, :], start=(mt == 0), stop=(mt == 1))
                            return ins
                        R.op("pe", mmv, r=[("Vs", kb), "PTs"], w=[("ps", pi2)])

                        def nrm(e, psv=psv, c0=c0):
                            ins = None
                            for h in range(4):
                                for dc in range(2):
                                    ins = e.tensor_tensor(out=oT[:, h * 2 + dc, c0:c0 + 8],
                                                          in0=psv[:, (h * 2 + dc) * 8:(h * 2 + dc + 1) * 8],
                                                          in1=rdens[:, h, :], op=ALU.mult)
                            return ins
                        R.op("dve", nrm, r=[("ps", pi2), "rdens"], w=[("oT", c) for c in range(NCH)])
                for ec in range(NCH):
                    pi = ec % 2
                    ps = psb[pi]

                    def mm(e, ps=ps, ec=ec, n=n):
                        ins = None
                        for c in range(NCH):
                            ins = e.matmul(ps[:, :n], wo[:, c, ec * 128:(ec + 1) * 128], oT[:, c, :n],
                                           start=(c == 0), stop=(c == NCH - 1))
                        return ins
                    R.op("pe", mm, r=["wo"] + [("oT", c) for c in range(NCH)], w=[("ps", pi)])
                    R.op("dve", lambda e, ps=ps, ec=ec, t0=t0, n=n: e.tensor_tensor(
                        out=self.x[:, ec, t0:t0 + n], in0=self.x[:, ec, t0:t0 + n], in1=ps[:, :n], op=ALU.add),
                        r=[("ps", pi)], w=[("x", ec, blk)])
            R.barrier()
            self.stack = old

    def phase_mlp(self, l):
        R = self.R
        with contextlib.ExitStack() as st:
            old, self.stack = self.stack, st
            self.sq_scr = self.sb("sq_scr", [128, NCH, 512], BF16)
            self.rstd = self.sb("rstd", [128, 512], F32)
            self.psb = [self.ps(f"psb{i}", [128, 512], F32) for i in range(8)]
            self.hb = self.sb("hb", [128, NCH, T], BF16)
            wu = [self.sb(f"wu{i}", [128, NCH, 512], BF16) for i in range(2)]
            wd = [self.sb(f"wd{i}", [128, 4, D], BF16) for i in range(2)]
            hT = [self.sb(f"hT{i}", [128, 4, 512], BF16) for i in range(2)]
            rl = [self.sb(f"rl{i}", [128, 512], F32) for i in range(2)]
            wu_v = self.w_up_d[l].rearrange("(c p) f -> p c f", p=128)
            wd_v = self.w_dn_d[l].rearrange("(fc p) e -> p fc e", p=128)

            for blk in range(len(BLOCKS)):
                self.norm_block(blk, ("norm_g", l, 2), self.hb, [("hb", blk)])

            nrl = 0
            for j in range(8):
                b = j % 2
                R.dma("pool", wu[b][:], wu_v[:, :, j * 512:(j + 1) * 512], w=[("wu", b)])
                R.dma("pool", wd[b][:], wd_v[:, j * 4:(j + 1) * 4, :], w=[("wd", b)])
                for blk, (t0, n) in enumerate(BLOCKS):
                    hb_ = (j * len(BLOCKS) + blk) % 2
                    for fc in range(4):
                        ps = self.psb[fc]

                        def mm(e, ps=ps, fc=fc, t0=t0, n=n, b=b):
                            ins = None
                            for c in range(NCH):
                                ins = e.matmul(ps[:, :n], wu[b][:, c, fc * 128:(fc + 1) * 128],
                                               self.hb[:, c, t0:t0 + n], start=(c == 0), stop=(c == NCH - 1))
                            return ins
                        R.op("pe", mm, r=[("wu", b), ("hb", blk)], w=[("ps", fc)])
                        rb = nrl % 2
                        nrl += 1
                        R.op("act", lambda e, ps=ps, rb=rb, n=n: e.activation(out=rl[rb][:, :n], in_=ps[:, :n],
                                                                               func=AF.Relu),
                             r=[("ps", fc)], w=[("rl", rb)])
                        R.op("pool", lambda e, rb=rb, fc=fc, n=n, hb_=hb_: e.tensor_tensor(
                            out=hT[hb_][:, fc, :n], in0=rl[rb][:, :n], in1=rl[rb][:, :n], op=ALU.mult),
                            r=[("rl", rb)], w=[("hT", hb_, fc)])
                    for oc in range(NCH):
                        pi = 4 + oc % 4
                        ps = self.psb[pi]

                        def mm2(e, ps=ps, oc=oc, n=n, b=b, hb_=hb_):
                            ins = None
                            for fc in range(4):
                                ins = e.matmul(ps[:, :n], wd[b][:, fc, oc * 128:(oc + 1) * 128],
                                               hT[hb_][:, fc, :n], start=(fc == 0), stop=(fc == 3))
                            return ins
                        R.op("pe", mm2, r=[("wd", b)] + [("hT", hb_, fc) for fc in range(4)], w=[("ps", pi)])
                        R.op("dve", lambda e, ps=ps, oc=oc, t0=t0, n=n: e.tensor_tensor(
                            out=self.x[:, oc, t0:t0 + n], in0=self.x[:, oc, t0:t0 + n], in1=ps[:, :n], op=ALU.add),
                            r=[("ps", pi)], w=[("x", oc, blk)])
            R.barrier()
            self.stack = old

    def final_norm(self):
        R = self.R
        with contextlib.ExitStack() as st:
            old, self.stack = self.stack, st
            self.sq_scr = self.sb("sq_scr", [128, NCH, 512], BF16)
            self.rstd = self.sb("rstd", [128, 512], F32)
            self.psb = [self.ps(f"psb{i}", [128, 512], F32) for i in range(8)]
            yb = [self.sb(f"yb{i}", [128, NCH, 512], F32) for i in range(2)]
            yv = self.yT_d.rearrange("(c p) t -> p c t", p=128)
            for blk, (t0, n) in enumerate(BLOCKS):
                b = blk % 2
                self.norm_block(blk, ("final_g",), yb[b], [("yb", b)], out_off=0)
                R.dma("sp", yv[:, :, t0:t0 + n], yb[b][:, :, :n], r=[("yb", b)])
            R.barrier()
            self.stack = old


_CACHE = {}
PHASES = ("conv", "rwkv", "attn", "mlp")
NLAYERS = DEPTH
DBG = {}


def get_nc(phases):
    key = tuple(phases)
    if key not in _CACHE:
        b = Builder(phases)
        nc = b.build()
        _CACHE[key] = (nc, set(b.dins), set(b.douts))
    return _CACHE[key]


def make_rconst():
    i = np.arange(64)
    su = (i[:, None] < i[None, :]).astype(np.float32)
    sl_ = (i[:, None] > i[None, :]).astype(np.float32)
    u = (i[:, None] <= i[None, :]).astype(np.float32)
    ey = np.eye(64, dtype=np.float32)
    cm = np.stack([np.broadcast_to(m[:, None, :], (64, 8, 64)) for m in (su, sl_, u, ey)], axis=1)
    out = np.zeros((128, 2048 + 256), np.float32)
    out[:64, :2048] = cm.reshape(64, 2048)
    out[64:, :2048] = cm.reshape(64, 2048)
    out[:, 2048:2176] = np.eye(128, dtype=np.float32)
    out[:64, 2176:2240] = 1.0
    out[64:, 2240:2304] = 1.0
    return out


def extra_inputs(inp, c, sl):
    f32 = np.float32
    return {
        "rconst": lambda: make_rconst(),
        "sshiftT": lambda: np.ascontiguousarray(np.asarray(inp["state_shift"][:, sl], f32).transpose(0, 2, 1)),
        "swkvT": lambda: np.ascontiguousarray(np.asarray(inp["state_wkv"][:, sl], f32).transpose(0, 1, 2, 4, 3)),
        **{nm: (lambda nm=nm: np.asarray(inp[nm], f32)) for nm in
           ("rw_w_r", "rw_w_k", "rw_w_v", "rw_w_o", "rw_w1", "rw_w2", "rw_a1", "rw_a2", "rw_g1", "rw_g2", "rw_v1", "rw_v2")},
        "cv_w_in": lambda: np.asarray(inp["cv_w_in"], f32),
        "cv_w_out": lambda: np.asarray(inp["cv_w_out"], f32),
        "sconvT": lambda: np.ascontiguousarray(np.asarray(inp["state_conv"][:, sl], f32).transpose(0, 3, 1, 2)),
    }


def make_in_maps(inp, used, n_cores=8):
    vecs = pack_vecs(inp)
    maps = []
    for c in range(n_cores):
        xp = np.asarray(inp["x_prompt"][c], np.float32)
        xs = np.asarray(inp["x_sample"][c * NSEQ:(c + 1) * NSEQ], np.float32).reshape(TS, D)
        xT = np.ascontiguousarray(np.concatenate([xp, xs], axis=0).T)
        sl = slice(c * NSEQ, (c + 1) * NSEQ)
        sl = slice(c * NSEQ, (c + 1) * NSEQ)
        f32 = np.float32
        m = {"xT": lambda: xT, "vecs": lambda: vecs,
             "mlp_w_up": lambda: np.asarray(inp["mlp_w_up"], f32),
             "mlp_w_down": lambda: np.asarray(inp["mlp_w_down"], f32),
             "memT": lambda: np.ascontiguousarray(np.asarray(inp["mem_prompt"][c], f32).T),
             "xa_w_kv": lambda: np.asarray(inp["xa_w_kv"], f32),
             "xa_w_q": lambda: np.asarray(inp["xa_w_q"], f32),
             "xa_w_o": lambda: np.asarray(inp["xa_w_o"], f32),
             "cKT": lambda: np.ascontiguousarray(np.asarray(inp["cache_mem_k"][:, sl], f32).transpose(0, 1, 3, 4, 2)),
             "cV": lambda: np.ascontiguousarray(np.asarray(inp["cache_mem_v"][:, sl], f32).reshape(DEPTH, NSEQ, MEM, D))}
        m.update(extra_inputs(inp, c, sl))
        maps.append({k: v() for k, v in m.items() if k in used})
    return maps


def kernel(**inputs):
    phases = PHASES
    nc, used, douts = get_nc(phases)
    in_maps = make_in_maps(inputs, used)
    res = run_bass_kernel_spmd(nc, in_maps, core_ids=list(range(8)))
    outs = res.results
    yT = np.stack([np.asarray(o["yT"]) for o in outs])
    y_prompt = np.ascontiguousarray(yT[:, :, :TP].transpose(0, 2, 1))
    y_sample = np.ascontiguousarray(yT[:, :, TP:].transpose(0, 2, 1)).reshape(8 * NSEQ, 8, D)
    res_extra = {}
    for k_ in douts:
        if k_.startswith("dbg_"):
            res_extra[k_] = np.stack([np.asarray(o[k_]) for o in outs])
    if "convpT" in douts:
        cp = np.stack([np.asarray(o["convpT"]) for o in outs], axis=1)
        res_extra["conv_prompt"] = np.ascontiguousarray(cp.transpose(0, 1, 3, 2))
        cs = np.stack([np.asarray(o["convsT"]) for o in outs], axis=1)
        res_extra["conv_sample"] = np.ascontiguousarray(cs.transpose(0, 1, 3, 4, 2)).reshape(2, 8 * NSEQ, 30, D)
    if "shiftpT" in douts:
        sp = np.stack([np.asarray(o["shiftpT"]) for o in outs], axis=1)
        res_extra["shift_prompt"] = np.ascontiguousarray(sp.transpose(0, 1, 3, 2)).reshape(2, 8, D)
        ss = np.stack([np.asarray(o["shiftsT"]) for o in outs], axis=1)
        res_extra["shift_sample"] = np.ascontiguousarray(ss.transpose(0, 1, 4, 3, 2)).reshape(2, 8 * NSEQ, D)
        wp = np.stack([np.asarray(o["wkvpT"]) for o in outs], axis=1)
        res_extra["wkv_prompt"] = np.ascontiguousarray(wp.transpose(0, 1, 2, 4, 3))
        ws = np.stack([np.asarray(o["wkvsT"]) for o in outs], axis=1)
        res_extra["wkv_sample"] = np.ascontiguousarray(ws.transpose(0, 1, 2, 3, 5, 4)).reshape(2, 8 * NSEQ, 16, 64, 64)
    DBG["extra"] = res_extra
    if "mem_k" not in douts:
        return y_prompt, y_sample
    mem_k = np.stack([np.asarray(o["mem_k"]) for o in outs], axis=1).reshape(DEPTH, 8, MEM, 4, 256)
    mem_v = np.stack([np.asarray(o["mem_v"]) for o in outs], axis=1).reshape(DEPTH, 8, MEM, 4, 256)
    if DBG.get("short"):
        return y_prompt, y_sample, mem_k, mem_v
    f32 = np.float32
    ex = res_extra
    conv_p = ex.get("conv_prompt", np.zeros((2, 8, 30, D), f32))
    conv_s = ex.get("conv_sample", np.zeros((2, 8 * NSEQ, 30, D), f32))
    shift_p = ex.get("shift_prompt", np.zeros((2, 8, D), f32))
    shift_s = ex.get("shift_sample", np.zeros((2, 8 * NSEQ, D), f32))
    wkv_p = ex.get("wkv_prompt", np.zeros((2, 8, 16, 64, 64), f32))
    wkv_s = ex.get("wkv_sample", np.zeros((2, 8 * NSEQ, 16, 64, 64), f32))
    return (y_prompt, y_sample, mem_k, mem_v, conv_p.astype(f32), shift_p.astype(f32), wkv_p.astype(f32),
            conv_s.astype(f32), shift_s.astype(f32), wkv_s.astype(f32))
```

```python
import contextlib
import numpy as np
import concourse.bass as bass
import concourse.mybir as mybir
from concourse.bass_utils import run_bass_kernel_spmd

F32 = mybir.dt.float32
BF16 = mybir.dt.bfloat16
ALU = mybir.AluOpType
AF = mybir.ActivationFunctionType
AX = mybir.AxisListType

D = 1024
NCH = 8
TP = 2048
NSEQ = 16
TS = 128
T = TP + TS
DEPTH = 4
MEM = 256
DFF = 4096
EPOCH = 16000
NDMASEM = 24

BLOCKS = [(0, 512), (512, 512), (1024, 512), (1536, 512), (2048, 128)]


class Rec:
    ENGS = ("pe", "act", "dve", "pool", "sp")

    def __init__(self, nc, stack):
        self.nc = nc
        self.stack = stack
        self.lists = {e: [] for e in self.ENGS}
        self.count = {e: 0 for e in self.ENGS}
        self.sems = {}
        self.seen = {e: {} for e in self.ENGS}
        self.last_w = {}
        self.readers = {}
        self.dma_slot = 0
        self.dma_val = [0] * NDMASEM
        for i in range(NDMASEM):
            self.sems[("dma", i)] = stack.enter_context(nc.semaphore(f"dma{i}"))
        self.nwait = 0

    def _sem(self, key):
        if key not in self.sems:
            self.sems[key] = self.stack.enter_context(self.nc.semaphore(f"s_{key[0]}_{key[1]}"))
        return self.sems[key]

    def _wait(self, eng, ev):
        key, val = ev
        if self.seen[eng].get(key, 0) >= val:
            return
        self.seen[eng][key] = val
        sem = self._sem(key)
        self.lists[eng].append(lambda e, sem=sem, val=val: e.wait_ge(sem, val))
        self.nwait += 1

    def _deps(self, eng, r, w):
        evs = []
        for res in r:
            ev = self.last_w.get(res)
            if ev is not None:
                evs.append(ev)
        for res in w:
            ev = self.last_w.get(res)
            if ev is not None:
                evs.append(ev)
            evs.extend(self.readers.get(res, ()))
        for ev in evs:
            self._wait(eng, ev)

    def _commit(self, ev, r, w):
        for res in r:
            self.readers.setdefault(res, []).append(ev)
        for res in w:
            self.last_w[res] = ev
            self.readers[res] = []

    def op(self, eng, fn, r=(), w=()):
        self._deps(eng, r, w)
        n = self.count[eng]
        key = (eng, n // EPOCH)
        val = n % EPOCH + 1
        sem = self._sem(key)
        self.count[eng] = n + 1
        self.lists[eng].append(lambda e, fn=fn, sem=sem: fn(e).then_inc(sem, 1))
        self._commit((key, val), r, w)

    def dma(self, eng, out, in_, r=(), w=(), **kw):
        self._deps(eng, r, w)
        s = self.dma_slot
        self.dma_slot = (s + 1) % NDMASEM
        key = ("dma", s)
        if self.dma_val[s] > 0:
            self._wait(eng, (key, self.dma_val[s]))
        self.dma_val[s] += 16
        val = self.dma_val[s]
        sem = self.sems[key]
        self.lists[eng].append(
            lambda e, out=out, in_=in_, sem=sem, kw=kw: e.dma_start(out=out, in_=in_, **kw).then_inc(sem, 16))
        self._commit((key, val), r, w)

    def barrier(self):
        evs = []
        for e in self.ENGS:
            n = self.count[e]
            if n > 0:
                evs.append(((e, (n - 1) // EPOCH), (n - 1) % EPOCH + 1))
        for s in range(NDMASEM):
            if self.dma_val[s] > 0:
                evs.append((("dma", s), self.dma_val[s]))
        for e in self.ENGS:
            for ev in evs:
                if ev[0][0] == e:
                    continue
                self._wait(e, ev)

    def finish(self):
        for s in range(NDMASEM):
            if self.dma_val[s] > 0:
                self._wait("sp", (("dma", s), self.dma_val[s]))
        for e in self.ENGS:
            n = self.count[e]
            if n > 0 and e != "sp":
                self._wait("sp", ((e, (n - 1) // EPOCH), (n - 1) % EPOCH + 1))
        nc = self.nc
        lists = self.lists
        with nc.Block() as block:
            @block.tensor
            def _(e):
                for f in lists["pe"]:
                    f(e)

            @block.scalar
            def _(e):
                for f in lists["act"]:
                    f(e)

            @block.vector
            def _(e):
                for f in lists["dve"]:
                    f(e)

            @block.gpsimd
            def _(e):
                for f in lists["pool"]:
                    f(e)

            @block.sync
            def _(e):
                for f in lists["sp"]:
                    f(e)


def vec_layout():
    names = []
    for l in range(DEPTH):
        for j in range(3):
            names.append(("norm_g", l, j))
    names.append(("final_g",))
    for ci in range(2):
        names += [("cv_b_in", ci, 0), ("cv_b_in", ci, 1)]
        for j in range(31):
            names.append(("cv_w_dw", ci, j))
        names += [("cv_b_dw", ci), ("cv_ln_g", ci), ("cv_ln_b", ci), ("cv_b_out", ci)]
    for ri in range(2):
        for j in range(6):
            names.append(("rw_mix", ri, j))
        names += [("rw_w0", ri), ("rw_a0", ri), ("rw_k_k", ri), ("rw_k_a", ri), ("rw_r_k", ri),
                  ("rw_lnx_g", ri), ("rw_lnx_b", ri)]
    names.append(("rw_v0", 0))
    return {n: i for i, n in enumerate(names)}


VIDX = vec_layout()
NVEC = len(VIDX)


def pack_vecs(inp):
    out = np.zeros((128, NVEC, 8), np.float32)

    def put(key, v):
        out[:, VIDX[key], :] = np.asarray(v, np.float32).reshape(8, 128).T

    for l in range(DEPTH):
        for j in range(3):
            put(("norm_g", l, j), inp["norm_g"][l, j])
    put(("final_g",), inp["final_g"])
    for ci in range(2):
        put(("cv_b_in", ci, 0), inp["cv_b_in"][ci, :D])
        put(("cv_b_in", ci, 1), inp["cv_b_in"][ci, D:])
        for j in range(31):
            put(("cv_w_dw", ci, j), inp["cv_w_dw"][ci, j])
        put(("cv_b_dw", ci), inp["cv_b_dw"][ci])
        put(("cv_ln_g", ci), inp["cv_ln_g"][ci])
        put(("cv_ln_b", ci), inp["cv_ln_b"][ci])
        put(("cv_b_out", ci), inp["cv_b_out"][ci])
    for ri in range(2):
        for j in range(6):
            put(("rw_mix", ri, j), inp["rw_mix"][ri, j])
        put(("rw_w0", ri), inp["rw_w0"][ri])
        put(("rw_a0", ri), inp["rw_a0"][ri])
        put(("rw_k_k", ri), inp["rw_k_k"][ri])
        put(("rw_k_a", ri), inp["rw_k_a"][ri])
        put(("rw_r_k", ri), inp["rw_r_k"][ri].reshape(-1))
        put(("rw_lnx_g", ri), inp["rw_lnx_g"][ri])
        put(("rw_lnx_b", ri), inp["rw_lnx_b"][ri])
    put(("rw_v0", 0), inp["rw_v0"][0])
    return out.reshape(128, NVEC * 8)


class Builder:
    def __init__(self, phases=("mlp",)):
        self.phases = phases
        self.nc = bass.Bass("TRN2", target_bir_lowering=False)
        self.stack = contextlib.ExitStack()
        self.R = Rec(self.nc, self.stack)
        self.dins = {}
        self.douts = {}

    def dram_in(self, name, shape):
        if name not in self.dins:
            self.dins[name] = self.nc.dram_tensor(name, list(shape), F32, kind="ExternalInput").ap()
        return self.dins[name]

    def dram_out(self, name, shape):
        if name not in self.douts:
            self.douts[name] = self.nc.dram_tensor(name, list(shape), F32, kind="ExternalOutput").ap()
        return self.douts[name]

    def sb(self, name, shape, dtype):
        self.uid = getattr(self, "uid", 0) + 1
        return self.stack.enter_context(self.nc.sbuf_tensor(f"{name}_{self.uid}", list(shape), dtype))

    def ps(self, name, shape, dtype=F32):
        self.uid = getattr(self, "uid", 0) + 1
        return self.stack.enter_context(self.nc.psum_tensor(f"{name}_{self.uid}", list(shape), dtype))

    w_up_d = property(lambda self: self.dram_in("mlp_w_up", [DEPTH, D, DFF]))
    w_dn_d = property(lambda self: self.dram_in("mlp_w_down", [DEPTH, DFF, D]))
    memT_d = property(lambda self: self.dram_in("memT", [D, MEM]))
    wkv_d = property(lambda self: self.dram_in("xa_w_kv", [DEPTH, D, 2 * D]))
    wq_d = property(lambda self: self.dram_in("xa_w_q", [DEPTH, D, D]))
    wo_d = property(lambda self: self.dram_in("xa_w_o", [DEPTH, D, D]))
    cKT_d = property(lambda self: self.dram_in("cKT", [DEPTH, NSEQ, 128, NCH * MEM]))
    cV_d = property(lambda self: self.dram_in("cV", [DEPTH, NSEQ, 128, 2 * D]))
    memk_d = property(lambda self: self.dram_out("mem_k", [DEPTH, MEM, D]))
    memv_d = property(lambda self: self.dram_out("mem_v", [DEPTH, MEM, D]))

    def vec(self, key):
        i = VIDX[key]
        return self.vecs[:, i * 8:(i + 1) * 8]

    def build(self):
        nc, R = self.nc, self.R
        self.xT_d = self.dram_in("xT", [D, T])
        self.vecs_d = self.dram_in("vecs", [128, NVEC * 8])
        self.yT_d = self.dram_out("yT", [D, T])

        self.x = self.sb("x", [128, NCH, T], F32)
        self.vecs = self.sb("vecs_sb", [128, NVEC * 8], F32)
        self.ones_m = self.sb("ones_m", [128, 128], BF16)
        self.psb = None
        self.ones_1 = self.sb("ones_1", [128, 128], BF16)
        R.op("pool", lambda e: e.memset(self.ones_1[:], 1.0), w=["ones_1"])
        if "attn" in self.phases:
            self.memT = self.sb("memT_sb", [128, NCH, MEM], BF16)
            R.dma("pool", self.memT[:], self.memT_d.rearrange("(c p) m -> p c m", p=128), w=["memT"])

        R.op("pool", lambda e: e.memset(self.ones_m[:], 1.0 / D), w=["ones_m"])
        self.eps_rms = self.sb("eps_rms", [128, 1], F32)
        R.op("pool", lambda e: e.memset(self.eps_rms[:], 1e-6), w=["consts"])
        self.eps_lnx = self.sb("eps_lnx", [128, 1], F32)
        R.op("pool", lambda e: e.memset(self.eps_lnx[:], 64e-5), w=["consts"])
        if "rwkv" in self.phases:
            cd = self.dram_in("rconst", [128, 4 * 8 * 64 + 128 + 128])
            self.cmask = self.sb("cmask", [128, 4, 8, 64], BF16)
            self.identf = self.sb("identf", [128, 128], F32)
            self.identb = self.sb("identb", [128, 128], BF16)
            self.blk64 = self.sb("blk64", [128, 128], BF16)
            R.dma("pool", self.cmask[:].rearrange("p a b c -> p (a b c)"), cd[:, 0:2048], w=["consts"])
            R.dma("sp", self.identf[:], cd[:, 2048:2176], w=["consts"])
            R.dma("pool", self.identb[:], cd[:, 2048:2176], w=["consts"])
            R.dma("pool", self.blk64[:], cd[:, 2176:2304], w=["consts"])
            self.xspill = self.nc.dram_tensor("xspill", [D, T], F32, kind="Internal").ap()
            self.vfirst_d = self.nc.dram_tensor("vfirst", [D, T], F32, kind="Internal").ap()
        self.eps_ln = self.sb("eps_ln", [128, 1], F32)
        R.op("pool", lambda e: e.memset(self.eps_ln[:], 1e-5), w=["consts"])
        R.dma("sp", self.vecs[:], self.vecs_d[:], w=["vecs"])
        xv = self.xT_d.rearrange("(c p) t -> p c t", p=128)
        for c in range(NCH):
            R.dma("sp", self.x[:, c, :], xv[:, c, :], w=[("x", c, b) for b in range(len(BLOCKS))])

        with contextlib.ExitStack() as st:
            old, self.stack = self.stack, st
            for l in range(NLAYERS):
                if "conv" in self.phases and l % 2 == 0:
                    self.phase_conv(l)
                if "rwkv" in self.phases and l % 2 == 1:
                    self.phase_rwkv(l)
                if "attn" in self.phases:
                    self.phase_attn(l)
                if "mlp" in self.phases:
                    self.phase_mlp(l)
            self.final_norm()
            R.barrier()
            self.stack = old
        R.finish()
        return nc

    def rms_rstd(self, blk, rstd, tagps=7):
        R = self.R
        t0, n = BLOCKS[blk]
        sq = self.sq_scr
        ps = self.psb[tagps]
        R.op("act", lambda e: e.activation(out=sq[:, :, :n], in_=self.x[:, :, t0:t0 + n], func=AF.Square),
             r=[("x", c, blk) for c in range(NCH)], w=["sq_scr"])

        def mm(e):
            ins = None
            for c in range(NCH):
                ins = e.matmul(ps[:, :n], self.ones_m[:], sq[:, c, :n], start=(c == 0), stop=(c == NCH - 1))
            return ins
        R.op("pe", mm, r=["sq_scr", "ones_m"], w=[("ps", tagps)])
        R.op("act", lambda e: e.activation(out=rstd[:, :n], in_=ps[:, :n], func=AF.Sqrt, bias=self.eps_rms[:, 0:1]),
             r=[("ps", tagps), "consts"], w=["rstd"])
        R.op("dve", lambda e: e.reciprocal(out=rstd[:, :n], in_=rstd[:, :n]), r=["rstd"], w=["rstd"])

    def norm_block(self, blk, gkey, out_tile, out_res, out_off=None):
        R = self.R
        t0, n = BLOCKS[blk]
        off = t0 if out_off is None else out_off
        self.rms_rstd(blk, self.rstd)
        g = self.vec(gkey)

        def f(e):
            ins = None
            for c in range(NCH):
                ins = e.scalar_tensor_tensor(out=out_tile[:, c, off:off + n], in0=self.x[:, c, t0:t0 + n],
                                             scalar=g[:, c:c + 1], in1=self.rstd[:, :n],
                                             op0=ALU.mult, op1=ALU.mult)
            return ins
        R.op("dve", f, r=[("x", c, blk) for c in range(NCH)] + ["rstd", "vecs"], w=out_res)


    def phase_conv(self, l):
        R = self.R
        ci = l // 2
        with contextlib.ExitStack() as st:
            old, self.stack = self.stack, st
            self.sq_scr = self.sb("sq_scr", [128, NCH, 512], BF16)
            self.rstd = self.sb("rstd", [128, 512], F32)
            self.psb = [self.ps(f"psb{i}", [128, 512], F32) for i in range(8)]
            win = self.sb("win", [128, NCH, 2 * D], BF16)
            wout = self.sb("wout", [128, NCH, D], BF16)
            hbk = self.sb("hbk", [128, NCH, 512], BF16)
            up = self.sb("up", [128, NCH, NSEQ * 38], F32)
            ups = up[:].rearrange("p c (b t) -> p c b t", t=38)
            gate = [self.sb(f"gate{i}", [128, 512], F32) for i in range(2)]
            z = self.sb("z", [128, NCH, 512], F32)
            zb = self.sb("zb", [128, NCH, 512], BF16)
            mean = self.sb("mean", [128, 512], F32)
            var = self.sb("var", [128, 512], F32)
            mr = self.sb("mr", [128, 512], F32)
            tmp = [self.sb(f"tmpc{i}", [128, 512], F32) for i in range(2)]
            psb = self.psb
            win_v = self.dram_in("cv_w_in", [2, D, 2 * D])[ci].rearrange("(c p) e -> p c e", p=128)
            wout_v = self.dram_in("cv_w_out", [2, D, D])[ci].rearrange("(c p) e -> p c e", p=128)
            for j in range(4):
                R.dma("pool", win[:, :, j * 512:(j + 1) * 512], win_v[:, :, j * 512:(j + 1) * 512], w=["win"])
            for j in range(2):
                R.dma("pool", wout[:, :, j * 512:(j + 1) * 512], wout_v[:, :, j * 512:(j + 1) * 512], w=["wout"])
            sconv_d = self.dram_in("sconvT", [2, D, NSEQ, 30])
            convp_d = self.dram_out("convpT", [2, D, 30])
            convs_d = self.dram_out("convsT", [2, D, NSEQ, 30])
            R.op("pool", lambda e: e.memset(up[:, :, 0:30], 0.0), w=["up"])
            b1 = self.vec(("cv_b_in", ci, 0))
            b2 = self.vec(("cv_b_in", ci, 1))
            bdw = self.vec(("cv_b_dw", ci))
            lng = self.vec(("cv_ln_g", ci))
            lnb = self.vec(("cv_ln_b", ci))
            bout = self.vec(("cv_b_out", ci))
            wdw = [self.vec(("cv_w_dw", ci, j)) for j in range(31)]
            ng = 0
            for blk, (t0, n) in enumerate(BLOCKS):
                samp = blk == 4
                if samp:
                    for c in range(NCH):
                        R.dma("sp", ups[:, c, :, 0:30], sconv_d[ci, c * 128:(c + 1) * 128], r=["up"], w=["up"])
                self.norm_block(blk, ("norm_g", l, 0), hbk, ["hbk"], out_off=0)
                if 0 < blk < 4:
                    R.op("pool", lambda e: e.tensor_copy(out=up[:, :, 0:30], in_=up[:, :, 512:542]), r=["up"], w=["up"])
                for ec in range(NCH):
                    psA, psG = psb[(ec % 2) * 2], psb[(ec % 2) * 2 + 1]
                    pa, pg = (ec % 2) * 2, (ec % 2) * 2 + 1

                    def mm(e, psA=psA, psG=psG, ec=ec, n=n):
                        ins = None
                        for c in range(NCH):
                            ins = e.matmul(psA[:, :n], win[:, c, ec * 128:(ec + 1) * 128], hbk[:, c, :n],
                                           start=(c == 0), stop=(c == NCH - 1))
                        for c in range(NCH):
                            ins = e.matmul(psG[:, :n], win[:, c, D + ec * 128:D + (ec + 1) * 128], hbk[:, c, :n],
                                           start=(c == 0), stop=(c == NCH - 1))
                        return ins
                    R.op("pe", mm, r=["win", "hbk"], w=[("ps", pa), ("ps", pg)])
                    gb = ng % 2
                    ng += 1
                    R.op("act", lambda e, psG=psG, gb=gb, ec=ec, n=n: e.activation(
                        out=gate[gb][:, :n], in_=psG[:, :n], func=AF.Sigmoid, bias=b2[:, ec:ec + 1]),
                        r=[("ps", pg), "vecs"], w=[("gate", gb)])
                    if not samp:
                        R.op("dve", lambda e, psA=psA, gb=gb, ec=ec, n=n: e.scalar_tensor_tensor(
                            out=up[:, ec, 30:30 + n], in0=psA[:, :n], scalar=b1[:, ec:ec + 1], in1=gate[gb][:, :n],
                            op0=ALU.add, op1=ALU.mult), r=[("ps", pa), ("gate", gb), "vecs"], w=["up"])
                    else:
                        R.op("dve", lambda e, psA=psA, gb=gb, ec=ec, n=n: e.scalar_tensor_tensor(
                            out=ups[:, ec, :, 30:38], in0=psA[:, :n].rearrange("p (b t) -> p b t", t=8),
                            scalar=b1[:, ec:ec + 1], in1=gate[gb][:, :n].rearrange("p (b t) -> p b t", t=8),
                            op0=ALU.add, op1=ALU.mult), r=[("ps", pa), ("gate", gb), "vecs"], w=["up"])
                for c in range(NCH):
                    def src(j, c=c, n=n, samp=samp):
                        if samp:
                            return ups[:, c, :, j:j + 8]
                        return up[:, c, j:j + n]

                    def dst(tile, n=n, samp=samp):
                        if samp:
                            return tile[:, :n].rearrange("p (b t) -> p b t", t=8)
                        return tile[:, :n]

                    za = dst(z[:, c, :])
                    zo = dst(tmp[c % 2])
                    accs = [(za, ("z", c)), (zo, ("tmpc", c % 2))]
                    R.op("dve", lambda e, c=c, src=src, za=za: e.tensor_scalar(
                        out=za, in0=src(0), scalar1=wdw[0][:, c:c + 1], scalar2=bdw[:, c:c + 1], op0=ALU.mult, op1=ALU.add),
                        r=["up", "vecs"], w=[("z", c)])
                    R.op("dve", lambda e, c=c, src=src, zo=zo: e.tensor_scalar(
                        out=zo, in0=src(1), scalar1=wdw[1][:, c:c + 1], scalar2=None, op0=ALU.mult),
                        r=["up", "vecs"], w=[("tmpc", c % 2)])
                    for j in range(2, 31):
                        ac, acn = accs[j % 2]
                        R.op("dve", lambda e, c=c, j=j, src=src, ac=ac: e.scalar_tensor_tensor(
                            out=ac, in0=src(j), scalar=wdw[j][:, c:c + 1], in1=ac, op0=ALU.mult, op1=ALU.add),
                            r=["up", "vecs", acn], w=[acn])
                    R.op("dve", lambda e, za=za, zo=zo: e.tensor_tensor(out=za, in0=za, in1=zo, op=ALU.add),
                         r=[("z", c), ("tmpc", c % 2)], w=[("z", c)])
                    R.op("pool", lambda e, c=c, n=n: e.tensor_copy(out=zb[:, c, :n], in_=z[:, c, :n]), r=[("z", c)], w=[("zb", c)])
                    R.op("act", lambda e, c=c, n=n: e.activation(out=self.sq_scr[:, c, :n], in_=z[:, c, :n], func=AF.Square),
                         r=[("z", c)], w=["sq_scr"])
                if samp and DBG.get("dump"):
                    dd = self.dram_out("dbg_z", [128, NCH, 128])
                    R.dma("sp", dd[:], z[:, :, :128], r=[("z", c) for c in range(NCH)])
                    dd2 = self.dram_out("dbg_up", [128, NCH, NSEQ * 38])
                    R.dma("sp", dd2[:], up[:], r=["up"])
                def mmst(e, n=n):
                    ins = None
                    for c in range(NCH):
                        ins = e.matmul(psb[4][:, :n], self.ones_m[:], zb[:, c, :n], start=(c == 0), stop=(c == NCH - 1))
                    for c in range(NCH):
                        ins = e.matmul(psb[5][:, :n], self.ones_m[:], self.sq_scr[:, c, :n], start=(c == 0), stop=(c == NCH - 1))
                    return ins
                R.op("pe", mmst, r=["ones_m", "sq_scr"] + [("zb", c) for c in range(NCH)], w=[("ps", 4), ("ps", 5)])
                R.op("dve", lambda e, n=n: e.tensor_copy(out=mean[:, :n], in_=psb[4][:, :n]), r=[("ps", 4)], w=["mean"])
                R.op("dve", lambda e, n=n: e.tensor_tensor(out=var[:, :n], in0=mean[:, :n], in1=mean[:, :n], op=ALU.mult),
                     r=["mean"], w=["var"])
                R.op("dve", lambda e, n=n: e.tensor_tensor(out=var[:, :n], in0=psb[5][:, :n], in1=var[:, :n], op=ALU.subtract),
                     r=[("ps", 5), "var"], w=["var"])
                R.op("act", lambda e, n=n: e.activation(out=var[:, :n], in_=var[:, :n], func=AF.Sqrt, bias=self.eps_ln[:, 0:1]),
                     r=["var", "consts"], w=["var"])
                R.op("dve", lambda e, n=n: e.reciprocal(out=var[:, :n], in_=var[:, :n]), r=["var"], w=["var"])
                R.op("dve", lambda e, n=n: e.tensor_tensor(out=mr[:, :n], in0=mean[:, :n], in1=var[:, :n], op=ALU.mult),
                     r=["mean", "var"], w=["mr"])
                for c in range(NCH):
                    tb = c % 2
                    R.op("dve", lambda e, c=c, n=n, tb=tb: e.tensor_tensor(out=tmp[tb][:, :n], in0=z[:, c, :n], in1=var[:, :n],
                                                                           op=ALU.mult), r=[("z", c), "var"], w=[("tmpc", tb)])
                    R.op("dve", lambda e, c=c, n=n, tb=tb: e.tensor_tensor(out=tmp[tb][:, :n], in0=tmp[tb][:, :n], in1=mr[:, :n],
                                                                           op=ALU.subtract), r=[("tmpc", tb), "mr"], w=[("tmpc", tb)])
                    R.op("act", lambda e, c=c, n=n, tb=tb: e.activation(out=hbk[:, c, :n], in_=tmp[tb][:, :n], func=AF.Silu,
                                                                        bias=lnb[:, c:c + 1], scale=lng[:, c:c + 1]),
                         r=[("tmpc", tb), "vecs"], w=["hbk"])
                for ec in range(NCH):
                    pi = ec % 2
                    ps = psb[pi]

                    def mm(e, ps=ps, ec=ec, n=n):
                        ins = None
                        for c in range(NCH):
                            ins = e.matmul(ps[:, :n], wout[:, c, ec * 128:(ec + 1) * 128], hbk[:, c, :n],
                                           start=(c == 0), stop=(c == NCH - 1))
                        return ins
                    R.op("pe", mm, r=["wout", "hbk"], w=[("ps", pi)])
                    R.op("dve", lambda e, ps=ps, ec=ec, t0=t0, n=n: e.scalar_tensor_tensor(
                        out=self.x[:, ec, t0:t0 + n], in0=ps[:, :n], scalar=bout[:, ec:ec + 1], in1=self.x[:, ec, t0:t0 + n],
                        op0=ALU.add, op1=ALU.add), r=[("ps", pi), "vecs"], w=[("x", ec, blk)])
                if blk == 3:
                    for c in range(NCH):
                        R.dma("sp", convp_d[ci, c * 128:(c + 1) * 128, :], up[:, c, 512:542], r=["up"])
                if samp:
                    for c in range(NCH):
                        R.dma("sp", convs_d[ci, c * 128:(c + 1) * 128], ups[:, c, :, 8:38], r=["up"])
            R.barrier()
            self.stack = old


    def phase_rwkv(self, l):
        R = self.R
        nc = self.nc
        ri = l // 2
        NB = 128
        xs_d = self.xspill
        xs_v = xs_d.rearrange("(c p) t -> p c t", p=128)
        allx = [("x", c, b) for c in range(NCH) for b in range(len(BLOCKS))]
        for c in range(NCH):
            R.dma("sp", xs_v[:, c, :], self.x[:, c, :], r=allx)
        R.barrier()
        with contextlib.ExitStack() as st:
            old, self.stack = self.stack, st
            self.psb_save = self.psb
            def slot(j):
                return self.x[:, j // 2, (j % 2) * 1024:(j % 2) * 1024 + 1024]
            def bslot(j):
                return slot(j).rearrange("p (a b) -> p a b", b=NB)
            names = ["xb", "hf", "xx", "rf", "kf", "vf", "lw", "cs1", "cs2", "eNi", "af", "kkn", "yf"]
            Fb = {nm: bslot(i) for i, nm in enumerate(names)}
            SAV = slot(13)[:, 0:512].rearrange("p (c v) -> p c v", v=64)
            ytm = slot(14)[:, 0:512].rearrange("p (c v) -> p c v", v=64)
            yc = slot(15)[:, 0:512].rearrange("p (c v) -> p c v", v=64)
            bf = lambda nm, shape: self.sb(nm, shape, BF16)
            xm = [bf(f"xm{i}", [128, NCH, NB]) for i in range(3)]
            At, Bt, Kt, Rt, Vb, BWb, KWb, gfb, sqb = [bf(nm, [128, NCH, NB]) for nm in
                                                      ("At", "Bt", "Kt", "Rt", "Vb", "BWb", "KWb", "gfb", "sqb")]
            yg = xm[0]
            tw = bf("tw", [64, NB]); ta = bf("ta", [64, NB]); tv = bf("tv", [32, NB]); tg = bf("tg", [128, 2, NB])
            Pm, Qm, Tm, MKA, MBR, MKR, AtT, VT, BWT, KWT, XV, SAb = [
                bf(nm, [128, NCH, 64]) for nm in ("Pm", "Qm", "Tm", "MKA", "MBR", "MKR", "AtT", "VT", "BWT", "KWT", "XV", "SAb")]
            Ah = bf("Ah", [128, NCH, 64])
            Sf = self.sb("Sf", [128, NCH, 64], F32)
            Sb = bf("Sb", [128, NCH, 64])
            WC = self.sb("WC", [128, NCH, 16], F32)
            hlast = self.sb("hlast", [128, NCH, 16], F32)
            st1 = self.sb("st1", [128, NCH], F32)
            st2 = self.sb("st2", [128, NCH], F32)
            rnb = self.sb("rnb", [128, NB], F32)
            rnb4 = self.sb("rnb4", [128, 4 * NB], F32)
            wr, wk, wv, wo = [bf(nm, [128, NCH, D]) for nm in ("wr", "wk", "wv", "wo")]
            w1 = bf("w1", [128, NCH, 64]); w2 = bf("w2", [64, D])
            a1 = bf("a1", [128, NCH, 64]); a2 = bf("a2", [64, D])
            g1 = bf("g1", [128, NCH, 160]); g2 = bf("g2", [128, 2, D])
            if ri > 0:
                v1 = bf("v1", [128, NCH, 32]); v2 = bf("v2", [32, D])
            ps = [self.ps(f"rps{i}", [128, 512], F32) for i in range(8)]
            cm = self.cmask
            mSU, mSL, mU, mI = (cm[:, i] for i in range(4))

            def wload(dst, name, shape, view, nsplit=1):
                src = self.dram_in(name, shape)[ri if name not in ("rw_v1", "rw_v2") else 0]
                src = src.rearrange(view, p=128) if view else src
                if nsplit == 1:
                    R.dma("pool", dst, src, w=[name])
                else:
                    for j in range(nsplit):
                        R.dma("pool", dst[:, :, j * 512:(j + 1) * 512], src[:, :, j * 512:(j + 1) * 512], w=[name])
            wload(w1[:], "rw_w1", [2, D, 64], "(c p) e -> p c e")
            wload(w2[:], "rw_w2", [2, 64, D], None)
            wload(a1[:], "rw_a1", [2, D, 64], "(c p) e -> p c e")
            wload(a2[:], "rw_a2", [2, 64, D], None)
            wload(g1[:], "rw_g1", [2, D, 160], "(c p) e -> p c e")
            g2d = self.dram_in("rw_g2", [2, 160, D])[ri]
            R.dma("pool", g2[:, 0, :], g2d[0:128, :], w=["rw_g2"])
            R.dma("pool", g2[0:32, 1, :], g2d[128:160, :], w=["rw_g2"])
            if ri > 0:
                wload(v1[:], "rw_v1", [1, D, 32], "(c p) e -> p c e")
                wload(v2[:], "rw_v2", [1, 32, D], None)
            for t_, nm in ((wk, "rw_w_k"), (wv, "rw_w_v"), (wr, "rw_w_r"), (wo, "rw_w_o")):
                wload(t_[:], nm, [2, D, D], "(c p) e -> p c e", 2)
            WALL = ["rw_w_r", "rw_w_k", "rw_w_v", "rw_w_o", "rw_w1", "rw_w2", "rw_a1", "rw_a2", "rw_g1", "rw_g2",
                    "rw_v1", "rw_v2"]
            sshift_d = self.dram_in("sshiftT", [2, D, NSEQ])
            swkv_d = self.dram_in("swkvT", [2, NSEQ, 16, 64, 64])
            shiftp_d = self.dram_out("shiftpT", [2, 128, NCH])
            shifts_d = self.dram_out("shiftsT", [2, 128, NCH, NSEQ])
            wkvp_d = self.dram_out("wkvpT", [2, 16, 64, 64])
            wkvs_d = self.dram_out("wkvsT", [2, NSEQ, 16, 64, 64])
            vfd = self.vfirst_d.rearrange("(c p) t -> p c t", p=128)

            def V(key):
                return self.vec(key)[:, :].unsqueeze(2).to_broadcast([128, NCH, NB])

            def dve(fn, r, w):
                R.op("dve", fn, r=list(r) + ["vecs", "consts"], w=w)

            def act(fn, r, w):
                R.op("act", fn, r=list(r) + ["vecs", "consts"], w=w)

            npj = [0]

            def proj(W, wname, src, srcres, cols, evac):
                for g in range(cols // 512):
                    pi = 4 + npj[0] % 4
                    npj[0] += 1
                    p_ = ps[pi]

                    def mm(e, p_=p_, g=g):
                        ins = None
                        for j in range(4):
                            ec = g * 4 + j
                            for c in range(NCH):
                                ins = e.matmul(p_[:, j * NB:(j + 1) * NB], W[:, c, ec * 128:(ec + 1) * 128], src[:, c, :],
                                               start=(c == 0), stop=(c == NCH - 1))
                        return ins
                    R.op("pe", mm, r=[wname, srcres], w=[("rps", pi)])
                    evac(p_[:, :].rearrange("p (j t) -> p j t", t=NB), pi, g)

            R.op("pool", lambda e: e.memset(Sf[:], 0.0), w=["Sf"])
            R.op("pool", lambda e: e.memset(Sb[:], 0.0), w=["Sb"])
            R.op("pool", lambda e: e.memset(hlast[:], 0.0), w=["hlast"])

            nblk = 17

            def do_block(blk):
                samp = blk == 16
                t0 = blk * NB
                C = 8 if samp else 64
                NCK = NB // C
                LV = 2 if samp else 5
                xbufs = [Fb["xb"], self.x[:, :, 2048:2176]]
                xb, xbn = xbufs[blk % 2], ("xb", blk % 2)
                hf, xx = Fb["hf"], Fb["xx"]
                rf, kf, vf, lw, cs1, cs2 = Fb["rf"], Fb["kf"], Fb["vf"], Fb["lw"], Fb["cs1"], Fb["cs2"]
                eNi, af, kkn, yf = Fb["eNi"], Fb["af"], Fb["kkn"], Fb["yf"]
                if blk == blk_first:
                    R.dma("sp", xb, xs_v[:, :, t0:t0 + NB], w=[xbn])
                if blk + 1 < nblk and (blk + 1) in blk_set:
                    R.dma("sp", xbufs[(blk + 1) % 2], xs_v[:, :, t0 + NB:t0 + 2 * NB], w=[("xb", (blk + 1) % 2)])
                act(lambda e: e.activation(out=sqb[:], in_=xb, func=AF.Square), [xbn], ["sqb"])

                def mmn(e):
                    ins = None
                    for c in range(NCH):
                        ins = e.matmul(ps[6][:, :NB], self.ones_m[:], sqb[:, c, :], start=(c == 0), stop=(c == NCH - 1))
                    return ins
                R.op("pe", mmn, r=["sqb", "ones_m"], w=[("rps", 6)])
                act(lambda e: e.activation(out=rnb[:], in_=ps[6][:, :NB], func=AF.Sqrt, bias=self.eps_rms[:, 0:1]),
                    [("rps", 6)], ["rnb"])
                dve(lambda e: e.reciprocal(out=rnb[:], in_=rnb[:]), ["rnb"], ["rnb"])
                dve(lambda e: e.tensor_tensor(out=hf, in0=xb, in1=V(("norm_g", l, 0)), op=ALU.mult), [xbn], ["hf"])
                dve(lambda e: e.tensor_tensor(out=hf, in0=hf, in1=rnb[:, :].unsqueeze(1).to_broadcast([128, NCH, NB]),
                                              op=ALU.mult), ["hf", "rnb"], ["hf"])
                if samp:
                    for c in range(NCH):
                        R.dma("sp", hlast[:, c, :], sshift_d[ri, c * 128:(c + 1) * 128, :], w=["hlast"])
                    h4 = hf.rearrange("p c (b t) -> p c b t", t=8)
                    x4 = xx.rearrange("p c (b t) -> p c b t", t=8)
                    for c in range(NCH):
                        dve(lambda e, c=c: e.tensor_tensor(out=x4[:, c, :, 1:8], in0=h4[:, c, :, 0:7], in1=h4[:, c, :, 1:8],
                                                           op=ALU.subtract), ["hf"], ["xx"])
                        dve(lambda e, c=c: e.tensor_tensor(out=x4[:, c, :, 0], in0=hlast[:, c, :], in1=h4[:, c, :, 0],
                                                           op=ALU.subtract), ["hf", "hlast"], ["xx"])
                    for c in range(NCH):
                        R.dma("sp", shifts_d[ri, :, c, :], h4[:, c, :, 7], r=["hf"], allow_slow_non_contiguous=True)
                else:
                    dve(lambda e: e.tensor_tensor(out=xx[:, :, 1:NB], in0=hf[:, :, 0:NB - 1], in1=hf[:, :, 1:NB],
                                                  op=ALU.subtract), ["hf"], ["xx"])
                    dve(lambda e: e.tensor_tensor(out=xx[:, :, 0], in0=hlast[:, :, 0], in1=hf[:, :, 0], op=ALU.subtract),
                        ["hf", "hlast"], ["xx"])
                    dve(lambda e: e.tensor_copy(out=hlast[:, :, 0], in_=hf[:, :, NB - 1]), ["hf", "xx"], ["hlast"])
                    if blk == 15:
                        R.dma("sp", shiftp_d[ri], hlast[:, :, 0], r=["hlast"], allow_slow_non_contiguous=True)
                nmx = [0]

                def pool(fn, r, w):
                    R.op("pool", fn, r=list(r) + ["vecs", "consts"], w=w)

                def mix(i):
                    b = nmx[0] % 3
                    nmx[0] += 1
                    pool(lambda e: e.tensor_tensor(out=xm[b][:], in0=xx, in1=V(("rw_mix", ri, i)), op=ALU.mult),
                         ["xx"], [("xm", b)])
                    pool(lambda e: e.tensor_tensor(out=xm[b][:], in0=xm[b][:], in1=hf, op=ALU.add), ["hf", ("xm", b)], [("xm", b)])
                    return xm[b], ("xm", b)

                def lora1(W, wname, src, srcres, rank, dst, dstres, func):
                    pi = 4 + npj[0] % 4
                    npj[0] += 1
                    p_ = ps[pi]

                    def mm(e):
                        ins = None
                        for c in range(NCH):
                            ins = e.matmul(p_[:rank, :NB], W[:, c, :rank], src[:, c, :], start=(c == 0), stop=(c == NCH - 1))
                        return ins
                    R.op("pe", mm, r=[wname, srcres], w=[("rps", pi)])
                    if func is None:
                        dve(lambda e: e.tensor_copy(out=dst[:rank, :], in_=p_[:rank, :NB]), [("rps", pi)], [dstres])
                    else:
                        act(lambda e: e.activation(out=dst[:rank, :], in_=p_[:rank, :NB], func=func), [("rps", pi)], [dstres])

                def lora2(W2, wname, mid, midres, rank, evac):
                    for ec in range(NCH):
                        pi = 4 + npj[0] % 4
                        npj[0] += 1
                        p_ = ps[pi]
                        R.op("pe", lambda e, p_=p_, ec=ec: e.matmul(p_[:, :NB], W2[:rank, ec * 128:(ec + 1) * 128], mid[:rank, :],
                                                                    start=True, stop=True),
                             r=[wname, midres], w=[("rps", pi)])
                        evac(p_, pi, ec)

                xw, xwr = mix(1)
                xa, xar = mix(4)
                xk, xkr = mix(2)
                lora1(w1, "rw_w1", xw, xwr, 64, tw, "tw", AF.Tanh)
                w0v = self.vec(("rw_w0", ri))
                lora2(w2, "rw_w2", tw, "tw", 64, lambda p_, pi, ec: act(
                    lambda e: e.activation(out=lw[:, ec, :], in_=p_[:, :NB], func=AF.Sigmoid, bias=w0v[:, ec:ec + 1]),
                    [("rps", pi)], ["lw"]))
                pool(lambda e: e.tensor_scalar(out=lw, in0=lw, scalar1=-0.6065306597126334, scalar2=None, op0=ALU.mult),
                     ["lw"], ["lw"])
                lw4 = lw.rearrange("p c (k t) -> p c k t", t=C)
                a4 = cs1.rearrange("p c (k t) -> p c k t", t=C)
                b4 = cs2.rearrange("p c (k t) -> p c k t", t=C)
                src4, srcn = lw4, "lw"
                dsts = [(a4, "cs1"), (b4, "cs2")]
                sh = 1
                k_ = 0
                while sh < C:
                    d4, dn = dsts[k_ % 2]
                    pool(lambda e, d4=d4, src4=src4, sh=sh: e.tensor_tensor(
                        out=d4[:, :, :, sh:C], in0=src4[:, :, :, sh:C], in1=src4[:, :, :, 0:C - sh], op=ALU.add),
                        [srcn], [dn])
                    pool(lambda e, d4=d4, src4=src4, sh=sh: e.tensor_copy(out=d4[:, :, :, 0:sh], in_=src4[:, :, :, 0:sh]),
                        [srcn], [dn])
                    src4, srcn = d4, dn
                    sh *= 2
                    k_ += 1
                Li4, Lin = src4, srcn
                Li = cs1 if Lin == "cs1" else cs2
                Le, Len = (cs2, "cs2") if Lin == "cs1" else (cs1, "cs1")
                pool(lambda e: e.tensor_tensor(out=Le, in0=Li, in1=lw, op=ALU.subtract), [Lin, "lw"], [Len])
                lora1(a1, "rw_a1", xa, xar, 64, ta, "ta", None)
                a0v = self.vec(("rw_a0", ri))
                lora2(a2, "rw_a2", ta, "ta", 64, lambda p_, pi, ec: act(
                    lambda e: e.activation(out=af[:, ec, :], in_=p_[:, :NB], func=AF.Sigmoid, bias=a0v[:, ec:ec + 1]),
                    [("rps", pi)], ["af"]))
                proj(wk, "rw_w_k", xk, xkr, D, lambda p4, pi, g: dve(
                    lambda e: e.tensor_copy(out=kf[:, g * 4:(g + 1) * 4, :], in_=p4), [("rps", pi)], ["kf"]))
                dve(lambda e: e.tensor_tensor(out=kkn, in0=kf, in1=V(("rw_k_k", ri)), op=ALU.mult), ["kf"], ["kkn"])
                act(lambda e: e.activation(out=sqb[:], in_=kkn, func=AF.Square), ["kkn"], ["sqb"])
                for hf_ in range(2):
                    pi = 4 + npj[0] % 4
                    npj[0] += 1
                    p_ = ps[pi]

                    def mmk(e, p_=p_, hf_=hf_):
                        ins = None
                        for j in range(4):
                            ins = e.matmul(p_[:, j * NB:(j + 1) * NB], self.blk64[:], sqb[:, hf_ * 4 + j, :], start=True, stop=True)
                        return ins
                    R.op("pe", mmk, r=["sqb", "consts"], w=[("rps", pi)])
                    act(lambda e, p_=p_: e.activation(out=rnb4[:], in_=p_[:, :], func=AF.Sqrt), [("rps", pi)], ["rnb4"])
                    dve(lambda e: e.tensor_scalar(out=rnb4[:], in0=rnb4[:], scalar1=1e-12, scalar2=None, op0=ALU.max),
                        ["rnb4"], ["rnb4"])
                    dve(lambda e: e.reciprocal(out=rnb4[:], in_=rnb4[:]), ["rnb4"], ["rnb4"])
                    dve(lambda e, hf_=hf_: e.tensor_tensor(out=kkn[:, hf_ * 4:(hf_ + 1) * 4, :], in0=kkn[:, hf_ * 4:(hf_ + 1) * 4, :],
                                                           in1=rnb4[:, :].rearrange("p (j t) -> p j t", t=NB), op=ALU.mult),
                        ["kkn", "rnb4"], ["kkn"])
                xv, xvr = mix(3)
                xg, xgr = mix(5)
                xr, xrr = mix(0)
                proj(wv, "rw_w_v", xv, xvr, D, lambda p4, pi, g: dve(
                    lambda e: e.tensor_copy(out=vf[:, g * 4:(g + 1) * 4, :], in_=p4), [("rps", pi)], ["vf"]))
                if ri == 0:
                    R.dma("sp", vfd[:, :, t0:t0 + NB], vf, r=["vf"])
                else:
                    lora1(v1, "rw_v1", xv, xvr, 32, tv, "tv", None)
                    v0v = self.vec(("rw_v0", 0))
                    vg, vgn = xx, "xx"
                    tmpV, tmpVn = hf, "hf"
                    lora2(v2, "rw_v2", tv, "tv", 32, lambda p_, pi, ec: act(
                        lambda e: e.activation(out=vg[:, ec, :], in_=p_[:, :NB], func=AF.Sigmoid, bias=v0v[:, ec:ec + 1]),
                        [("rps", pi)], [vgn]))
                    R.dma("sp", tmpV, vfd[:, :, t0:t0 + NB], w=[tmpVn])
                    dve(lambda e: e.tensor_tensor(out=tmpV, in0=tmpV, in1=vf, op=ALU.subtract), [tmpVn, "vf"], [tmpVn])
                    dve(lambda e: e.tensor_tensor(out=tmpV, in0=tmpV, in1=vg, op=ALU.mult), [tmpVn, vgn], [tmpVn])
                    dve(lambda e: e.tensor_tensor(out=vf, in0=vf, in1=tmpV, op=ALU.add), [tmpVn, "vf"], ["vf"])
                dve(lambda e: e.tensor_copy(out=Vb[:], in_=vf), ["vf"], ["Vb"])
                for hf_ in range(2):
                    rk_ = 128 if hf_ == 0 else 32
                    pi = 4 + npj[0] % 4
                    npj[0] += 1
                    p_ = ps[pi]

                    def mmg(e, p_=p_, hf_=hf_, rk_=rk_):
                        ins = None
                        for c in range(NCH):
                            ins = e.matmul(p_[:rk_, :NB], g1[:, c, hf_ * 128:hf_ * 128 + rk_], xg[:, c, :],
                                           start=(c == 0), stop=(c == NCH - 1))
                        return ins
                    R.op("pe", mmg, r=["rw_g1", xgr], w=[("rps", pi)])
                    act(lambda e, p_=p_, hf_=hf_, rk_=rk_: e.activation(out=tg[:rk_, hf_, :], in_=p_[:rk_, :NB], func=AF.Sigmoid),
                        [("rps", pi)], ["tg"])
                for g in range(2):
                    pi = 4 + npj[0] % 4
                    npj[0] += 1
                    p_ = ps[pi]

                    def mmg2(e, p_=p_, g=g):
                        ins = None
                        for j in range(4):
                            ec = g * 4 + j
                            e.matmul(p_[:, j * NB:(j + 1) * NB], g2[:, 0, ec * 128:(ec + 1) * 128], tg[:, 0, :], start=True, stop=False)
                            ins = e.matmul(p_[:, j * NB:(j + 1) * NB], g2[:32, 1, ec * 128:(ec + 1) * 128], tg[:32, 1, :],
                                           start=False, stop=True)
                        return ins
                    R.op("pe", mmg2, r=["rw_g2", "tg"], w=[("rps", pi)])
                    dve(lambda e, p_=p_, g=g: e.tensor_copy(out=gfb[:, g * 4:(g + 1) * 4, :],
                                                            in_=p_[:, :].rearrange("p (j t) -> p j t", t=NB)),
                        [("rps", pi)], ["gfb"])

                act(lambda e: e.activation(out=WC[:, :, :NCK], in_=Li4[:, :, :, C - 1], func=AF.Exp), [Lin], ["WC"])
                act(lambda e: e.activation(out=eNi, in_=Li, func=AF.Exp, scale=-1.0), [Lin], ["eNi"])
                act(lambda e: e.activation(out=Le, in_=Le, func=AF.Exp), [Len], [Len])
                act(lambda e: e.activation(out=Li, in_=Li, func=AF.Exp), [Lin], [Lin])
                eLe, eLen, eLi, eLin = Le, Len, Li, Lin

                def ev_r(p4, pi, g):
                    dve(lambda e: e.tensor_copy(out=rf[:, g * 4:(g + 1) * 4, :], in_=p4), [("rps", pi)], ["rf"])
                    dve(lambda e: e.tensor_tensor(out=Rt[:, g * 4:(g + 1) * 4, :], in0=p4, in1=eLi[:, g * 4:(g + 1) * 4, :],
                                                  op=ALU.mult), [("rps", pi), eLin], ["Rt"])
                proj(wr, "rw_w_r", xr, xrr, D, ev_r)
                dve(lambda e: e.scalar_tensor_tensor(out=At[:], in0=kkn, scalar=-1.0, in1=eLe, op0=ALU.mult, op1=ALU.mult),
                    ["kkn", eLen], ["At"])
                tmpA, tmpAn = eLe, eLen
                dve(lambda e: e.scalar_tensor_tensor(out=tmpA, in0=af, scalar=-1.0, in1=V(("rw_k_a", ri)), op0=ALU.add, op1=ALU.mult),
                    ["af", "At"], [tmpAn])
                dve(lambda e: e.scalar_tensor_tensor(out=kf, in0=tmpA, scalar=1.0, in1=kf, op0=ALU.add, op1=ALU.mult),
                    [tmpAn, "kf"], ["kf"])
                dve(lambda e: e.tensor_tensor(out=kkn, in0=kkn, in1=af, op=ALU.mult), ["kkn", "af", "At"], ["kkn"])
                dve(lambda e: e.tensor_tensor(out=kkn, in0=kkn, in1=eNi, op=ALU.mult), ["kkn", "eNi"], ["kkn"])
                dve(lambda e: e.tensor_copy(out=Bt[:], in_=kkn), ["kkn"], ["Bt"])
                WCb = WC[:, :, :NCK].unsqueeze(3).to_broadcast([128, NCH, NCK, C])
                dve(lambda e: e.tensor_tensor(out=BWb[:].rearrange("p c (k t) -> p c k t", t=C),
                                              in0=kkn.rearrange("p c (k t) -> p c k t", t=C), in1=WCb, op=ALU.mult),
                    ["kkn", "WC"], ["BWb"])
                dve(lambda e: e.tensor_tensor(out=tmpA, in0=rf, in1=V(("rw_r_k", ri)), op=ALU.mult), ["rf", "kf"], [tmpAn])
                dve(lambda e: e.tensor_tensor(out=sqb[:], in0=tmpA, in1=kf, op=ALU.mult), [tmpAn, "kf", "kkn"], ["sqb"])
                dve(lambda e: e.tensor_tensor(out=tmpA, in0=kf, in1=eNi, op=ALU.mult), ["kf", "eNi", "sqb"], [tmpAn])
                dve(lambda e: e.tensor_copy(out=Kt[:], in_=tmpA), [tmpAn], ["Kt"])
                dve(lambda e: e.tensor_tensor(out=KWb[:].rearrange("p c (k t) -> p c k t", t=C),
                                              in0=tmpA.rearrange("p c (k t) -> p c k t", t=C), in1=WCb, op=ALU.mult),
                    [tmpAn, "WC"], ["KWb"])
                RL = [(0, 128)]

                def headmm2(pi, lhs, rhs, mrows, ncols, rres):
                    def f(e):
                        ins = None
                        for h in range(16):
                            pb, c = (h % 2) * 64, h // 2
                            ins = e.matmul(ps[pi][pb:pb + mrows, c * 64:c * 64 + ncols], lhs(pb, c), rhs(pb, c),
                                           start=True, stop=True)
                        return ins
                    R.op("pe", f, r=rres, w=[("rps", pi)])

                def pv(pi, ncols):
                    return ps[pi][:, :].rearrange("p (c t) -> p c t", t=64)[:, :, :ncols]

                def rows_op(fn, r, w):
                    for (r0, r1) in RL:
                        dve(lambda e, r0=r0, r1=r1: fn(e, r0, r1), r, w)

                def do_chunk(ck):
                    o = ck * C
                    fmx = lambda tile: (lambda pb, c: tile[pb:pb + 64, c, o:o + C])
                    tk = lambda tile: (lambda pb, c: tile[pb:pb + C, c, :C])
                    tvv = lambda tile: (lambda pb, c: tile[pb:pb + C, c, :])

                    def evm(dst, dstn, pi, mask):
                        rows_op(lambda e, r0, r1: e.tensor_tensor(out=dst[r0:r1, :, :C], in0=pv(pi, C)[r0:r1],
                                                                  in1=mask[r0:r1, :, :C], op=ALU.mult),
                                [("rps", pi)], [dstn])

                    def evc(dst, dstn, pi, ncols):
                        rows_op(lambda e, r0, r1: e.tensor_copy(out=dst[r0:r1, :, :ncols], in_=pv(pi, ncols)[r0:r1]),
                                [("rps", pi)], [dstn])

                    headmm2(0, fmx(Bt), fmx(At), C, C, ["Bt", "At"])
                    evm(Pm, "Pm", 0, mSU)
                    headmm2(1, fmx(At), fmx(Bt), C, C, ["Bt", "At"])
                    evm(Qm, "Qm", 1, mSL)
                    rows_op(lambda e, r0, r1: e.tensor_tensor(out=Tm[r0:r1, :, :C], in0=Pm[r0:r1, :, :C],
                                                              in1=mI[r0:r1, :, :C], op=ALU.add), ["Pm"], ["Tm"])
                    headmm2(2, fmx(Kt), fmx(At), C, C, ["Kt", "At"])
                    evm(MKA, "MKA", 2, mSU)
                    headmm2(3, fmx(Bt), fmx(Rt), C, C, ["Bt", "Rt"])
                    evm(MBR, "MBR", 3, mU)
                    headmm2(0, fmx(Kt), fmx(Rt), C, C, ["Kt", "Rt"])
                    evm(MKR, "MKR", 0, mU)
                    if DBG.get("rl", 9) < 2.2:
                        return
                    for n_ in range(1, LV + 1):
                        if n_ < LV:
                            headmm2(0, tk(Qm), tk(Pm), C, C, ["Pm", "Qm"])
                        headmm2(1, tk(Pm), tk(Qm), C, C, ["Pm", "Qm"])
                        if n_ < LV:
                            evc(Pm, "Pm", 0, C)
                        evc(Qm, "Qm", 1, C)
                        headmm2(2, tk(Qm), tk(Tm), C, C, ["Qm", "Tm"])
                        rows_op(lambda e, r0, r1: e.tensor_tensor(out=Tm[r0:r1, :, :C], in0=pv(2, C)[r0:r1],
                                                                  in1=Tm[r0:r1, :, :C], op=ALU.add),
                                [("rps", 2), "Tm"], ["Tm"])
                    if DBG.get("rl", 9) < 2.5:
                        return
                    ptv = pv(4, 64)
                    for srcT, srcn_, dstT, dstn_ in ((At, "At", AtT, "AtT"), (Vb, "Vb", VT, "VT"),
                                                     (BWb, "BWb", BWT, "BWT"), (KWb, "KWb", KWT, "KWT")):
                        def ftr(e, srcT=srcT):
                            ins = None
                            for h in range(16):
                                pb, c = (h % 2) * 64, h // 2
                                ins = e.matmul(ps[4][pb:pb + C, c * 64:(c + 1) * 64], srcT[pb:pb + 64, c, o:o + C],
                                               self.identb[pb:pb + 64, pb:pb + 64], start=True, stop=True)
                            return ins
                        R.op("pe", ftr, r=[srcn_, "consts"], w=[("rps", 4)])
                        rows_op(lambda e, r0, r1, dstT=dstT: e.tensor_copy(out=dstT[r0:r1, :, :], in_=ptv[r0:r1]),
                                [("rps", 4)], [dstn_])
                    headmm2(3, tvv(AtT), tk(Tm), 64, C, ["AtT", "Tm"])
                    dve(lambda e: e.tensor_copy(out=Ah[:, :, :C], in_=pv(3, C)), [("rps", 3)], ["Ah"])
                    if DBG.get("rl", 9) < 2.7:
                        return
                    headmm2(0, tk(MKA), tvv(VT), C, 64, ["MKA", "VT"])
                    evc(XV, "XV", 0, 64)
                    headmm2(1, tk(Tm), tvv(XV), C, 64, ["Tm", "XV"])
                    evc(SAV, "SAV", 1, 64)
                    if DBG.get("rl", 9) < 3:
                        return
                    if samp:
                        R.dma("sp", Sf[:], swkv_d[ri, ck].rearrange("(c h2) k v -> (h2 k) c v", h2=2), w=["Sf"])
                        dve(lambda e: e.tensor_copy(out=Sb[:], in_=Sf[:]), ["Sf"], ["Sb"])
                    fS = lambda pb, c: Sb[pb:pb + 64, c, :]
                    headmm2(2, lambda pb, c: Ah[pb:pb + 64, c, :C], fS, C, 64, ["Ah", "Sb"])
                    rows_op(lambda e, r0, r1: e.tensor_tensor(out=SAb[r0:r1, :, :], in0=pv(2, 64)[r0:r1],
                                                              in1=SAV[r0:r1, :, :], op=ALU.add),
                            [("rps", 2), "SAV"], ["SAb"])

                    def fy(e):
                        ins = None
                        for h in range(16):
                            pb, c = (h % 2) * 64, h // 2
                            o_ = ps[3][pb:pb + C, c * 64:(c + 1) * 64]
                            e.matmul(o_, Rt[pb:pb + 64, c, o:o + C], Sb[pb:pb + 64, c, :], start=True, stop=False)
                            e.matmul(o_, MBR[pb:pb + C, c, :C], SAb[pb:pb + C, c, :], start=False, stop=False)
                            ins = e.matmul(o_, MKR[pb:pb + C, c, :C], VT[pb:pb + C, c, :], start=False, stop=True)
                        return ins
                    R.op("pe", fy, r=["Rt", "Sb", "MBR", "SAb", "MKR", "VT"], w=[("rps", 3)])
                    evc(ytm, "ytm", 3, 64)

                    def fs(e):
                        ins = None
                        for h in range(16):
                            pb, c = (h % 2) * 64, h // 2
                            o_ = ps[0][pb:pb + 64, c * 64:(c + 1) * 64]
                            e.matmul(o_, BWT[pb:pb + C, c, :], SAb[pb:pb + C, c, :], start=True, stop=False)
                            ins = e.matmul(o_, KWT[pb:pb + C, c, :], VT[pb:pb + C, c, :], start=False, stop=True)
                        return ins
                    R.op("pe", fs, r=["BWT", "SAb", "KWT", "VT"], w=[("rps", 0)])
                    dve(lambda e: e.tensor_tensor(out=Sf[:], in0=Sf[:], in1=WC[:, :, ck:ck + 1].to_broadcast([128, NCH, 64]),
                                                  op=ALU.mult), ["Sf", "WC"], ["Sf"])
                    dve(lambda e: e.tensor_tensor(out=Sf[:], in0=Sf[:], in1=pv(0, 64), op=ALU.add), ["Sf", ("rps", 0)], ["Sf"])
                    dve(lambda e: e.tensor_copy(out=Sb[:], in_=Sf[:]), ["Sf"], ["Sb"])
                    if samp:
                        R.dma("sp", wkvs_d[ri, ck].rearrange("(c h2) k v -> (h2 k) c v", h2=2), Sf[:], r=["Sf"])
                    elif blk == 15 and ck == NCK - 1:
                        R.dma("sp", wkvp_d[ri].rearrange("(c h2) k v -> (h2 k) c v", h2=2), Sf[:], r=["Sf"])
                    if DBG.get("rl", 9) < 4:
                        return
                    rows_op(lambda e, r0, r1: e.tensor_reduce(out=st1[r0:r1, :], in_=ytm[r0:r1], axis=AX.X, op=ALU.add),
                            ["ytm"], ["st1"])
                    rows_op(lambda e, r0, r1: e.tensor_scalar(out=st1[r0:r1, :], in0=st1[r0:r1, :], scalar1=1.0 / 64,
                                                              scalar2=None, op0=ALU.mult), ["st1"], ["st1"])
                    rows_op(lambda e, r0, r1: e.tensor_tensor(out=yc[r0:r1], in0=ytm[r0:r1],
                                                              in1=st1[r0:r1, :].unsqueeze(2).to_broadcast([r1 - r0, NCH, 64]),
                                                              op=ALU.subtract), ["ytm", "st1"], ["yc"])
                    rows_op(lambda e, r0, r1: e.tensor_tensor(out=ytm[r0:r1], in0=yc[r0:r1], in1=yc[r0:r1], op=ALU.mult),
                            ["yc"], ["ytm"])
                    rows_op(lambda e, r0, r1: e.tensor_reduce(out=st2[r0:r1, :], in_=ytm[r0:r1], axis=AX.X, op=ALU.add),
                            ["ytm"], ["st2"])
                    for (r0, r1) in RL:
                        act(lambda e, r0=r0, r1=r1: e.activation(out=st2[r0:r1, :], in_=st2[r0:r1, :], func=AF.Sqrt,
                                                                 scale=1.0 / 64, bias=self.eps_lnx[r0:r1, 0:1]),
                            ["st2"], ["st2"])
                    rows_op(lambda e, r0, r1: e.reciprocal(out=st2[r0:r1, :], in_=st2[r0:r1, :]), ["st2"], ["st2"])
                    rows_op(lambda e, r0, r1: e.tensor_tensor(out=yc[r0:r1], in0=yc[r0:r1],
                                                              in1=st2[r0:r1, :].unsqueeze(2).to_broadcast([r1 - r0, NCH, 64]),
                                                              op=ALU.mult), ["yc", "st2"], ["yc"])

                    def ftb(e):
                        ins = None
                        for h in range(16):
                            pb, c = (h % 2) * 64, h // 2
                            ins = e.matmul(ps[1][pb:pb + 64, c * 64:c * 64 + C], yc[pb:pb + C, c, :],
                                           self.identf[pb:pb + C, pb:pb + C], start=True, stop=True)
                        return ins
                    R.op("pe", ftb, r=["yc", "consts"], w=[("rps", 1)])
                    dve(lambda e: e.tensor_copy(out=yf[:, :, o:o + C], in_=pv(1, C)), [("rps", 1)], ["yf"])

                for ck in range(NCK if DBG.get("rl", 9) >= 2 else 0):
                    do_chunk(ck)
                dve(lambda e: e.tensor_tensor(out=yf, in0=yf, in1=V(("rw_lnx_g", ri)), op=ALU.mult), ["yf"], ["yf"])
                dve(lambda e: e.tensor_tensor(out=yf, in0=yf, in1=V(("rw_lnx_b", ri)), op=ALU.add), ["yf"], ["yf"])
                for hf_ in range(2):
                    pi = 4 + npj[0] % 4
                    npj[0] += 1
                    p_ = ps[pi]

                    def mmb(e, p_=p_, hf_=hf_):
                        ins = None
                        for j in range(4):
                            ins = e.matmul(p_[:, j * NB:(j + 1) * NB], self.blk64[:], sqb[:, hf_ * 4 + j, :], start=True, stop=True)
                        return ins
                    R.op("pe", mmb, r=["sqb", "consts"], w=[("rps", pi)])
                    dve(lambda e, p_=p_, hf_=hf_: e.tensor_tensor(out=rnb4[:, :].rearrange("p (j t) -> p j t", t=NB),
                                                                  in0=p_[:, :].rearrange("p (j t) -> p j t", t=NB),
                                                                  in1=vf[:, hf_ * 4:(hf_ + 1) * 4, :], op=ALU.mult),
                        [("rps", pi), "vf"], ["rnb4"])
                    dve(lambda e, hf_=hf_: e.tensor_tensor(out=yf[:, hf_ * 4:(hf_ + 1) * 4, :], in0=yf[:, hf_ * 4:(hf_ + 1) * 4, :],
                                                           in1=rnb4[:, :].rearrange("p (j t) -> p j t", t=NB), op=ALU.add),
                        ["yf", "rnb4"], ["yf"])
                dve(lambda e: e.tensor_tensor(out=yg[:], in0=yf, in1=gfb[:], op=ALU.mult), ["yf", "gfb"], [("xm", 0)])

                def ev_o(p4, pi, g):
                    dve(lambda e: e.tensor_tensor(out=xb[:, g * 4:(g + 1) * 4, :], in0=xb[:, g * 4:(g + 1) * 4, :], in1=p4,
                                                  op=ALU.add), [("rps", pi), xbn], [xbn])
                proj(wo, "rw_w_o", yg, ("xm", 0), D, ev_o)
                R.dma("sp", xs_v[:, :, t0:t0 + NB], xb, r=[xbn])
            blk_list = list(DBG.get("blks", range(nblk)) if DBG.get("rl", 9) >= 1 else [])
            blk_set = set(blk_list)
            blk_first = blk_list[0] if blk_list else -1
            for blk in blk_list:
                do_block(blk)
            R.barrier()
            self.psb = self.psb_save
            self.stack = old
        for c in range(NCH):
            R.dma("sp", self.x[:, c, :], xs_v[:, c, :], w=[("x", c, b) for b in range(len(BLOCKS))])
        R.barrier()

    def phase_attn(self, l):
        R = self.R
        with contextlib.ExitStack() as st:
            old, self.stack = self.stack, st
            self.sq_scr = self.sb("sq_scr", [128, NCH, 512], BF16)
            self.rstd = self.sb("rstd", [128, 512], F32)
            self.psb = [self.ps(f"psb{i}", [128, 512], F32) for i in range(8)]
            wq = self.sb("wq", [128, NCH, D], BF16)
            wo = self.sb("wo", [128, NCH, D], BF16)
            KTp = self.sb("KTp", [128, NCH, MEM], BF16)
            Vp = self.sb("Vp", [128, 2, D], BF16)
            hbk = self.sb("hbk", [128, NCH, 512], BF16)
            qT = self.sb("qT", [128, NCH, 512], BF16)
            oT = self.sb("oT", [128, NCH, 512], BF16)
            PT = [self.sb(f"PT{i}", [128, 2, 512], BF16) for i in range(2)]
            rden = self.sb("rden", [128, 512], F32)
            wkvb = [self.sb(f"wkvb{i}", [128, NCH, 512], BF16) for i in range(2)]
            kvo = [self.sb(f"kvo{i}", [128, 512], F32) for i in range(2)]
            KTs = [self.sb(f"KTs{i}", [128, NCH, MEM], BF16) for i in range(2)]
            Vs = [self.sb(f"Vs{i}", [128, 2, D], BF16) for i in range(2)]
            PTs = self.sb("PTs", [128, 8, 8], BF16)
            rdens = self.sb("rdens", [128, 4, 8], F32)
            psb = self.psb

            wkv_v = self.wkv_d[l].rearrange("(c p) e -> p c e", p=128)
            nk = 0
            for j in range(4):
                b = j % 2
                R.dma("pool", wkvb[b][:], wkv_v[:, :, j * 512:(j + 1) * 512], w=[("wkvb", b)])
                for mt in range(2):
                    ps = psb[mt]

                    def mm(e, ps=ps, b=b, mt=mt):
                        ins = None
                        for c in range(NCH):
                            ins = e.matmul(ps[:, :], self.memT[:, c, mt * 128:(mt + 1) * 128], wkvb[b][:, c, :],
                                           start=(c == 0), stop=(c == NCH - 1))
                        return ins
                    R.op("pe", mm, r=["memT", ("wkvb", b)], w=[("ps", mt)])
                    kb = nk % 2
                    nk += 1
                    R.op("dve", lambda e, ps=ps, kb=kb: e.tensor_copy(out=kvo[kb][:], in_=ps[:]),
                         r=[("ps", mt)], w=[("kvo", kb)])
                    dst = self.memk_d if j < 2 else self.memv_d
                    R.dma("sp", dst[l, mt * 128:(mt + 1) * 128, (j % 2) * 512:(j % 2 + 1) * 512], kvo[kb][:],
                          r=[("kvo", kb)])
                    if j >= 2:
                        R.op("dve", lambda e, ps=ps, mt=mt, j=j: e.tensor_copy(
                            out=Vp[:, mt, (j - 2) * 512:(j - 1) * 512], in_=ps[:]),
                            r=[("ps", mt)], w=["Vp"])
                if j < 2:
                    for ec in range(4):
                        pi = 2 + ec % 2
                        ps = psb[pi]

                        def mm(e, ps=ps, b=b, ec=ec):
                            ins = None
                            for c in range(NCH):
                                ins = e.matmul(ps[:, :MEM], wkvb[b][:, c, ec * 128:(ec + 1) * 128], self.memT[:, c, :],
                                               start=(c == 0), stop=(c == NCH - 1))
                            return ins
                        R.op("pe", mm, r=["memT", ("wkvb", b)], w=[("ps", pi)])
                        R.op("dve", lambda e, ps=ps, j=j, ec=ec: e.tensor_copy(out=KTp[:, j * 4 + ec, :], in_=ps[:, :MEM]),
                             r=[("ps", pi)], w=["KTp"])

            for hf in range(2):
                R.dma("pool", wq[:, :, hf * 512:(hf + 1) * 512],
                      self.wq_d[l].rearrange("(c p) e -> p c e", p=128)[:, :, hf * 512:(hf + 1) * 512], w=["wq"])
                R.dma("pool", wo[:, :, hf * 512:(hf + 1) * 512],
                      self.wo_d[l].rearrange("(c p) e -> p c e", p=128)[:, :, hf * 512:(hf + 1) * 512], w=["wo"])

            npt = 0
            lvl = DBG.get("lvl", 9)
            for blk, (t0, n) in enumerate(BLOCKS if lvl >= 2 else []):
                self.norm_block(blk, ("norm_g", l, 1), hbk, ["hbk"], out_off=0)
                for ec in range(NCH):
                    pi = ec % 2
                    ps = psb[pi]

                    def mm(e, ps=ps, ec=ec, n=n):
                        ins = None
                        for c in range(NCH):
                            ins = e.matmul(ps[:, :n], wq[:, c, ec * 128:(ec + 1) * 128], hbk[:, c, :n],
                                           start=(c == 0), stop=(c == NCH - 1))
                        return ins
                    R.op("pe", mm, r=["wq", "hbk"], w=[("ps", pi)])
                    R.op("dve", lambda e, ps=ps, ec=ec, n=n: e.tensor_copy(out=qT[:, ec, :n], in_=ps[:, :n]),
                         r=[("ps", pi)], w=[("qT", ec)])
                if lvl < 3:
                    continue
                if blk < 4:
                    for h in range(4):
                        pb = npt % 2
                        npt += 1
                        for mt in range(2):
                            pi = 2 + mt
                            ps = psb[pi]

                            def mm(e, ps=ps, h=h, mt=mt, n=n):
                                ins = None
                                for dc in range(2):
                                    ins = e.matmul(ps[:, :n], KTp[:, h * 2 + dc, mt * 128:(mt + 1) * 128],
                                                   qT[:, h * 2 + dc, :n], start=(dc == 0), stop=(dc == 1))
                                return ins
                            R.op("pe", mm, r=["KTp", ("qT", h * 2), ("qT", h * 2 + 1)], w=[("ps", pi)])
                            R.op("act", lambda e, ps=ps, pb=pb, mt=mt, n=n: e.activation(
                                out=PT[pb][:, mt, :n], in_=ps[:, :n], func=AF.Exp, scale=1.0 / 16.0),
                                r=[("ps", pi)], w=[("PT", pb, mt)])
                        ps4 = psb[4]

                        def mmd(e, pb=pb, n=n):
                            ins = None
                            for mt in range(2):
                                ins = e.matmul(ps4[:, :n], self.ones_1[:], PT[pb][:, mt, :n], start=(mt == 0), stop=(mt == 1))
                            return ins
                        R.op("pe", mmd, r=["ones_1", ("PT", pb, 0), ("PT", pb, 1)], w=[("ps", 4)])
                        R.op("dve", lambda e, n=n: e.reciprocal(out=rden[:, :n], in_=ps4[:, :n]), r=[("ps", 4)], w=["rden"])
                        for dc in range(2):
                            pi = 5 + dc
                            ps = psb[pi]

                            def mmv(e, ps=ps, h=h, dc=dc, pb=pb, n=n):
                                ins = None
                                for mt in range(2):
                                    ins = e.matmul(ps[:, :n], Vp[:, mt, h * 256 + dc * 128:h * 256 + (dc + 1) * 128],
                                                   PT[pb][:, mt, :n], start=(mt == 0), stop=(mt == 1))
                                return ins
                            R.op("pe", mmv, r=["Vp", ("PT", pb, 0), ("PT", pb, 1)], w=[("ps", pi)])
                            R.op("dve", lambda e, ps=ps, h=h, dc=dc, n=n: e.tensor_tensor(
                                out=oT[:, h * 2 + dc, :n], in0=ps[:, :n], in1=rden[:, :n], op=ALU.mult),
                                r=[("ps", pi), "rden"], w=[("oT", h * 2 + dc)])
                else:
                    for sb_ in range(DBG.get("nseq", NSEQ)):
                        kb = sb_ % 2
                        R.dma("pool", KTs[kb][:].rearrange("p a m -> p (a m)"), self.cKT_d[l, sb_], w=[("KTs", kb)])
                        R.dma("pool", Vs[kb][:].rearrange("p a e -> p (a e)"), self.cV_d[l, sb_], w=[("Vs", kb)])
                        c0 = sb_ * 8
                        ps = psb[2 + sb_ % 2]
                        pi = 2 + sb_ % 2

                        def mms(e, ps=ps, kb=kb, c0=c0):
                            ins = None
                            for h in range(4):
                                for mt in range(2):
                                    for dc in range(2):
                                        ins = e.matmul(ps[:, (h * 2 + mt) * 8:(h * 2 + mt + 1) * 8],
                                                       KTs[kb][:, h * 2 + dc, mt * 128:(mt + 1) * 128],
                                                       qT[:, h * 2 + dc, c0:c0 + 8], start=(dc == 0), stop=(dc == 1))
                            return ins
                        R.op("pe", mms, r=[("KTs", kb)] + [("qT", c) for c in range(NCH)], w=[("ps", pi)])
                        R.op("act", lambda e, ps=ps: e.activation(out=PTs[:].rearrange("p a b -> p (a b)"), in_=ps[:, :64],
                                                                   func=AF.Exp, scale=1.0 / 16.0),
                             r=[("ps", pi)], w=["PTs"])
                        ps4 = psb[4]

                        def mmd(e):
                            ins = None
                            for h in range(4):
                                for mt in range(2):
                                    ins = e.matmul(ps4[:, h * 8:(h + 1) * 8], self.ones_1[:], PTs[:, h * 2 + mt, :],
                                                   start=(mt == 0), stop=(mt == 1))
                            return ins
                        R.op("pe", mmd, r=["ones_1", "PTs"], w=[("ps", 4)])
                        R.op("dve", lambda e: e.reciprocal(out=rdens[:].rearrange("p a b -> p (a b)"), in_=ps4[:, :32]),
                             r=[("ps", 4)], w=["rdens"])
                        pi2 = 5 + sb_ % 2
                        psv = psb[pi2]

                        def mmv(e, psv=psv, kb=kb):
                            ins = None
                            for h in range(4):
                                for dc in range(2):
                                    for mt in range(2):
                                        ins = e.matmul(psv[:, (h * 2 + dc) * 8:(h * 2 + dc + 1) * 8],
                                                       Vs[kb][:, mt, h * 256 + dc * 128:h * 256 + (dc + 1) * 128],
                                                       PTs[:, h * 2 + mt, :], start=(mt == 0), stop=(mt == 1))
                            return ins
                        R.op("pe", mmv, r=[("Vs", kb), "PTs"], w=[("ps", pi2)])

                        def nrm(e, psv=psv, c0=c0):
                            ins = None
                            for h in range(4):
                                for dc in range(2):
                                    ins = e.tensor_tensor(out=oT[:, h * 2 + dc, c0:c0 + 8],
                                                          in0=psv[:, (h * 2 + dc) * 8:(h * 2 + dc + 1) * 8],
                                                          in1=rdens[:, h, :], op=ALU.mult)
                            return ins
                        R.op("dve", nrm, r=[("ps", pi2), "rdens"], w=[("oT", c) for c in range(NCH)])
                for ec in range(NCH):
                    pi = ec % 2
                    ps = psb[pi]

                    def mm(e, ps=ps, ec=ec, n=n):
                        ins = None
                        for c in range(NCH):
                            ins = e.matmul(ps[:, :n], wo[:, c, ec * 128:(ec + 1) * 128], oT[:, c, :n],
                                           start=(c == 0), stop=(c == NCH - 1))
                        return ins
                    R.op("pe", mm, r=["wo"] + [("oT", c) for c in range(NCH)], w=[("ps", pi)])
                    R.op("dve", lambda e, ps=ps, ec=ec, t0=t0, n=n: e.tensor_tensor(
                        out=self.x[:, ec, t0:t0 + n], in0=self.x[:, ec, t0:t0 + n], in1=ps[:, :n], op=ALU.add),
                        r=[("ps", pi)], w=[("x", ec, blk)])
            R.barrier()
            self.stack = old

    def phase_mlp(self, l):
        R = self.R
        with contextlib.ExitStack() as st:
            old, self.stack = self.stack, st
            self.sq_scr = self.sb("sq_scr", [128, NCH, 512], BF16)
            self.rstd = self.sb("rstd", [128, 512], F32)
            self.psb = [self.ps(f"psb{i}", [128, 512], F32) for i in range(8)]
            self.hb = self.sb("hb", [128, NCH, T], BF16)
            wu = [self.sb(f"wu{i}", [128, NCH, 512], BF16) for i in range(2)]
            wd = [self.sb(f"wd{i}", [128, 4, D], BF16) for i in range(2)]
            hT = [self.sb(f"hT{i}", [128, 4, 512], BF16) for i in range(2)]
            rl = [self.sb(f"rl{i}", [128, 512], F32) for i in range(2)]
            wu_v = self.w_up_d[l].rearrange("(c p) f -> p c f", p=128)
            wd_v = self.w_dn_d[l].rearrange("(fc p) e -> p fc e", p=128)

            for blk in range(len(BLOCKS)):
                self.norm_block(blk, ("norm_g", l, 2), self.hb, [("hb", blk)])

            nrl = 0
            for j in range(8):
                b = j % 2
                R.dma("pool", wu[b][:], wu_v[:, :, j * 512:(j + 1) * 512], w=[("wu", b)])
                R.dma("pool", wd[b][:], wd_v[:, j * 4:(j + 1) * 4, :], w=[("wd", b)])
                for blk, (t0, n) in enumerate(BLOCKS):
                    hb_ = (j * len(BLOCKS) + blk) % 2
                    for fc in range(4):
                        ps = self.psb[fc]

                        def mm(e, ps=ps, fc=fc, t0=t0, n=n, b=b):
                            ins = None
                            for c in range(NCH):
                                ins = e.matmul(ps[:, :n], wu[b][:, c, fc * 128:(fc + 1) * 128],
                                               self.hb[:, c, t0:t0 + n], start=(c == 0), stop=(c == NCH - 1))
                            return ins
                        R.op("pe", mm, r=[("wu", b), ("hb", blk)], w=[("ps", fc)])
                        rb = nrl % 2
                        nrl += 1
                        R.op("act", lambda e, ps=ps, rb=rb, n=n: e.activation(out=rl[rb][:, :n], in_=ps[:, :n],
                                                                               func=AF.Relu),
                             r=[("ps", fc)], w=[("rl", rb)])
                        R.op("pool", lambda e, rb=rb, fc=fc, n=n, hb_=hb_: e.tensor_tensor(
                            out=hT[hb_][:, fc, :n], in0=rl[rb][:, :n], in1=rl[rb][:, :n], op=ALU.mult),
                            r=[("rl", rb)], w=[("hT", hb_, fc)])
                    for oc in range(NCH):
                        pi = 4 + oc % 4
                        ps = self.psb[pi]

                        def mm2(e, ps=ps, oc=oc, n=n, b=b, hb_=hb_):
                            ins = None
                            for fc in range(4):
                                ins = e.matmul(ps[:, :n], wd[b][:, fc, oc * 128:(oc + 1) * 128],
                                               hT[hb_][:, fc, :n], start=(fc == 0), stop=(fc == 3))
                            return ins
                        R.op("pe", mm2, r=[("wd", b)] + [("hT", hb_, fc) for fc in range(4)], w=[("ps", pi)])
                        R.op("dve", lambda e, ps=ps, oc=oc, t0=t0, n=n: e.tensor_tensor(
                            out=self.x[:, oc, t0:t0 + n], in0=self.x[:, oc, t0:t0 + n], in1=ps[:, :n], op=ALU.add),
                            r=[("ps", pi)], w=[("x", oc, blk)])
            R.barrier()
            self.stack = old

    def final_norm(self):
        R = self.R
        with contextlib.ExitStack() as st:
            old, self.stack = self.stack, st
            self.sq_scr = self.sb("sq_scr", [128, NCH, 512], BF16)
            self.rstd = self.sb("rstd", [128, 512], F32)
            self.psb = [self.ps(f"psb{i}", [128, 512], F32) for i in range(8)]
            yb = [self.sb(f"yb{i}", [128, NCH, 512], F32) for i in range(2)]
            yv = self.yT_d.rearrange("(c p) t -> p c t", p=128)
            for blk, (t0, n) in enumerate(BLOCKS):
                b = blk % 2
                self.norm_block(blk, ("final_g",), yb[b], [("yb", b)], out_off=0)
                R.dma("sp", yv[:, :, t0:t0 + n], yb[b][:, :, :n], r=[("yb", b)])
            R.barrier()
            self.stack = old


_CACHE = {}
PHASES = ("conv", "rwkv", "attn", "mlp")
NLAYERS = DEPTH
DBG = {}


def get_nc(phases):
    key = tuple(phases)
    if key not in _CACHE:
        b = Builder(phases)
        nc = b.build()
        _CACHE[key] = (nc, set(b.dins), set(b.douts))
    return _CACHE[key]


def make_rconst():
    i = np.arange(64)
    su = (i[:, None] < i[None, :]).astype(np.float32)
    sl_ = (i[:, None] > i[None, :]).astype(np.float32)
    u = (i[:, None] <= i[None, :]).astype(np.float32)
    ey = np.eye(64, dtype=np.float32)
    cm = np.stack([np.broadcast_to(m[:, None, :], (64, 8, 64)) for m in (su, sl_, u, ey)], axis=1)
    out = np.zeros((128, 2048 + 256), np.float32)
    out[:64, :2048] = cm.reshape(64, 2048)
    out[64:, :2048] = cm.reshape(64, 2048)
    out[:, 2048:2176] = np.eye(128, dtype=np.float32)
    out[:64, 2176:2240] = 1.0
    out[64:, 2240:2304] = 1.0
    return out


def extra_inputs(inp, c, sl):
    f32 = np.float32
    return {
        "rconst": lambda: make_rconst(),
        "sshiftT": lambda: np.ascontiguousarray(np.asarray(inp["state_shift"][:, sl], f32).transpose(0, 2, 1)),
        "swkvT": lambda: np.ascontiguousarray(np.asarray(inp["state_wkv"][:, sl], f32).transpose(0, 1, 2, 4, 3)),
        **{nm: (lambda nm=nm: np.asarray(inp[nm], f32)) for nm in
           ("rw_w_r", "rw_w_k", "rw_w_v", "rw_w_o", "rw_w1", "rw_w2", "rw_a1", "rw_a2", "rw_g1", "rw_g2", "rw_v1", "rw_v2")},
        "cv_w_in": lambda: np.asarray(inp["cv_w_in"], f32),
        "cv_w_out": lambda: np.asarray(inp["cv_w_out"], f32),
        "sconvT": lambda: np.ascontiguousarray(np.asarray(inp["state_conv"][:, sl], f32).transpose(0, 3, 1, 2)),
    }


def make_in_maps(inp, used, n_cores=8):
    vecs = pack_vecs(inp)
    maps = []
    for c in range(n_cores):
        xp = np.asarray(inp["x_prompt"][c], np.float32)
        xs = np.asarray(inp["x_sample"][c * NSEQ:(c + 1) * NSEQ], np.float32).reshape(TS, D)
        xT = np.ascontiguousarray(np.concatenate([xp, xs], axis=0).T)
        sl = slice(c * NSEQ, (c + 1) * NSEQ)
        sl = slice(c * NSEQ, (c + 1) * NSEQ)
        f32 = np.float32
        m = {"xT": lambda: xT, "vecs": lambda: vecs,
             "mlp_w_up": lambda: np.asarray(inp["mlp_w_up"], f32),
             "mlp_w_down": lambda: np.asarray(inp["mlp_w_down"], f32),
             "memT": lambda: np.ascontiguousarray(np.asarray(inp["mem_prompt"][c], f32).T),
             "xa_w_kv": lambda: np.asarray(inp["xa_w_kv"], f32),
             "xa_w_q": lambda: np.asarray(inp["xa_w_q"], f32),
             "xa_w_o": lambda: np.asarray(inp["xa_w_o"], f32),
             "cKT": lambda: np.ascontiguousarray(
                 np.asarray(inp["cache_mem_k"][:, sl], f32).reshape(DEPTH, NSEQ, MEM, 4, 2, 128).transpose(0, 1, 5, 3, 4, 2)
             ).reshape(DEPTH, NSEQ, 128, NCH * MEM),
             "cV": lambda: np.ascontiguousarray(
                 np.asarray(inp["cache_mem_v"][:, sl], f32).reshape(DEPTH, NSEQ, 2, 128, D).transpose(0, 1, 3, 2, 4)
             ).reshape(DEPTH, NSEQ, 128, 2 * D)}
        m.update(extra_inputs(inp, c, sl))
        maps.append({k: v() for k, v in m.items() if k in used})
    return maps


def kernel(**inputs):
    phases = PHASES
    nc, used, douts = get_nc(phases)
    in_maps = make_in_maps(inputs, used)
    res = run_bass_kernel_spmd(nc, in_maps, core_ids=list(range(8)))
    outs = res.results
    yT = np.stack([np.asarray(o["yT"]) for o in outs])
    y_prompt = np.ascontiguousarray(yT[:, :, :TP].transpose(0, 2, 1))
    y_sample = np.ascontiguousarray(yT[:, :, TP:].transpose(0, 2, 1)).reshape(8 * NSEQ, 8, D)
    res_extra = {}
    for k_ in douts:
        if k_.startswith("dbg_"):
            res_extra[k_] = np.stack([np.asarray(o[k_]) for o in outs])
    if "convpT" in douts:
        cp = np.stack([np.asarray(o["convpT"]) for o in outs], axis=1)
        res_extra["conv_prompt"] = np.ascontiguousarray(cp.transpose(0, 1, 3, 2))
        cs = np.stack([np.asarray(o["convsT"]) for o in outs], axis=1)
        res_extra["conv_sample"] = np.ascontiguousarray(cs.transpose(0, 1, 3, 4, 2)).reshape(2, 8 * NSEQ, 30, D)
    if "shiftpT" in douts:
        sp = np.stack([np.asarray(o["shiftpT"]) for o in outs], axis=1)
        res_extra["shift_prompt"] = np.ascontiguousarray(sp.transpose(0, 1, 3, 2)).reshape(2, 8, D)
        ss = np.stack([np.asarray(o["shiftsT"]) for o in outs], axis=1)
        res_extra["shift_sample"] = np.ascontiguousarray(ss.transpose(0, 1, 4, 3, 2)).reshape(2, 8 * NSEQ, D)
        wp = np.stack([np.asarray(o["wkvpT"]) for o in outs], axis=1)
        res_extra["wkv_prompt"] = np.ascontiguousarray(wp.transpose(0, 1, 2, 4, 3))
        ws = np.stack([np.asarray(o["wkvsT"]) for o in outs], axis=1)
        res_extra["wkv_sample"] = np.ascontiguousarray(ws.transpose(0, 1, 2, 3, 5, 4)).reshape(2, 8 * NSEQ, 16, 64, 64)
    DBG["extra"] = res_extra
    if "mem_k" not in douts:
        return y_prompt, y_sample
    mem_k = np.stack([np.asarray(o["mem_k"]) for o in outs], axis=1).reshape(DEPTH, 8, MEM, 4, 256)
    mem_v = np.stack([np.asarray(o["mem_v"]) for o in outs], axis=1).reshape(DEPTH, 8, MEM, 4, 256)
    if DBG.get("short"):
        return y_prompt, y_sample, mem_k, mem_v
    f32 = np.float32
    ex = res_extra
    conv_p = ex.get("conv_prompt", np.zeros((2, 8, 30, D), f32))
    conv_s = ex.get("conv_sample", np.zeros((2, 8 * NSEQ, 30, D), f32))
    shift_p = ex.get("shift_prompt", np.zeros((2, 8, D), f32))
    shift_s = ex.get("shift_sample", np.zeros((2, 8 * NSEQ, D), f32))
    wkv_p = ex.get("wkv_prompt", np.zeros((2, 8, 16, 64, 64), f32))
    wkv_s = ex.get("wkv_sample", np.zeros((2, 8 * NSEQ, 16, 64, 64), f32))
    return (y_prompt, y_sample, mem_k, mem_v, conv_p.astype(f32), shift_p.astype(f32), wkv_p.astype(f32),
            conv_s.astype(f32), shift_s.astype(f32), wkv_s.astype(f32))
```
